# Optimizing a Trainium2 kernel written in Bass

```python
import math
import jax, jax.numpy as jnp
from jax import lax
import numpy as np

D_MODEL = 1024
BATCH = 32
SEQ = 2048
DEPTH = 1

D_MIX = D_MODEL
GDN_WIDTH = D_MIX // 2
GDN_HEAD_DIM = 128
GDN_HEADS = GDN_WIDTH // GDN_HEAD_DIM
GDN_CHUNK = 64
LRU_WIDTH = D_MIX - GDN_WIDTH
LRU_BLOCKS = 8
LRU_BLOCK_DIM = LRU_WIDTH // LRU_BLOCKS
LRU_C = 8.0
CONV_WIDTH = 4
D_FF = 4 * D_MODEL
EPS = 1e-6
SPLIT_SIZES = (3 * GDN_WIDTH, GDN_WIDTH, GDN_HEADS, GDN_HEADS, LRU_WIDTH, LRU_WIDTH)
IN_COLS = sum(SPLIT_SIZES)
SPLIT_IDX = tuple(int(i) for i in np.cumsum(SPLIT_SIZES)[:-1])

kernel_name = "hymba_gdn_rglru_sqrelu_block"


def rmsnorm(x, w):
    xf = x.astype(jnp.float32)
    y = xf * lax.rsqrt(jnp.mean(xf * xf, axis=-1, keepdims=True) + EPS)
    return (y * w.astype(jnp.float32)).astype(x.dtype)


def l2norm(x):
    xf = x.astype(jnp.float32)
    return xf * lax.rsqrt(jnp.sum(xf * xf, axis=-1, keepdims=True) + EPS)


def causal_depthwise_conv(x, w):
    c = x.shape[-1]
    return lax.conv_general_dilated(
        x, w[:, None, :].astype(x.dtype), window_strides=(1,),
        padding=[(CONV_WIDTH - 1, 0)], dimension_numbers=("NWC", "WIO", "NWC"),
        feature_group_count=c)


def gated_delta_rule_chunked(q, k, v, g, beta):
    f32 = jnp.float32
    b, s, h, dk = q.shape
    dv = v.shape[-1]
    c = GDN_CHUNK
    n = s // c

    def to_chunks(t):
        t = jnp.moveaxis(t.astype(f32), 2, 1)
        return t.reshape(b, h, n, c, *t.shape[3:])

    q = to_chunks(q) * (dk ** -0.5)
    k = to_chunks(k)
    v = to_chunks(v)
    g_cum = jnp.cumsum(to_chunks(g), axis=-1)
    beta = to_chunks(beta)

    causal = jnp.tril(jnp.ones((c, c), dtype=bool))
    strict = jnp.tril(jnp.ones((c, c), dtype=bool), -1)
    diff = g_cum[..., :, None] - g_cum[..., None, :]
    decay = jnp.where(causal, jnp.exp(jnp.where(causal, diff, 0.0)), 0.0)

    k_beta = k * beta[..., None]
    v_beta = v * beta[..., None]
    a_low = jnp.where(strict, jnp.einsum("bhnid,bhnjd->bhnij", k_beta, k) * decay, 0.0)
    i_plus_a = a_low + jnp.eye(c, dtype=f32)
    rhs = jnp.concatenate([v_beta, k_beta * jnp.exp(g_cum)[..., None]], axis=-1)
    sol = lax.linalg.triangular_solve(i_plus_a, rhs, left_side=True, lower=True)
    u = sol[..., :dv]
    w = sol[..., dv:]
    qk_intra = jnp.where(causal, jnp.einsum("bhnid,bhnjd->bhnij", q, k) * decay, 0.0)

    def step(state, inp):
        q_c, k_c, u_c, w_c, attn_c, g_c = inp
        v_new = u_c - jnp.einsum("bhck,bhkv->bhcv", w_c, state)
        o_c = (jnp.einsum("bhck,bhkv->bhcv", q_c * jnp.exp(g_c)[..., None], state)
               + jnp.einsum("bhij,bhjv->bhiv", attn_c, v_new))
        g_last = g_c[..., -1]
        k_dec = k_c * jnp.exp(g_last[..., None] - g_c)[..., None]
        state = state * jnp.exp(g_last)[..., None, None] + jnp.einsum("bhck,bhcv->bhkv", k_dec, v_new)
        return state, o_c

    xs = tuple(jnp.moveaxis(t, 2, 0) for t in (q, k, u, w, qk_intra, g_cum))
    state0 = jnp.zeros((b, h, dk, dv), f32)
    _, o = lax.scan(step, state0, xs)
    o = jnp.moveaxis(o, 0, 2).reshape(b, h, s, dv)
    return jnp.moveaxis(o, 1, 2)


def rg_lru(x, wa, ba, wx, bx, a_param):
    f32 = jnp.float32
    b, s, d = x.shape
    xf = x.astype(f32)
    xb = xf.reshape(b, s, LRU_BLOCKS, LRU_BLOCK_DIM)
    r = jax.nn.sigmoid(jnp.einsum("bsgi,gij->bsgj", xb, wa.astype(f32)) + ba.astype(f32)).reshape(b, s, d)
    i = jax.nn.sigmoid(jnp.einsum("bsgi,gij->bsgj", xb, wx.astype(f32)) + bx.astype(f32)).reshape(b, s, d)
    log_a = -LRU_C * r * jax.nn.softplus(-a_param.astype(f32))
    a = jnp.exp(log_a)
    gated_x = jnp.sqrt(-jnp.expm1(2.0 * log_a)) * (i * xf)

    def combine(c1, c2):
        a1, b1 = c1
        a2, b2 = c2
        return a1 * a2, a2 * b1 + b2

    _, h = lax.associative_scan(combine, (a, gated_x), axis=1)
    return h


def hybrid_layer(x, norm_mix_w, w_in, gdn_conv_w, gdn_A_log, gdn_dt_bias, gdn_norm_w,
                 lru_conv_w, lru_conv_b, lru_gate_a_w, lru_gate_a_b, lru_gate_x_w, lru_gate_x_b,
                 lru_a_param, w_out, norm_mlp_w, w_ff1, w_ff2):
    b, s, _ = x.shape
    dt = x.dtype
    h = rmsnorm(x, norm_mix_w)
    proj = jnp.einsum("bsd,de->bse", h, w_in.astype(dt))
    qkv, z, b_logit, a_logit, lru_x, lru_gate = jnp.split(proj, SPLIT_IDX, axis=-1)

    qkv = jax.nn.silu(causal_depthwise_conv(qkv, gdn_conv_w))
    q, k, v = jnp.split(qkv, 3, axis=-1)
    q = l2norm(q.reshape(b, s, GDN_HEADS, GDN_HEAD_DIM))
    k = l2norm(k.reshape(b, s, GDN_HEADS, GDN_HEAD_DIM))
    v = v.reshape(b, s, GDN_HEADS, GDN_HEAD_DIM)
    beta = jax.nn.sigmoid(b_logit.astype(jnp.float32))
    g = -jnp.exp(gdn_A_log.astype(jnp.float32)) * jax.nn.softplus(
        a_logit.astype(jnp.float32) + gdn_dt_bias.astype(jnp.float32))
    o = gated_delta_rule_chunked(q, k, v, g, beta)
    z = z.reshape(b, s, GDN_HEADS, GDN_HEAD_DIM).astype(jnp.float32)
    o = rmsnorm(o, gdn_norm_w) * jax.nn.silu(z)
    gdn_out = o.reshape(b, s, GDN_WIDTH).astype(dt)

    xr = causal_depthwise_conv(lru_x, lru_conv_w) + lru_conv_b.astype(dt)
    hr = rg_lru(xr, lru_gate_a_w, lru_gate_a_b, lru_gate_x_w, lru_gate_x_b, lru_a_param)
    lru_out = (hr * jax.nn.gelu(lru_gate.astype(jnp.float32))).astype(dt)

    mix = jnp.concatenate([gdn_out, lru_out], axis=-1)
    x = x + jnp.einsum("bse,ed->bsd", mix, w_out.astype(dt))

    m = rmsnorm(x, norm_mlp_w)
    u = jax.nn.relu(jnp.einsum("bsd,df->bsf", m, w_ff1.astype(dt)))
    x = x + jnp.einsum("bsf,fd->bsd", u * u, w_ff2.astype(dt))
    return x


def setup_inputs(seed: int = 0) -> dict:
    key = jax.random.key(seed)
    ks = jax.random.split(key, 24)
    f32 = jnp.float32
    nrm = lambda k, shape, scale: (jax.random.normal(k, shape, f32) * scale)
    x = jax.random.normal(ks[0], (BATCH, SEQ, D_MODEL), f32)
    norm_mix_w = 1.0 + nrm(ks[1], (DEPTH, D_MODEL), 0.02)
    w_in = nrm(ks[2], (DEPTH, D_MODEL, IN_COLS), D_MODEL ** -0.5)
    gdn_conv_w = nrm(ks[3], (DEPTH, CONV_WIDTH, 3 * GDN_WIDTH), CONV_WIDTH ** -0.5)
    gdn_A_log = jnp.log(jax.random.uniform(ks[4], (DEPTH, GDN_HEADS), f32, 1.0, 16.0))
    dt0 = jnp.exp(jax.random.uniform(ks[5], (DEPTH, GDN_HEADS), f32, math.log(1e-3), math.log(1e-1)))
    gdn_dt_bias = dt0 + jnp.log(-jnp.expm1(-dt0))
    gdn_norm_w = 1.0 + nrm(ks[6], (DEPTH, GDN_HEAD_DIM), 0.02)
    lru_conv_w = nrm(ks[7], (DEPTH, CONV_WIDTH, LRU_WIDTH), CONV_WIDTH ** -0.5)
    lru_conv_b = nrm(ks[8], (DEPTH, LRU_WIDTH), 0.01)
    lru_gate_a_w = nrm(ks[9], (DEPTH, LRU_BLOCKS, LRU_BLOCK_DIM, LRU_BLOCK_DIM), LRU_BLOCK_DIM ** -0.5)
    lru_gate_a_b = nrm(ks[10], (DEPTH, LRU_BLOCKS, LRU_BLOCK_DIM), 0.01)
    lru_gate_x_w = nrm(ks[11], (DEPTH, LRU_BLOCKS, LRU_BLOCK_DIM, LRU_BLOCK_DIM), LRU_BLOCK_DIM ** -0.5)
    lru_gate_x_b = nrm(ks[12], (DEPTH, LRU_BLOCKS, LRU_BLOCK_DIM), 0.01)
    a_c = jax.random.uniform(ks[13], (DEPTH, LRU_WIDTH), f32, 0.9, 0.999)
    sig_l = a_c ** (1.0 / LRU_C)
    lru_a_param = jnp.log(sig_l) - jnp.log1p(-sig_l)
    w_out = nrm(ks[14], (DEPTH, D_MIX, D_MODEL), D_MIX ** -0.5)
    norm_mlp_w = 1.0 + nrm(ks[15], (DEPTH, D_MODEL), 0.02)
    w_ff1 = nrm(ks[16], (DEPTH, D_MODEL, D_FF), D_MODEL ** -0.5)
    w_ff2 = nrm(ks[17], (DEPTH, D_FF, D_MODEL), D_FF ** -0.5)
    final_norm_w = 1.0 + nrm(ks[18], (D_MODEL,), 0.02)
    return {"x": x, "norm_mix_w": norm_mix_w, "w_in": w_in, "gdn_conv_w": gdn_conv_w,
            "gdn_A_log": gdn_A_log, "gdn_dt_bias": gdn_dt_bias, "gdn_norm_w": gdn_norm_w,
            "lru_conv_w": lru_conv_w, "lru_conv_b": lru_conv_b,
            "lru_gate_a_w": lru_gate_a_w, "lru_gate_a_b": lru_gate_a_b,
            "lru_gate_x_w": lru_gate_x_w, "lru_gate_x_b": lru_gate_x_b,
            "lru_a_param": lru_a_param, "w_out": w_out, "norm_mlp_w": norm_mlp_w,
            "w_ff1": w_ff1, "w_ff2": w_ff2, "final_norm_w": final_norm_w}


def reference(x, norm_mix_w, w_in, gdn_conv_w, gdn_A_log, gdn_dt_bias, gdn_norm_w,
              lru_conv_w, lru_conv_b, lru_gate_a_w, lru_gate_a_b, lru_gate_x_w, lru_gate_x_b,
              lru_a_param, w_out, norm_mlp_w, w_ff1, w_ff2, final_norm_w):
    for layer in range(DEPTH):
        x = hybrid_layer(x, norm_mix_w[layer], w_in[layer], gdn_conv_w[layer], gdn_A_log[layer],
                         gdn_dt_bias[layer], gdn_norm_w[layer], lru_conv_w[layer], lru_conv_b[layer],
                         lru_gate_a_w[layer], lru_gate_a_b[layer], lru_gate_x_w[layer],
                         lru_gate_x_b[layer], lru_a_param[layer], w_out[layer], norm_mlp_w[layer],
                         w_ff1[layer], w_ff2[layer])
    return rmsnorm(x, final_norm_w)
```

```python
import numpy as np
import concourse.bass as bass
import concourse.mybir as mybir
from concourse.bass_utils import run_bass_kernel_spmd
from contextlib import ExitStack

F32 = mybir.dt.float32
BF16 = mybir.dt.bfloat16
AF = mybir.ActivationFunctionType
ALU = mybir.AluOpType

D = 1024
NCH = 8
TT = 512
NH = 4
EPS = 1e-6
N_DMA_SEMS = 24
N_SP_SEMS = 16
NGRP = 24
NVEC = 120


class V:
    __slots__ = ("t", "ap")

    def __init__(self, t, ap):
        self.t = t
        self.ap = ap


class LatePD:
    __slots__ = ("bound",)

    def __init__(self):
        self.bound = None

    @property
    def v(self):
        return LV(self, None, TT)

    def __getitem__(self, idx):
        n = TT
        if isinstance(idx, tuple) and isinstance(idx[-1], slice) and idx[-1].start is not None:
            n = idx[-1].stop - idx[-1].start
        return LV(self, idx, n)


class LV:
    __slots__ = ("t", "idx", "n")

    def __init__(self, t, idx, n):
        self.t = t
        self.idx = idx
        self.n = n

    @property
    def ap(self):
        b = self.t.bound
        return b.ap if self.idx is None else b.ap[self.idx]


class T:
    __slots__ = ("ap", "name", "last_w", "readers", "const", "root", "excl")

    def __init__(self, ap, name="", const=False, parent=None, excl=False):
        self.ap = ap
        self.name = name
        self.last_w = None
        self.readers = []
        self.const = const
        self.excl = excl
        self.root = parent.root if parent is not None else self

    def __getitem__(self, idx):
        return V(self.root, self.ap[idx])

    @property
    def v(self):
        return V(self.root, self.ap)


class Op:
    __slots__ = ("eng", "fn", "deps", "signal", "count", "is_dma", "dma_slot", "dma_val", "prev_dma")

    def __init__(self, eng, fn, is_dma):
        self.eng = eng
        self.fn = fn
        self.deps = []
        self.signal = False
        self.count = 0
        self.is_dma = is_dma
        self.dma_slot = -1
        self.dma_val = 0
        self.prev_dma = None


class Prog:
    ENGS = ("pe", "act", "dve", "pool", "sp")

    def __init__(self):
        self.ops = []
        self.n_dma_sp = 0
        self.n_dma_pl = 0
        self.dma_last = [None] * N_DMA_SEMS

    def add(self, eng, fn, reads=(), writes=(), dma=False):
        op = Op(eng, fn, dma)
        deps = {}
        for t in reads:
            if t.last_w is not None:
                deps[id(t.last_w)] = t.last_w
            if t.excl:
                for r in t.readers:
                    if r.eng != eng:
                        deps[id(r)] = r
        for t in writes:
            if t.last_w is not None:
                deps[id(t.last_w)] = t.last_w
            for r in t.readers:
                deps[id(r)] = r
        for t in reads:
            if not t.const:
                t.readers.append(op)
        for t in writes:
            t.last_w = op
            t.readers = []
        for d in deps.values():
            if d is op:
                continue
            if d.is_dma:
                op.deps.append(d)
            elif d.eng == eng and not dma:
                if eng != "pe":
                    op.deps.append(d)
                    d.signal = True
            else:
                op.deps.append(d)
                d.signal = True
        if dma:
            if eng == "sp":
                k = self.n_dma_sp
                self.n_dma_sp += 1
                slot = k % N_SP_SEMS
                val = 16 * (k // N_SP_SEMS + 1)
            else:
                k = self.n_dma_pl
                self.n_dma_pl += 1
                slot = N_SP_SEMS + k % (N_DMA_SEMS - N_SP_SEMS)
                val = 16 * (k // (N_DMA_SEMS - N_SP_SEMS) + 1)
            op.dma_slot = slot
            op.dma_val = val
            op.prev_dma = self.dma_last[slot]
            self.dma_last[slot] = op
        self.ops.append(op)
        return op

    def emit(self, nc):
        counts = {e: 0 for e in self.ENGS}
        for op in self.ops:
            if not op.is_dma and op.signal:
                counts[op.eng] += 1
                op.count = counts[op.eng]
        with ExitStack() as es:
            sems = {e: es.enter_context(nc.semaphore("s_" + e)) for e in self.ENGS}
            dsems = [es.enter_context(nc.semaphore("d%d" % i)) for i in range(N_DMA_SEMS)]
            block = es.enter_context(nc.Block())
            names = {"pe": "tensor", "act": "scalar", "dve": "vector", "pool": "gpsimd", "sp": "sync"}
            dma_last = self.dma_last
            for e in self.ENGS:
                myops = [op for op in self.ops if op.eng == e]

                def body(eng, myops=myops, e=e):
                    waited = {}

                    def wait(sem, val, key):
                        if waited.get(key, 0) >= val:
                            return
                        waited[key] = val
                        eng.wait_ge(sem, val)

                    for op in myops:
                        for d in op.deps:
                            if d.is_dma:
                                wait(dsems[d.dma_slot], d.dma_val, ("d", d.dma_slot))
                            else:
                                wait(sems[d.eng], d.count, d.eng)
                        if op.is_dma:
                            if op.prev_dma is not None:
                                wait(dsems[op.dma_slot], op.prev_dma.dma_val, ("d", op.dma_slot))
                            op.fn(eng).then_inc(dsems[op.dma_slot], 16)
                        else:
                            ins = op.fn(eng)
                            if op.signal:
                                ins.then_inc(sems[e], 1)
                    if e == "sp":
                        for d in dma_last:
                            if d is not None:
                                wait(dsems[d.dma_slot], d.dma_val, ("d", d.dma_slot))

                getattr(block, names[e])(body)


import os as _os
DEBUG_TAGS = bool(_os.environ.get("KCRIT"))
EVAC_PRIO = bool(int(_os.environ.get("KEVAC", "0")))
TAGS = {}


def fsz(v):
    if isinstance(v, LV):
        return v.n
    n = 1
    for d in v.ap.shape[1:]:
        n *= d
    return n


class Rec:
    def __init__(self, prio=0):
        self.chunks = [[]]
        self.prio = prio

    def cut(self):
        if self.chunks[-1]:
            self.chunks.append([])

    def op(self, eng, fn, outs, ins, dma=False, cost=0.3):
        if DEBUG_TAGS:
            import sys
            f = sys._getframe(1)
            while f.f_code.co_name in ("op", "mm", "tr", "act", "tt", "ts", "stt", "copy", "recip", "memset", "scan", "dma"):
                f = f.f_back
            TAGS[id(fn)] = f.f_lineno
        prio = self.prio
        if EVAC_PRIO and eng in ("act", "dve") and any(isinstance(v, LV) for v in ins):
            prio = -1
        self.chunks[-1].append((eng, fn, tuple(ins), tuple(outs), dma, cost, prio))

    def mm(self, out, lhsT, rhs, start=True, stop=True):
        passes = 4 if (not isinstance(rhs, LV) and rhs.ap.dtype == F32) else 1
        self.op("pe", lambda e: e.matmul(out.ap, lhsT=lhsT.ap, rhs=rhs.ap, start=start, stop=stop),
                [out], [lhsT, rhs], cost=0.035 + 0.000417 * passes * max(fsz(rhs), 64))

    def tr(self, out, in_, ident):
        self.op("pe", lambda e: e.transpose(out.ap, in_.ap, ident.ap), [out], [in_, ident], cost=0.1)

    def act(self, out, in_, func, scale=None, bias=None):
        kw = {}
        ins = [in_]
        if scale is not None:
            if isinstance(scale, V):
                kw["scale"] = scale.ap
                ins.append(scale)
            else:
                kw["scale"] = scale
        if bias is not None:
            if isinstance(bias, V):
                kw["bias"] = bias.ap
                ins.append(bias)
            else:
                kw["bias"] = bias
        self.op("act", lambda e: e.activation(out=out.ap, in_=in_.ap, func=func, **kw), [out], ins,
                cost=0.15 + 0.0008 * fsz(in_))

    def tt(self, eng, out, a, b, op):
        c = (0.07 + 0.00105 * fsz(out)) if eng == "dve" else (0.1 + 0.00227 * fsz(out))
        self.op(eng, lambda e: e.tensor_tensor(out=out.ap, in0=a.ap, in1=b.ap, op=op), [out], [a, b], cost=c)

    def ts(self, eng, out, a, s1, op0, s2=None, op1=None):
        ins = [a]
        if isinstance(s1, V):
            ins.append(s1)
            s1 = s1.ap
        if isinstance(s2, V):
            ins.append(s2)
            s2 = s2.ap
        c = (0.07 + 0.00105 * fsz(out)) if eng == "dve" else (0.1 + 0.00115 * fsz(out))
        if isinstance(s1, bass.AP) and eng == "pool":
            c = 0.1 + 0.016 * fsz(out)
        if op1 is None:
            self.op(eng, lambda e: e.tensor_scalar(out=out.ap, in0=a.ap, scalar1=s1, scalar2=None, op0=op0),
                    [out], ins, cost=c)
        else:
            self.op(eng, lambda e: e.tensor_scalar(out=out.ap, in0=a.ap, scalar1=s1, scalar2=s2, op0=op0, op1=op1),
                    [out], ins, cost=c)

    def stt(self, out, a, scalar, b, op0, op1):
        ins = [a, b]
        if isinstance(scalar, V):
            ins.append(scalar)
            scalar = scalar.ap
        self.op("dve", lambda e: e.scalar_tensor_tensor(out=out.ap, in0=a.ap, scalar=scalar, in1=b.ap,
                                                        op0=op0, op1=op1), [out], ins,
                cost=0.07 + 0.00133 * fsz(out))

    def copy(self, eng, out, in_):
        if eng == "act":
            self.op("act", lambda e: e.activation(out=out.ap, in_=in_.ap, func=AF.Copy), [out], [in_],
                    cost=0.15 + 0.0008 * fsz(in_))
        else:
            c = (0.07 + 0.00105 * fsz(out)) if eng == "dve" else (0.1 + 0.0012 * fsz(out))
            self.op(eng, lambda e: e.tensor_copy(out=out.ap, in_=in_.ap), [out], [in_], cost=c)

    def recip(self, out, in_):
        self.op("dve", lambda e: e.reciprocal(out=out.ap, in_=in_.ap), [out], [in_], cost=0.07 + 0.006 * fsz(out))

    def memset(self, eng, out, val):
        self.op(eng, lambda e: e.memset(out.ap, val), [out], [], cost=0.1 + 0.0006 * fsz(out))

    def scan(self, out, d0, d1, init):
        ins = [d0, d1]
        if isinstance(init, V):
            ins.append(init)
            init = init.ap
        self.op("dve", lambda e: e.tensor_tensor_scan(out=out.ap, data0=d0.ap, data1=d1.ap, initial=init,
                                                      op0=ALU.mult, op1=ALU.add), [out], ins,
                cost=0.07 + 0.0022 * fsz(out))

    def dma(self, eng, out, in_, extra_ins=()):
        self.op(eng, lambda e: e.dma_start(out=out.ap, in_=in_.ap), [out], [in_] + list(extra_ins), dma=True,
                cost=2.2 + out.ap.nbytes() / 150e3)


def list_schedule(spec, window=None):
    import os
    if window is None:
        window = int(os.environ.get("KWIN", "128"))
    n = len(spec)
    deps = [None] * n
    last_w = {}
    readers = {}
    for i, (eng, fn, reads, writes, dma, cost, prio) in enumerate(spec):
        d = set()
        for t in reads:
            k = id(t)
            if k in last_w:
                d.add(last_w[k])
            if t.excl:
                for r in readers.get(k, ()):
                    if spec[r][0] != eng:
                        d.add(r)
        for t in writes:
            k = id(t)
            if k in last_w:
                d.add(last_w[k])
            d.update(readers.get(k, ()))
        for t in reads:
            if not t.const:
                readers.setdefault(id(t), []).append(i)
        for t in writes:
            last_w[id(t)] = i
            readers[id(t)] = []
        d.discard(i)
        deps[i] = tuple(d)
    engs = ("pe", "act", "dve", "pool", "sp")
    if int(os.environ.get("KBLEVEL", "1")):
        bl = [0.0] * n
        for i in range(n - 1, -1, -1):
            bl[i] += spec[i][5]
            for d in deps[i]:
                v = bl[i] + 0.4
                if v > bl[d]:
                    bl[d] = v
        spec = [(o[0], o[1], o[2], o[3], o[4], o[5], -bl[i]) for i, o in enumerate(spec)]
    pending = {e: [i for i in range(n) if spec[i][0] == e] for e in engs}
    head = {e: 0 for e in engs}
    done = [False] * n
    finish = [0.0] * n
    efree = {e: 0.0 for e in engs}
    order = []
    LAT = float(os.environ.get("KLAT", "0.4"))
    INF = 1e30
    _sc = {e: float(os.environ.get("KS_" + e, "1.0")) for e in engs}
    if any(v != 1.0 for v in _sc.values()):
        spec = [(o[0], o[1], o[2], o[3], o[4], o[5] * (_sc[o[0]] if not o[4] else 1.0), o[6]) for o in spec]
    _kd = float(os.environ.get("KDMA", "1.0"))
    if _kd != 1.0:
        spec = [(o[0], o[1], o[2], o[3], o[4], o[5] * (_kd if o[4] else 1.0), o[6]) for o in spec]

    def candidate(e):
        lst = pending[e]
        h = head[e]
        while h < len(lst) and done[lst[h]]:
            h += 1
        head[e] = h
        best = None
        bkey = None
        cnt = 0
        j = h
        ef = efree[e]
        while j < len(lst) and cnt < window:
            i = lst[j]
            j += 1
            if done[i]:
                continue
            cnt += 1
            ok = True
            st = ef
            for d in deps[i]:
                if not done[d]:
                    ok = False
                    break
                f = finish[d] + (LAT if spec[d][0] != e or spec[d][4] else 0.05)
                if f > st:
                    st = f
            if not ok:
                continue
            key = (st, spec[i][6], i)
            if bkey is None or key < bkey:
                best, bkey = i, key
                if st <= ef + 1e-9 and spec[i][6] == 0:
                    break
        return best, (bkey[0] if bkey else INF)

    binder = {}
    startt = {}
    last_on = {}
    remaining = n
    while remaining:
        pick = None
        pstart = INF
        for e in engs:
            i, st = candidate(e)
            if i is not None and st < pstart:
                pick, pstart = i, st
        assert pick is not None, "scheduler deadlock"
        eng, fn, reads, writes, dma, cost, prio = spec[pick]
        done[pick] = True
        finish[pick] = pstart + cost
        if DEBUG_TAGS:
            bind = ("eng", last_on.get(eng))
            bt = efree[eng]
            for d in deps[pick]:
                f = finish[d] + (LAT if spec[d][0] != eng or spec[d][4] else 0.05)
                if f > bt + 1e-9:
                    bt = f
                    bind = ("dep", d)
            binder[pick] = bind
            startt[pick] = pstart
            last_on[eng] = pick
        efree[eng] = pstart + (0.15 if dma else cost)
        order.append(pick)
        remaining -= 1
    import os
    if os.environ.get("KDEBUG"):
        busy = {e: 0.0 for e in engs}
        for i in range(n):
            busy[spec[i][0]] += (0.15 if spec[i][4] else spec[i][5])
        print("sched makespan(us):", max(finish), "busy:", {e: round(v) for e, v in busy.items()})
    if DEBUG_TAGS:
        import collections
        cur = max(range(n), key=lambda i: finish[i])
        t_hi = float(os.environ.get("KCRIT_HI", "1e9"))
        t_lo = float(os.environ.get("KCRIT_LO", "0"))
        agg = collections.OrderedDict()
        tot = collections.Counter()
        while cur is not None:
            kind, prev = binder.get(cur, ("eng", None))
            if t_lo <= startt[cur] <= t_hi:
                key = (spec[cur][0], TAGS.get(id(spec[cur][1]), 0), kind)
                a = agg.setdefault(key, [0, 0.0])
                a[0] += 1
                a[1] += finish[cur] - startt[cur]
                tot[(spec[cur][0], kind)] += finish[cur] - startt[cur]
            cur = prev
        print("critical path ops by (eng, line, binding):")
        for k, (c, tme) in sorted(agg.items(), key=lambda x: -x[1][1])[:40]:
            print("  ", k, "n=%d time=%.1f" % (c, tme))
        print("totals:", {k: round(v, 1) for k, v in tot.items()})
    return order


def merge_chunks(recs):
    lists = [[c for c in r.chunks if c] for r in recs]
    lists = [l for l in lists if l]
    pos = [0] * len(lists)
    out = []
    while True:
        best = None
        bestf = None
        for i, l in enumerate(lists):
            if pos[i] < len(l):
                f = (pos[i] + 0.5) / len(l)
                if best is None or f < bestf:
                    best, bestf = i, f
        if best is None:
            break
        out.append(lists[best][pos[best]])
        pos[best] += 1
    return out


def build_program(nseq, S):
    NT = S // TT
    NTOK = nseq * S
    nc = bass.Bass("TRN2", target_bir_lowering=False)
    xT_d = nc.dram_tensor("xT", [NCH, 128, NTOK], F32, kind="ExternalInput").ap()
    wts_d = nc.dram_tensor("wts", [NGRP, 128, 4096], F32, kind="ExternalInput").ap()
    cm_d = nc.dram_tensor("cm", [128, 8, 128], F32, kind="ExternalInput").ap()
    vecs_d = nc.dram_tensor("vecs", [128, NVEC], F32, kind="ExternalInput").ap()
    wgate_d = nc.dram_tensor("wgate", [128, 2, 4, 128], F32, kind="ExternalInput").ap()
    wba_d = nc.dram_tensor("wba", [128, 8, 8], F32, kind="ExternalInput").ap()
    yT_d = nc.dram_tensor("yT", [NCH, 128, NTOK], F32, kind="ExternalOutput").ap()
    wbf_d = nc.dram_tensor("wbf", [NGRP, 128, 4096], BF16, kind="Internal").ap()

    es = ExitStack()
    with es:
        sb_bytes = [0]

        def sb(name, shape, dt):
            n = 1
            for d in shape[1:]:
                n *= d
            sb_bytes[0] += n * (4 if dt == F32 else 2)
            return es.enter_context(nc.sbuf_tensor(name, shape, dt))

        def ps(name, shape, dt):
            return es.enter_context(nc.psum_tensor(name, shape, dt))

        ringA_h = sb("ringA", [128, 2, 4096], BF16)
        ringB_h = sb("ringB", [128, 3, 4096], BF16)
        ringA = [T(ringA_h[:, i, :]) for i in range(2)]
        ringB = [T(ringB_h[:, i, :]) for i in range(3)]
        x_h = sb("x", [128, 2, NCH, TT], F32)
        xA = T(x_h[:, 0, :, :])
        xBc = [T(x_h[:, 1, c, :]) for c in range(NCH)]
        hA_h = sb("hA", [128, NCH, TT], BF16)
        hB_h = sb("hB", [128, NCH, TT], BF16)
        hA = [T(hA_h[:, c, :]) for c in range(NCH)]
        hB = [T(hB_h[:, c, :]) for c in range(NCH)]
        uT_h = sb("uT", [128, 16, TT], BF16)
        uT = [T(uT_h[:, c, :]) for c in range(16)]
        qkn_h = sb("qkn", [128, NH, 4, 2, 128], BF16)
        qkn = [T(qkn_h[:, :, :, kq, :]) for kq in range(2)]
        vT_h = sb("vT", [128, NH, TT], BF16)
        vT = [T(vT_h[:, h, :]) for h in range(NH)]
        sz_h = sb("sz", [128, NH, TT], BF16)
        sz = [T(sz_h[:, h, :]) for h in range(NH)]
        gg_h = sb("gg", [128, 4, TT], BF16)
        gg = [T(gg_h[:, c, :]) for c in range(4)]
        mix_h = sb("mix", [128, NCH, TT], BF16)
        mix = [T(mix_h[:, c, :]) for c in range(NCH)]
        osb_h = sb("osb", [128, NH, TT], BF16)
        osb = [T(osb_h[:, h, :]) for h in range(NH)]
        raw_h = sb("raw", [128, 2, TT + 4], F32)
        raw = [T(raw_h[:, i, :]) for i in range(2)]
        acc_h = sb("acc", [128, 2, TT], F32)
        acc = [T(acc_h[:, i, :]) for i in range(2)]
        sqb_h = sb("sqb", [128, 2, TT], BF16)
        sqb = [T(sqb_h[:, i, :]) for i in range(2)]
        rs_h = sb("rs", [128, 2, TT], F32)
        rs = [T(rs_h[:, i, :]) for i in range(2)]
        rsB = T(sb("rsB", [128, TT], F32)[:, :])
        reluB_h = sb("reluB", [128, 2, TT], BF16)
        reluB = [T(reluB_h[:, i, :]) for i in range(2)]
        lt_h = sb("lt", [128, 4, TT], F32)
        lt = [T(lt_h[:, i, :]) for i in range(4)]
        xrb = T(sb("xrb", [128, TT], BF16)[:, :])
        halo_h = sb("halo", [128, 16, 4], F32)
        halo = [T(halo_h[:, c, :]) for c in range(16)]
        hst_h = sb("hst", [128, 4, 2], F32)
        hst = [T(hst_h[:, c, :]) for c in range(4)]
        S32_h = sb("S32", [128, NH, 128], F32)
        S32 = [T(S32_h[:, h, :]) for h in range(NH)]
        S32A = S32_h[:, :, :]
        Sbf_h = sb("Sbf", [128, NH, 128], BF16)
        Sbf = [T(Sbf_h[:, h, :]) for h in range(NH)]
        SbfA = Sbf_h[:, :, :]
        cm = T(sb("cm_sb", [128, 8, 128], F32)[:, :, :], const=True)
        cmb = T(sb("cmb", [128, 4, 128], BF16)[:, :, :], const=True)
        vecs = T(sb("vecs_sb", [128, NVEC], F32)[:, :], const=True)
        dvec = T(sb("dvec", [128, 24], F32)[:, :], const=True)
        wgate = T(sb("wgate_sb", [128, 2, 4, 128], BF16)[:, :, :, :], const=True)
        wba = T(sb("wba_sb", [128, 8, 8], BF16)[:, :, :], const=True)
        L1, L2, MB, SM, ID, ONES, CM0, CM1 = [cm[:, i, :] for i in range(8)]
        Cb = cmb[:, 0, :]
        SMb = cmb[:, 1, :]
        IDb = cmb[:, 2, :]
        ONESb = cmb[:, 3, :]
        gt_h = sb("gt", [128, 12, 16], F32)
        gtT = [T(gt_h[:, i, :]) for i in range(12)]
        NSET = 2
        TN = ["decT", "decS", "egcB", "N0", "Y0", "Xa", "Ya", "kg", "Ptmp"]
        ut_h = sb("ut", [128, len(TN), NH, 128], BF16)
        UT = {nm: [T(ut_h[:, i, h, :]) for h in range(NH)] for i, nm in enumerate(TN)}
        UTA = {nm: ut_h[:, i, :, :] for i, nm in enumerate(TN)}
        PN = ["Pfin", "kdec", "vtok", "attnT", "qg", "nw0"]
        up_h = sb("up", [128, NSET, len(PN), NH, 128], BF16)
        UP = [{nm: [T(up_h[:, s_, i, h, :]) for h in range(NH)] for i, nm in enumerate(PN)} for s_ in range(NSET)]
        UPA = [{nm: up_h[:, s_, i, :, :] for i, nm in enumerate(PN)} for s_ in range(NSET)]
        ug_h = sb("ug", [128, NH, 128], F32)
        UG = [T(ug_h[:, h, :]) for h in range(NH)]
        vnew_h = sb("vnew", [128, NH, 128], BF16)
        VNEW = [T(vnew_h[:, h, :]) for h in range(NH)]
        c4_h = sb("c4", [128, 3, NH, 128], BF16)
        c4 = T(c4_h[:, :, :, :], const=True)

        def bank(name, dt=F32, n=TT):
            h_ = ps(name, [128, n], dt)
            return h_, T(h_[:, :], excl=True)
        pdense = [bank("pd%d" % i)[1] for i in range(3)]
        _, pstat = bank("pstat")
        pdense += [bank("pq%d" % i)[1] for i in range(2)]
        qc_h, QC = bank("qc")
        qd_h, QD = bank("qd")

        wts_t = [T(wts_d[g], const=True) for g in range(NGRP)]
        wbf_t = [T(wbf_d[g]) for g in range(NGRP)]
        xT_t = T(xT_d, const=True)

        P = Prog()

        spec = []
        bank_free_after = [-1] * 8

        def flush(recs):
            ops = [o for chunk in merge_chunks(recs) for o in chunk]
            base = len(spec)
            last_use = {}
            for i, o in enumerate(ops):
                for v in o[2] + o[3]:
                    if isinstance(v.t, LatePD):
                        last_use[id(v.t)] = base + i
            for i, (eng, fn, ins, outs, dma, cost, prio) in enumerate(ops):
                idx = base + i
                rt = []
                for grp in (ins, outs):
                    lst = []
                    for v in grp:
                        t = v.t
                        if isinstance(t, LatePD):
                            if t.bound is None:
                                cands = [k for k in range(len(pdense)) if bank_free_after[k] < idx]
                                assert cands, "too many live short-lived PSUM tiles"
                                k = min(cands, key=lambda k_: bank_free_after[k_])
                                t.bound = pdense[k]
                                bank_free_after[k] = last_use[id(t)]
                            t = t.bound.root
                        lst.append(t)
                    rt.append(lst)
                spec.append((eng, fn, rt[0], rt[1], dma, cost, prio))

        o_nw1, o_nw2, o_fnw, o_cw, o_lcw, o_lcb, o_lba, o_lbx, o_lap, o_alog, o_dtb, o_gnw = \
            0, 8, 16, 24, 72, 88, 92, 96, 100, 104, 108, 112

        R = Rec()
        R.dma("sp", cm.v, T(cm_d, const=True).v)
        R.dma("sp", vecs.v, T(vecs_d, const=True).v)
        R.dma("pool", cmb.v, V(T(cm_d, const=True), cm_d[:, 2:6, :]))
        R.dma("pool", wgate.v, T(wgate_d, const=True).v)
        R.dma("pool", wba.v, T(wba_d, const=True).v)
        for g in range(6):
            R.dma("pool", wbf_t[g].v, wts_t[g].v)
        for h in range(NH):
            R.copy("pool", c4[:, 0, h, :], SMb)
            R.copy("pool", c4[:, 1, h, :], IDb)
            R.copy("pool", c4[:, 2, h, :], Cb)
        tmpv = T(sb("tmpv", [128, 8, 16], F32)[:, :, :])

        def ln1p_small(R, out, e, w, k0):
            z = tmpv[:, k0, 0:w]
            z2 = tmpv[:, k0 + 1, 0:w]
            pl = tmpv[:, k0 + 2, 0:w]
            R.ts("dve", z, e, 2.0, ALU.add)
            R.recip(z, z)
            R.tt("dve", z, z, e, ALU.mult)
            R.tt("dve", z2, z, z, ALU.mult)
            R.ts("dve", pl, z2, 1.0 / 9, ALU.mult, 1.0 / 7, ALU.add)
            R.tt("dve", pl, pl, z2, ALU.mult)
            R.ts("dve", pl, pl, 1.0 / 5, ALU.add)
            R.tt("dve", pl, pl, z2, ALU.mult)
            R.ts("dve", pl, pl, 1.0 / 3, ALU.add)
            R.tt("dve", pl, pl, z2, ALU.mult)
            R.ts("dve", pl, pl, 1.0, ALU.add)
            R.tt("dve", pl, pl, z, ALU.mult)
            R.ts("dve", out, pl, 2.0, ALU.mult)

        def softplus(R, out, x, w, k0):
            ab = tmpv[:, k0 + 3, 0:w]
            l1 = tmpv[:, k0 + 4, 0:w]
            R.ts("dve", ab, x, -1.0, ALU.mult)
            R.tt("dve", ab, ab, x, ALU.min)
            R.act(ab, ab, AF.Exp)
            ln1p_small(R, l1, ab, w, k0)
            R.stt(out, x, 0.0, l1, ALU.max, ALU.add)

        ngl = tmpv[:, 7, 0:4]
        R.ts("dve", ngl, vecs[:, o_lap:o_lap + 4], -1.0, ALU.mult)
        softplus(R, ngl, ngl, 4, 0)
        R.ts("dve", dvec[:, 0:4], ngl, -8.0, ALU.mult)
        R.act(dvec[:, 4:8], vecs[:, o_alog:o_alog + 4], AF.Exp)
        R.ts("dve", dvec[:, 4:8], dvec[:, 4:8], -1.0, ALU.mult)
        R.ts("dve", dvec[:, 8:12], vecs[:, o_lba:o_lba + 4], -1.0, ALU.mult)
        R.ts("dve", dvec[:, 12:16], vecs[:, o_lbx:o_lbx + 4], -1.0, ALU.mult)
        R.ts("dve", dvec[:, 16:20], dvec[:, 0:4], 2.0, ALU.mult)
        flush([R])

        ring_state = {"A": [0, 0], "B": [0, 0]}
        ntiles = nseq * NT
        seqA = [g for _ in range(ntiles) for g in range(6)]
        seqB = [g for _ in range(ntiles) for g in (6, 7, 8, 9, 10, 11, 12, 13, 14, 15, 16, 17, 18, 19, 20, 21, 22, 23)]

        def ring_next(R, which):
            ring = ringA if which == "A" else ringB
            seq = seqA if which == "A" else seqB
            st = ring_state[which]
            while st[0] < len(seq) and st[0] < st[1] + len(ring):
                R.dma("sp", ring[st[0] % len(ring)].v, wbf_t[seq[st[0]]].v)
                st[0] += 1
            slot = ring[st[1] % len(ring)]
            st[1] += 1
            return slot

        def next_pdB():
            return LatePD()

        def sig_of(R, out, x, nscale):
            R.act(out, x, AF.Exp, scale=nscale)
            R.act(out, out, AF.Ln, bias=1.0)
            R.act(out, out, AF.Exp, scale=-1.0)

        def rmsnorm_to(R, xc, hdst, wcol, rsbuf):
            for c in range(NCH):
                R.act(hdst[c].v, xc(c), AF.Square)
            for c in range(NCH):
                R.mm(pstat.v, ONESb, hdst[c].v, start=(c == 0), stop=(c == NCH - 1))
            R.act(rsbuf.v, pstat.v, AF.Ln, scale=1.0 / D, bias=EPS)
            R.act(rsbuf.v, rsbuf.v, AF.Exp, scale=-0.5)
            R.cut()
            for c in range(NCH):
                R.stt(hdst[c].v, xc(c), vecs[:, wcol + c:wcol + c + 1], rsbuf.v, ALU.mult, ALU.mult)
            R.cut()

        def gen_A(ti):
            s_i, j_i = divmod(ti, NT)
            tok0 = s_i * S + j_i * TT
            first = (j_i == 0)
            R = Rec()
            R.dma("sp", xA.v, V(xT_t, xT_d[:, :, tok0:tok0 + TT].rearrange("c p t -> p c t")))
            if first:
                for c in range(16):
                    R.memset("pool", halo[c].v, 0.0)
                for c in range(4):
                    R.memset("pool", hst[c].v, 0.0)
                for h in range(NH):
                    R.memset("pool", S32[h].v, 0.0)
                    R.memset("pool", Sbf[h].v, 0.0)
            R.cut()
            rmsnorm_to(R, lambda c: xA[:, c, :], hA, o_nw1, rs[0])
            if ti == 0:
                for g in range(6, NGRP):
                    R.dma("pool", wbf_t[g].v, wts_t[g].v, extra_ins=[rs[0].v])

            pg = LatePD()
            for jj in range(4):
                for c in range(NCH):
                    R.mm(pg[:, 8 * jj:8 * jj + 8], hA[c][:, 128 * jj:128 * (jj + 1)], wba[:, c, :],
                         start=(c == 0), stop=(c == NCH - 1))
            R.cut()
            beta, nbeta, gtm, tA, egc, erem, egl0, egl1 = [gtT[i] for i in range(8)]
            dtb3 = V(vecs, vecs.ap[:, o_dtb:o_dtb + 4])
            for jj in range(4):
                R.act(beta[:, 4 * jj:4 * jj + 4], pg[:, 8 * jj:8 * jj + 4], AF.Exp, scale=-1.0)
                R.tt("dve", tA[:, 4 * jj:4 * jj + 4], pg[:, 8 * jj + 4:8 * jj + 8], dtb3, ALU.add)
            R.act(beta.v, beta.v, AF.Ln, bias=1.0)
            R.act(beta.v, beta.v, AF.Exp, scale=-1.0)
            R.ts("dve", nbeta.v, beta.v, -1.0, ALU.mult)
            softplus(R, tA.v, tA.v, 16, 0)
            for jj in range(4):
                R.tt("dve", gtm[:, 4 * jj:4 * jj + 4], tA[:, 4 * jj:4 * jj + 4], dvec[:, 4:8], ALU.mult)
            pg2 = LatePD()
            R.mm(pg2[:, 0:16], L2, gtm.v)
            R.mm(pg2[:, 16:32], L1, gtm.v)
            R.mm(pg2[:, 32:48], CM0, gtm.v)
            R.act(egc.v, pg2[:, 0:16], AF.Exp)
            R.act(erem.v, pg2[:, 16:32], AF.Exp)
            R.act(egl0.v, pg2[:, 32:48], AF.Exp)
            R.cut()

            rot = [0]

            def conv4(R, ch, wcol, bias, out_v, pdt):
                rw = raw[rot[0] % 2]
                rot[0] += 1
                R.copy("pool", rw[:, 0:3], halo[ch][:, 0:3])
                R.copy("act", rw[:, 3:3 + TT], pdt.v)
                R.copy("pool", halo[ch][:, 0:3], rw[:, TT:TT + 3])
                if bias is None:
                    R.ts("dve", out_v, rw[:, 0:TT], vecs[:, wcol:wcol + 1], ALU.mult)
                else:
                    R.ts("dve", out_v, rw[:, 0:TT], vecs[:, wcol:wcol + 1], ALU.mult, bias, ALU.add)
                for k in range(1, 4):
                    R.stt(out_v, rw[:, k:k + TT], vecs[:, wcol + k:wcol + k + 1], out_v, ALU.mult, ALU.add)

            R2 = Rec()
            for g in range(6):
                Rc = R if g < 3 else R2
                wslot = ring_next(Rc, "A")
                for n in range(4):
                    ch = 4 * g + n
                    pdt = LatePD()
                    for c in range(NCH):
                        Rc.mm(pdt.v, wslot[:, 512 * c + 128 * n:512 * c + 128 * (n + 1)], hA[c].v,
                             start=(c == 0), stop=(c == NCH - 1))
                    Rc.cut()
                    if ch < 12:
                        a_ = acc[ch % 2]
                        conv4(Rc, ch, o_cw + 4 * ch, None, a_.v, pdt)
                        sg_ = rs[ch % 2]
                        sig_of(Rc, sg_.v, a_.v, -1.0)
                        if ch >= 8:
                            Rc.tt("pool", vT[ch - 8].v, a_.v, sg_.v, ALU.mult)
                        else:
                            kq = 1 if ch < 4 else 0
                            h = ch % 4
                            Rc.tt("pool", a_.v, a_.v, sg_.v, ALU.mult)
                            sq_ = sqb[ch % 2]
                            Rc.act(sq_.v, a_.v, AF.Square)
                            Rc.mm(pstat.v, ONESb, sq_.v)
                            r_ = rs[ch % 2]
                            if kq == 1:
                                Rc.act(r_.v, pstat.v, AF.Ln, scale=128.0, bias=128.0 * EPS)
                            else:
                                Rc.act(r_.v, pstat.v, AF.Ln, scale=1.0, bias=EPS)
                            Rc.act(r_.v, r_.v, AF.Exp, scale=-0.5)
                            Rc.tt("dve", qkn[kq][:, h, :, :],
                                 V(a_, a_.ap.rearrange("p (j t) -> p j t", j=4)),
                                 V(r_, r_.ap.rearrange("p (j t) -> p j t", j=4)), ALU.mult)
                    elif ch < 16:
                        sg_ = rs[ch % 2]
                        a_ = acc[ch % 2]
                        Rc.copy("act", a_.v, pdt.v)
                        sig_of(Rc, sg_.v, a_.v, -1.0)
                        Rc.tt("pool", sz[ch - 12].v, a_.v, sg_.v, ALU.mult)
                    elif ch < 20:
                        sg_ = rs[ch % 2]
                        a_ = acc[ch % 2]
                        Rc.copy("act", a_.v, pdt.v)
                        Rc.act(sg_.v, a_.v, AF.Square)
                        Rc.ts("pool", sg_.v, sg_.v, 0.044715, ALU.mult, 1.0, ALU.add)
                        Rc.tt("dve", sg_.v, a_.v, sg_.v, ALU.mult)
                        sig_of(Rc, sg_.v, sg_.v, -1.5957691216057308)
                        Rc.tt("dve", gg[ch - 16].v, a_.v, sg_.v, ALU.mult)
                    else:
                        lc = ch - 20
                        xr, ra, a2, ig = lt
                        conv4(Rc, 12 + lc, o_lcw + 4 * lc, vecs[:, o_lcb + lc:o_lcb + lc + 1], xr.v, pdt)
                        Rc.copy("act", xrb.v, xr.v)
                        pr = LatePD()
                        Rc.mm(pr.v, wgate[:, 0, lc, :], xrb.v)
                        pi = LatePD()
                        Rc.mm(pi.v, wgate[:, 1, lc, :], xrb.v)
                        Rc.act(ra.v, pr.v, AF.Exp, scale=-1.0, bias=dvec[:, 8 + lc:9 + lc])
                        Rc.act(ra.v, ra.v, AF.Ln, bias=1.0)
                        Rc.act(ra.v, ra.v, AF.Exp, scale=-1.0)
                        Rc.act(a2.v, ra.v, AF.Exp, scale=dvec[:, 16 + lc:17 + lc])
                        Rc.act(ra.v, ra.v, AF.Exp, scale=dvec[:, lc:lc + 1])
                        Rc.act(a2.v, a2.v, AF.Ln, scale=-1.0, bias=1.0)
                        Rc.act(a2.v, a2.v, AF.Exp, scale=0.5)
                        Rc.act(ig.v, pi.v, AF.Exp, scale=-1.0, bias=dvec[:, 12 + lc:13 + lc])
                        Rc.act(ig.v, ig.v, AF.Ln, bias=1.0)
                        Rc.act(ig.v, ig.v, AF.Exp, scale=-1.0)
                        Rc.tt("pool", ig.v, ig.v, xr.v, ALU.mult)
                        Rc.tt("pool", ig.v, ig.v, a2.v, ALU.mult)
                        Rc.scan(xr.v, ra.v, ig.v, hst[lc][:, 0:1])
                        Rc.copy("pool", hst[lc][:, 0:1], xr[:, TT - 1:TT])
                        Rc.tt("pool", mix[4 + lc].v, xr.v, gg[lc].v, ALU.mult)
                    Rc.cut()

            def hv(lst):
                return [t.v for t in lst]

            def pre_jj(Rp, jj):
                st_ = jj % NSET
                u, ua = UT, UTA
                p, pa = UP[st_], UPA[st_]
                cols = [4 * jj + h for h in range(NH)]
                kT4 = qkn[0][:, :, jj, :]
                qT4 = qkn[1][:, :, jj, :]
                for h in range(NH):
                    Rp.ts("dve", UG[h].v, L2, gtm[:, cols[h]:cols[h] + 1], ALU.mult)
                Rp.cut()
                q = LatePD()
                for h in range(NH):
                    Rp.mm(q[:, 128 * h:128 * (h + 1)], L1, UG[h].v)
                Rp.op("act", lambda e, q=q: e.activation(out=ua["decT"], in_=q.v.ap, func=AF.Exp),
                      hv(u["decT"]), [q.v], cost=0.56)
                Rp.op("pool", lambda e: e.tensor_tensor(out=ua["decT"], in0=ua["decT"], in1=c4_h[:, 2, :, :], op=ALU.mult),
                      hv(u["decT"]), hv(u["decT"]) + [c4.v], cost=1.26)
                Rp.op("pool", lambda e: e.tensor_tensor(out=ua["decS"], in0=ua["decT"], in1=c4_h[:, 0, :, :], op=ALU.mult),
                      hv(u["decS"]), hv(u["decT"]) + [c4.v], cost=1.26)
                Rp.cut()
                q = LatePD()
                for h in range(NH):
                    Rp.mm(q[:, 128 * h:128 * (h + 1)], kT4.t[:, h, jj, :], kT4.t[:, h, jj, :])
                for h in range(NH):
                    Rp.stt(u["N0"][h].v, q[:, 128 * h:128 * (h + 1)], nbeta[:, cols[h]:cols[h] + 1],
                           u["decS"][h].v, ALU.mult, ALU.mult)
                Rp.cut()
                q = LatePD()
                for h in range(NH):
                    Rp.mm(q[:, 128 * h:128 * (h + 1)], kT4.t[:, h, jj, :], qT4.t[:, h, jj, :])
                Rp.op("dve", lambda e, q=q: e.tensor_tensor(out=pa["attnT"], in0=q.v.ap, in1=ua["decT"], op=ALU.mult),
                      hv(p["attnT"]), [q.v] + hv(u["decT"]), cost=0.62)
                Rp.cut()
                q = LatePD()
                for h in range(NH):
                    Rp.mm(q[:, 128 * h:128 * (h + 1)], ONES, UG[h].v)
                Rp.op("act", lambda e, q=q: e.activation(out=ua["egcB"], in_=q.v.ap, func=AF.Exp),
                      hv(u["egcB"]), [q.v], cost=0.56)
                Rp.op("dve", lambda e: e.tensor_tensor(out=pa["qg"], in0=qT4.ap, in1=ua["egcB"], op=ALU.mult),
                      hv(p["qg"]), [qT4] + hv(u["egcB"]), cost=0.62)
                Rp.cut()
                q = LatePD()
                for h in range(NH):
                    Rp.mm(q[:, 128 * h:128 * (h + 1)], kT4.t[:, h, jj, :], IDb)
                for h in range(NH):
                    Rp.act(u["kg"][h].v, q[:, 128 * h:128 * (h + 1)], AF.Copy, scale=egc[:, cols[h]:cols[h] + 1])
                    Rp.act(p["kdec"][h].v, q[:, 128 * h:128 * (h + 1)], AF.Copy, scale=erem[:, cols[h]:cols[h] + 1])
                Rp.cut()
                q = LatePD()
                for h in range(NH):
                    Rp.mm(q[:, 128 * h:128 * (h + 1)], vT[h][:, 128 * jj:128 * (jj + 1)], IDb)
                Rp.op("act", lambda e, q=q: e.activation(out=pa["vtok"], in_=q.v.ap, func=AF.Copy),
                      hv(p["vtok"]), [q.v], cost=0.56)
                Rp.cut()
                q = LatePD()
                for h in range(NH):
                    Rp.mm(q[:, 128 * h:128 * (h + 1)], u["N0"][h].v, IDb)
                Rp.op("act", lambda e, q=q: e.activation(out=ua["Y0"], in_=q.v.ap, func=AF.Copy),
                      hv(u["Y0"]), [q.v], cost=0.56)
                Rp.op("pool", lambda e: e.tensor_tensor(out=pa["Pfin"], in0=ua["N0"], in1=c4_h[:, 1, :, :], op=ALU.add),
                      hv(p["Pfin"]), hv(u["N0"]) + [c4.v], cost=1.26)
                Rp.cut()
                Xs = ["N0", "Xa", "N0", "Xa", "N0", "Xa"]
                Ys = ["Y0", "Ya", "Y0", "Ya", "Y0", "Ya", "Y0"]
                Ps = [("p", "Pfin"), ("u", "Ptmp"), ("p", "Pfin"), ("u", "Ptmp"), ("p", "Pfin"), ("u", "Ptmp"),
                      ("p", "Pfin")]
                NLEV = 6

                def PT(k):
                    w, nm = Ps[k]
                    return (u[nm], ua[nm]) if w == "u" else (p[nm], pa[nm])
                for r in range(0, NLEV + 1):
                    Yc = Ys[r]
                    qy = qx = qp = None
                    if r < NLEV:
                        Xc = Xs[r]
                        qy = LatePD()
                        for h in range(NH):
                            Rp.mm(qy[:, 128 * h:128 * (h + 1)], u[Xc][h].v, u[Yc][h].v)
                    if r < NLEV - 1:
                        qx = LatePD()
                        for h in range(NH):
                            Rp.mm(qx[:, 128 * h:128 * (h + 1)], u[Yc][h].v, u[Xc][h].v)
                    if r >= 1:
                        Pc_t, Pc_a = PT(r - 1)
                        Pn_t, Pn_a = PT(r)
                        qp = LatePD()
                        for h in range(NH):
                            Rp.mm(qp[:, 128 * h:128 * (h + 1)], u[Yc][h].v, Pc_t[h].v)
                    if qy is not None:
                        Yn = Ys[r + 1]
                        Rp.op("dve", lambda e, qy=qy, Yn=Yn: e.tensor_copy(out=ua[Yn], in_=qy.v.ap),
                              hv(u[Yn]), [qy.v], cost=0.62)
                    if qx is not None:
                        Xn = Xs[r + 1]
                        Rp.op("act", lambda e, qx=qx, Xn=Xn: e.activation(out=ua[Xn], in_=qx.v.ap, func=AF.Copy),
                              hv(u[Xn]), [qx.v], cost=0.56)
                    if qp is not None:
                        Rp.op("dve", lambda e, qp=qp, Pn_a=Pn_a, Pc_a=Pc_a: e.tensor_tensor(out=Pn_a, in0=qp.v.ap, in1=Pc_a, op=ALU.add),
                              hv(Pn_t), [qp.v] + hv(Pc_t), cost=0.62)
                    Rp.cut()
                q = LatePD()
                for h in range(NH):
                    Rp.mm(q[:, 128 * h:128 * (h + 1)], u["kg"][h].v, p["Pfin"][h].v)
                Rp.op("act", lambda e, q=q: e.activation(out=pa["nw0"], in_=q.v.ap, func=AF.Copy, scale=-1.0),
                      hv(p["nw0"]), [q.v], cost=0.56)
                Rp.cut()

            def rec_jj(Rr, jj):
                st_ = jj % NSET
                p, pa = UP[st_], UPA[st_]
                cols = [4 * jj + h for h in range(NH)]
                for c in range(1):
                    lo, hi = 0, 128
                    for h in range(NH):
                        Rr.mm(QC[lo:hi, 128 * h:128 * (h + 1)], p["Pfin"][h][lo:hi, lo:hi], p["vtok"][h][lo:hi, :],
                              start=True, stop=False)
                        Rr.mm(QC[lo:hi, 128 * h:128 * (h + 1)], p["nw0"][h][:, lo:hi], Sbf[h].v, start=False, stop=True)
                    for h in range(NH):
                        Rr.ts("dve", VNEW[h][lo:hi, :], QC[lo:hi, 128 * h:128 * (h + 1)],
                              beta[lo:hi, cols[h]:cols[h] + 1], ALU.mult)
                    Rr.cut()
                    for h in range(NH):
                        Rr.mm(QD[:, 128 * h + lo:128 * h + hi], Sbf[h].v, p["qg"][h][:, lo:hi], start=True, stop=False)
                        Rr.mm(QD[:, 128 * h + lo:128 * h + hi], VNEW[h][lo:hi, :], p["attnT"][h][lo:hi, lo:hi],
                              start=False, stop=True)
                    for h in range(NH):
                        Rr.mm(QC[:, 128 * h:128 * (h + 1)], p["kdec"][h][lo:hi, :], VNEW[h][lo:hi, :])
                    egl = egl0 if c == 0 else egl1
                    for h in range(NH):
                        Rr.stt(S32[h].v, S32[h].v, egl[:, cols[h]:cols[h] + 1], QC[:, 128 * h:128 * (h + 1)],
                               ALU.mult, ALU.add)
                    Rr.op("act", lambda e: e.activation(out=SbfA, in_=S32A, func=AF.Copy), hv(Sbf), hv(S32), cost=0.56)
                    Rr.cut()
                Rr.op("act", lambda e, jj=jj: e.activation(out=osb_h[:, :, 128 * jj:128 * (jj + 1)],
                                                           in_=qd_h[:, :].rearrange("p (h t) -> p h t", h=NH), func=AF.Copy),
                      hv(osb), [QD.v], cost=0.56)
                Rr.cut()

            Rg = Rec()
            pre_jj(Rg, 0)
            gd = Rec()
            gd.chunks = merge_chunks([Rg])
            for jj in range(4):
                Rr = Rec()
                rec_jj(Rr, jj)
                Rp = Rec()
                if jj + 1 < 4:
                    pre_jj(Rp, jj + 1)
                gd.chunks.extend(merge_chunks([Rr, Rp]))
            flush_list = [[R], [R2, gd]]
            Rn = Rec()
            for h in range(NH):
                sq_ = sqb[h % 2]
                Rn.act(sq_.v, osb[h].v, AF.Square)
                Rn.mm(pstat.v, ONESb, sq_.v)
                r_ = rs[h % 2]
                Rn.act(r_.v, pstat.v, AF.Ln, scale=1.0 / 128, bias=EPS)
                Rn.act(r_.v, r_.v, AF.Exp, scale=-0.5)
                ot = acc[h % 2]
                Rn.stt(ot.v, osb[h].v, vecs[:, o_gnw:o_gnw + 1], r_.v, ALU.mult, ALU.mult)
                Rn.tt("dve", mix[h].v, ot.v, sz[h].v, ALU.mult)
                Rn.cut()
            flush_list.append([Rn])
            return flush_list

        def gen_B(ti):
            s_i, j_i = divmod(ti, NT)
            tok0 = s_i * S + j_i * TT
            R = Rec(prio=1)
            for c in range(NCH):
                R.dma("sp", xBc[c].v, V(xT_t, xT_d[c, :, tok0:tok0 + TT]))
            R.cut()
            for g in range(2):
                wslot = ring_next(R, "B")
                for n in range(4):
                    dc = 4 * g + n
                    pdt = next_pdB()
                    for c in range(NCH):
                        R.mm(pdt.v, wslot[:, 512 * c + 128 * n:512 * c + 128 * (n + 1)], mix[c].v,
                             start=(c == 0), stop=(c == NCH - 1))
                    R.tt("dve", xBc[dc].v, xBc[dc].v, pdt.v, ALU.add)
                    R.cut()
            rmsnorm_to(R, lambda c: xBc[c].v, hB, o_nw2, rsB)
            for half in range(2):
                for g in range(4):
                    wslot = ring_next(R, "B")
                    for n in range(4):
                        fc = 4 * g + n
                        pdt = next_pdB()
                        for c in range(NCH):
                            R.mm(pdt.v, wslot[:, 512 * c + 128 * n:512 * c + 128 * (n + 1)], hB[c].v,
                                 start=(c == 0), stop=(c == NCH - 1))
                        rl = reluB[fc % 2]
                        R.act(rl.v, pdt.v, AF.Relu)
                        R.tt("pool", uT[fc].v, rl.v, rl.v, ALU.mult)
                        R.cut()
                for g in range(4):
                    wslot = ring_next(R, "B")
                    for n in range(2):
                        dc = 2 * g + n
                        pdt = next_pdB()
                        for fc in range(16):
                            R.mm(pdt.v, wslot[:, 256 * fc + 128 * n:256 * fc + 128 * (n + 1)], uT[fc].v,
                                 start=(fc == 0), stop=(fc == 15))
                        R.tt("dve", xBc[dc].v, xBc[dc].v, pdt.v, ALU.add)
                        R.cut()
            for c in range(NCH):
                R.act(hB[c].v, xBc[c].v, AF.Square)
            for c in range(NCH):
                R.mm(pstat.v, ONESb, hB[c].v, start=(c == 0), stop=(c == NCH - 1))
            R.act(rsB.v, pstat.v, AF.Ln, scale=1.0 / D, bias=EPS)
            R.act(rsB.v, rsB.v, AF.Exp, scale=-0.5)
            for c in range(NCH):
                R.stt(xBc[c].v, xBc[c].v, vecs[:, o_fnw + c:o_fnw + c + 1], rsB.v, ALU.mult, ALU.mult)
                R.dma("sp", T(yT_d[c, :, tok0:tok0 + TT]).v, xBc[c].v)
            R.cut()
            return [[R]]

        for grp in gen_A(0):
            flush(grp)
        for ti in range(ntiles):
            Bl = gen_B(ti)
            if ti + 1 < ntiles:
                Al = gen_A(ti + 1)
                Aflat = Rec()
                Aflat.chunks = []
                for grp in Al:
                    Aflat.chunks.extend(merge_chunks(grp))
                flush([Bl[0][0], Aflat])
            else:
                flush(Bl[0])
        import os
        if os.environ.get("KDEBUG"):
            print("SBUF bytes/partition:", sb_bytes[0], "ops:", len(spec))
        order = list_schedule(spec)
        for i in order:
            eng, fn, reads, writes, dma, cost, prio = spec[i]
            P.add(eng, fn, reads, writes, dma)
        P.emit(nc)
    return nc


def _masks():
    m = np.arange(128)
    same = np.ones((128, 128), bool)
    cm = np.zeros((128, 8, 128), np.float32)
    cm[:, 0, :] = (m[:, None] > m[None, :]) & same
    cm[:, 1, :] = (m[:, None] <= m[None, :]) & same
    cm[:, 2, :] = (m[:, None] <= m[None, :]) & same
    cm[:, 3, :] = (m[:, None] < m[None, :]) & same
    cm[:, 4, :] = np.eye(128)
    cm[:, 5, :] = 1.0
    cm[:, 6, :] = 1.0
    cm[:, 7, :] = 0.0
    return cm


def prep_shared(inp):
    f = np.float32
    w_in = np.asarray(inp["w_in"], f)[0]
    wmain = np.concatenate([w_in[:, 0:1536], w_in[:, 1536:2048], w_in[:, 2568:3080], w_in[:, 2056:2568]], axis=1)
    wba_ = w_in[:, 2048:2056]
    w_out = np.asarray(inp["w_out"], f)[0]
    w1 = np.asarray(inp["w_ff1"], f)[0]
    w2 = np.asarray(inp["w_ff2"], f)[0]
    wts = np.zeros((NGRP, 128, 4096), f)

    def colgrp(W, g):
        return W[:, 512 * g:512 * (g + 1)].reshape(8, 128, 512).transpose(1, 0, 2).reshape(128, 4096)
    for g in range(6):
        wts[g] = colgrp(wmain, g)
    for g in range(2):
        wts[6 + g] = colgrp(w_out, g)
    for half in range(2):
        base = 8 + 8 * half
        for g in range(4):
            wts[base + g] = colgrp(w1, 4 * half + g)
        for g in range(4):
            blk = w2[2048 * half:2048 * (half + 1), 256 * g:256 * (g + 1)]
            wts[base + 4 + g] = blk.reshape(16, 128, 2, 128).transpose(1, 0, 2, 3).reshape(128, 4096)
    vecs = np.zeros((128, NVEC), f)

    def pc(v, n):
        return np.asarray(v, f).reshape(n, 128).T
    vecs[:, 0:8] = pc(inp["norm_mix_w"][0], 8)
    vecs[:, 8:16] = pc(inp["norm_mlp_w"][0], 8)
    vecs[:, 16:24] = pc(inp["final_norm_w"], 8)
    cw = np.asarray(inp["gdn_conv_w"], f)[0]
    vecs[:, 24:72] = cw.reshape(4, 12, 128).transpose(2, 1, 0).reshape(128, 48)
    lcw = np.asarray(inp["lru_conv_w"], f)[0]
    vecs[:, 72:88] = lcw.reshape(4, 4, 128).transpose(2, 1, 0).reshape(128, 16)
    vecs[:, 88:92] = pc(inp["lru_conv_b"][0], 4)
    vecs[:, 92:96] = pc(np.asarray(inp["lru_gate_a_b"], f)[0].reshape(512), 4)
    vecs[:, 96:100] = pc(np.asarray(inp["lru_gate_x_b"], f)[0].reshape(512), 4)
    vecs[:, 100:104] = pc(inp["lru_a_param"][0], 4)
    vecs[:, 104:108] = np.broadcast_to(np.asarray(inp["gdn_A_log"], f)[0][None, :], (128, 4))
    vecs[:, 108:112] = np.broadcast_to(np.asarray(inp["gdn_dt_bias"], f)[0][None, :], (128, 4))
    vecs[:, 112] = np.asarray(inp["gdn_norm_w"], f)[0]
    wgate = np.zeros((128, 2, 4, 128), f)
    for gi, key in enumerate(("lru_gate_a_w", "lru_gate_x_w")):
        wg = np.asarray(inp[key], f)[0]
        for lc in range(4):
            for b in range(2):
                wgate[64 * b:64 * (b + 1), gi, lc, 64 * b:64 * (b + 1)] = wg[2 * lc + b]
    wba = wba_.reshape(8, 128, 8).transpose(1, 0, 2).copy()
    return {"wts": wts, "cm": _masks(), "vecs": vecs, "wgate": wgate, "wba": np.ascontiguousarray(wba)}


def prep_x(xs):
    nseq, S, _ = xs.shape
    return np.ascontiguousarray(xs.reshape(nseq * S, NCH, 128).transpose(1, 2, 0))


def unprep_y(yT, nseq, S):
    return np.ascontiguousarray(yT.transpose(2, 0, 1).reshape(nseq, S, D))


_NC_CACHE = {}


def kernel(**inputs):
    x = np.asarray(inputs["x"], np.float32)
    B, S, _ = x.shape
    ncores = 8
    nseq = B // ncores
    shared = prep_shared(inputs)
    key = (nseq, S)
    if key not in _NC_CACHE:
        _NC_CACHE[key] = build_program(nseq, S)
    nc = _NC_CACHE[key]
    in_maps = []
    for c in range(ncores):
        m = dict(shared)
        m["xT"] = prep_x(x[c * nseq:(c + 1) * nseq])
        in_maps.append(m)
    res = run_bass_kernel_spmd(nc, in_maps, core_ids=list(range(ncores)))
    outs = [unprep_y(np.asarray(r["yT"]), nseq, S) for r in res.results]
    return np.concatenate(outs, axis=0).astype(np.float32)
```

```python
import numpy as np
import concourse.bass as bass
import concourse.mybir as mybir
from concourse.bass_utils import run_bass_kernel_spmd
from contextlib import ExitStack

F32 = mybir.dt.float32
BF16 = mybir.dt.bfloat16
AF = mybir.ActivationFunctionType
ALU = mybir.AluOpType

D = 1024
NCH = 8
TT = 512
NH = 4
EPS = 1e-6
N_DMA_SEMS = 24
N_SP_SEMS = 16
NGRP = 24
NVEC = 120


class V:
    __slots__ = ("t", "ap")

    def __init__(self, t, ap):
        self.t = t
        self.ap = ap


class LatePD:
    __slots__ = ("bound",)

    def __init__(self):
        self.bound = None

    @property
    def v(self):
        return LV(self, None, TT)

    def __getitem__(self, idx):
        n = TT
        if isinstance(idx, tuple) and isinstance(idx[-1], slice) and idx[-1].start is not None:
            n = idx[-1].stop - idx[-1].start
        return LV(self, idx, n)


class LV:
    __slots__ = ("t", "idx", "n")

    def __init__(self, t, idx, n):
        self.t = t
        self.idx = idx
        self.n = n

    @property
    def ap(self):
        b = self.t.bound
        return b.ap if self.idx is None else b.ap[self.idx]


class T:
    __slots__ = ("ap", "name", "last_w", "readers", "const", "root", "excl")

    def __init__(self, ap, name="", const=False, parent=None, excl=False):
        self.ap = ap
        self.name = name
        self.last_w = None
        self.readers = []
        self.const = const
        self.excl = excl
        self.root = parent.root if parent is not None else self

    def __getitem__(self, idx):
        return V(self.root, self.ap[idx])

    @property
    def v(self):
        return V(self.root, self.ap)


class Op:
    __slots__ = ("eng", "fn", "deps", "signal", "count", "is_dma", "dma_slot", "dma_val", "prev_dma")

    def __init__(self, eng, fn, is_dma):
        self.eng = eng
        self.fn = fn
        self.deps = []
        self.signal = False
        self.count = 0
        self.is_dma = is_dma
        self.dma_slot = -1
        self.dma_val = 0
        self.prev_dma = None


class Prog:
    ENGS = ("pe", "act", "dve", "pool", "sp")

    def __init__(self):
        self.ops = []
        self.n_dma_sp = 0
        self.n_dma_pl = 0
        self.dma_last = [None] * N_DMA_SEMS

    def add(self, eng, fn, reads=(), writes=(), dma=False):
        op = Op(eng, fn, dma)
        deps = {}
        for t in reads:
            if t.last_w is not None:
                deps[id(t.last_w)] = t.last_w
            if t.excl:
                for r in t.readers:
                    if r.eng != eng:
                        deps[id(r)] = r
        for t in writes:
            if t.last_w is not None:
                deps[id(t.last_w)] = t.last_w
            for r in t.readers:
                deps[id(r)] = r
        for t in reads:
            if not t.const:
                t.readers.append(op)
        for t in writes:
            t.last_w = op
            t.readers = []
        for d in deps.values():
            if d is op:
                continue
            if d.is_dma:
                op.deps.append(d)
            elif d.eng == eng and not dma:
                if eng != "pe":
                    op.deps.append(d)
                    d.signal = True
            else:
                op.deps.append(d)
                d.signal = True
        if dma:
            if eng == "sp":
                k = self.n_dma_sp
                self.n_dma_sp += 1
                slot = k % N_SP_SEMS
                val = 16 * (k // N_SP_SEMS + 1)
            else:
                k = self.n_dma_pl
                self.n_dma_pl += 1
                slot = N_SP_SEMS + k % (N_DMA_SEMS - N_SP_SEMS)
                val = 16 * (k // (N_DMA_SEMS - N_SP_SEMS) + 1)
            op.dma_slot = slot
            op.dma_val = val
            op.prev_dma = self.dma_last[slot]
            self.dma_last[slot] = op
        self.ops.append(op)
        return op

    def emit(self, nc):
        counts = {e: 0 for e in self.ENGS}
        for op in self.ops:
            if not op.is_dma and op.signal:
                counts[op.eng] += 1
                op.count = counts[op.eng]
        with ExitStack() as es:
            sems = {e: es.enter_context(nc.semaphore("s_" + e)) for e in self.ENGS}
            dsems = [es.enter_context(nc.semaphore("d%d" % i)) for i in range(N_DMA_SEMS)]
            block = es.enter_context(nc.Block())
            names = {"pe": "tensor", "act": "scalar", "dve": "vector", "pool": "gpsimd", "sp": "sync"}
            dma_last = self.dma_last
            for e in self.ENGS:
                myops = [op for op in self.ops if op.eng == e]

                def body(eng, myops=myops, e=e):
                    waited = {}

                    def wait(sem, val, key):
                        if waited.get(key, 0) >= val:
                            return
                        waited[key] = val
                        eng.wait_ge(sem, val)

                    for op in myops:
                        for d in op.deps:
                            if d.is_dma:
                                wait(dsems[d.dma_slot], d.dma_val, ("d", d.dma_slot))
                            else:
                                wait(sems[d.eng], d.count, d.eng)
                        if op.is_dma:
                            if op.prev_dma is not None:
                                wait(dsems[op.dma_slot], op.prev_dma.dma_val, ("d", op.dma_slot))
                            op.fn(eng).then_inc(dsems[op.dma_slot], 16)
                        else:
                            ins = op.fn(eng)
                            if op.signal:
                                ins.then_inc(sems[e], 1)
                    if e == "sp":
                        for d in dma_last:
                            if d is not None:
                                wait(dsems[d.dma_slot], d.dma_val, ("d", d.dma_slot))

                getattr(block, names[e])(body)


import os as _os
DEBUG_TAGS = bool(_os.environ.get("KCRIT"))
EVAC_PRIO = bool(int(_os.environ.get("KEVAC", "0")))
TAGS = {}


def fsz(v):
    if isinstance(v, LV):
        return v.n
    n = 1
    for d in v.ap.shape[1:]:
        n *= d
    return n


class Rec:
    def __init__(self, prio=0):
        self.chunks = [[]]
        self.prio = prio

    def cut(self):
        if self.chunks[-1]:
            self.chunks.append([])

    def op(self, eng, fn, outs, ins, dma=False, cost=0.3):
        if DEBUG_TAGS:
            import sys
            f = sys._getframe(1)
            while f.f_code.co_name in ("op", "mm", "tr", "act", "tt", "ts", "stt", "copy", "recip", "memset", "scan", "dma"):
                f = f.f_back
            TAGS[id(fn)] = f.f_lineno
        prio = self.prio
        if EVAC_PRIO and eng in ("act", "dve") and any(isinstance(v, LV) for v in ins):
            prio = -1
        self.chunks[-1].append((eng, fn, tuple(ins), tuple(outs), dma, cost, prio))

    def mm(self, out, lhsT, rhs, start=True, stop=True):
        passes = 4 if (not isinstance(rhs, LV) and rhs.ap.dtype == F32) else 1
        self.op("pe", lambda e: e.matmul(out.ap, lhsT=lhsT.ap, rhs=rhs.ap, start=start, stop=stop),
                [out], [lhsT, rhs], cost=0.035 + 0.000417 * passes * max(fsz(rhs), 64))

    def tr(self, out, in_, ident):
        self.op("pe", lambda e: e.transpose(out.ap, in_.ap, ident.ap), [out], [in_, ident], cost=0.1)

    def act(self, out, in_, func, scale=None, bias=None):
        kw = {}
        ins = [in_]
        if scale is not None:
            if isinstance(scale, V):
                kw["scale"] = scale.ap
                ins.append(scale)
            else:
                kw["scale"] = scale
        if bias is not None:
            if isinstance(bias, V):
                kw["bias"] = bias.ap
                ins.append(bias)
            else:
                kw["bias"] = bias
        self.op("act", lambda e: e.activation(out=out.ap, in_=in_.ap, func=func, **kw), [out], ins,
                cost=0.15 + 0.0008 * fsz(in_))

    def tt(self, eng, out, a, b, op):
        c = (0.07 + 0.00105 * fsz(out)) if eng == "dve" else (0.1 + 0.00227 * fsz(out))
        self.op(eng, lambda e: e.tensor_tensor(out=out.ap, in0=a.ap, in1=b.ap, op=op), [out], [a, b], cost=c)

    def ts(self, eng, out, a, s1, op0, s2=None, op1=None):
        ins = [a]
        if isinstance(s1, V):
            ins.append(s1)
            s1 = s1.ap
        if isinstance(s2, V):
            ins.append(s2)
            s2 = s2.ap
        c = (0.07 + 0.00105 * fsz(out)) if eng == "dve" else (0.1 + 0.00115 * fsz(out))
        if isinstance(s1, bass.AP) and eng == "pool":
            c = 0.1 + 0.016 * fsz(out)
        if op1 is None:
            self.op(eng, lambda e: e.tensor_scalar(out=out.ap, in0=a.ap, scalar1=s1, scalar2=None, op0=op0),
                    [out], ins, cost=c)
        else:
            self.op(eng, lambda e: e.tensor_scalar(out=out.ap, in0=a.ap, scalar1=s1, scalar2=s2, op0=op0, op1=op1),
                    [out], ins, cost=c)

    def stt(self, out, a, scalar, b, op0, op1):
        ins = [a, b]
        if isinstance(scalar, V):
            ins.append(scalar)
            scalar = scalar.ap
        self.op("dve", lambda e: e.scalar_tensor_tensor(out=out.ap, in0=a.ap, scalar=scalar, in1=b.ap,
                                                        op0=op0, op1=op1), [out], ins,
                cost=0.07 + 0.00133 * fsz(out))

    def copy(self, eng, out, in_):
        if eng == "act":
            self.op("act", lambda e: e.activation(out=out.ap, in_=in_.ap, func=AF.Copy), [out], [in_],
                    cost=0.15 + 0.0008 * fsz(in_))
        else:
            c = (0.07 + 0.00105 * fsz(out)) if eng == "dve" else (0.1 + 0.0012 * fsz(out))
            self.op(eng, lambda e: e.tensor_copy(out=out.ap, in_=in_.ap), [out], [in_], cost=c)

    def recip(self, out, in_):
        self.op("dve", lambda e: e.reciprocal(out=out.ap, in_=in_.ap), [out], [in_], cost=0.07 + 0.006 * fsz(out))

    def memset(self, eng, out, val):
        self.op(eng, lambda e: e.memset(out.ap, val), [out], [], cost=0.1 + 0.0006 * fsz(out))

    def scan(self, out, d0, d1, init):
        ins = [d0, d1]
        if isinstance(init, V):
            ins.append(init)
            init = init.ap
        self.op("dve", lambda e: e.tensor_tensor_scan(out=out.ap, data0=d0.ap, data1=d1.ap, initial=init,
                                                      op0=ALU.mult, op1=ALU.add), [out], ins,
                cost=0.07 + 0.0022 * fsz(out))

    def dma(self, eng, out, in_, extra_ins=()):
        self.op(eng, lambda e: e.dma_start(out=out.ap, in_=in_.ap), [out], [in_] + list(extra_ins), dma=True,
                cost=2.2 + out.ap.nbytes() / 150e3)


def list_schedule(spec, window=None):
    import os
    if window is None:
        window = int(os.environ.get("KWIN", "128"))
    n = len(spec)
    deps = [None] * n
    last_w = {}
    readers = {}
    for i, (eng, fn, reads, writes, dma, cost, prio) in enumerate(spec):
        d = set()
        for t in reads:
            k = id(t)
            if k in last_w:
                d.add(last_w[k])
            if t.excl:
                for r in readers.get(k, ()):
                    if spec[r][0] != eng:
                        d.add(r)
        for t in writes:
            k = id(t)
            if k in last_w:
                d.add(last_w[k])
            d.update(readers.get(k, ()))
        for t in reads:
            if not t.const:
                readers.setdefault(id(t), []).append(i)
        for t in writes:
            last_w[id(t)] = i
            readers[id(t)] = []
        d.discard(i)
        deps[i] = tuple(d)
    engs = ("pe", "act", "dve", "pool", "sp")
    if int(os.environ.get("KBLEVEL", "1")):
        bl = [0.0] * n
        for i in range(n - 1, -1, -1):
            bl[i] += spec[i][5]
            for d in deps[i]:
                v = bl[i] + 0.4
                if v > bl[d]:
                    bl[d] = v
        spec = [(o[0], o[1], o[2], o[3], o[4], o[5], -bl[i]) for i, o in enumerate(spec)]
    pending = {e: [i for i in range(n) if spec[i][0] == e] for e in engs}
    head = {e: 0 for e in engs}
    done = [False] * n
    finish = [0.0] * n
    efree = {e: 0.0 for e in engs}
    order = []
    LAT = float(os.environ.get("KLAT", "0.4"))
    INF = 1e30
    _sc = {e: float(os.environ.get("KS_" + e, "1.0")) for e in engs}
    if any(v != 1.0 for v in _sc.values()):
        spec = [(o[0], o[1], o[2], o[3], o[4], o[5] * (_sc[o[0]] if not o[4] else 1.0), o[6]) for o in spec]
    _kd = float(os.environ.get("KDMA", "1.0"))
    if _kd != 1.0:
        spec = [(o[0], o[1], o[2], o[3], o[4], o[5] * (_kd if o[4] else 1.0), o[6]) for o in spec]

    def candidate(e):
        lst = pending[e]
        h = head[e]
        while h < len(lst) and done[lst[h]]:
            h += 1
        head[e] = h
        best = None
        bkey = None
        cnt = 0
        j = h
        ef = efree[e]
        while j < len(lst) and cnt < window:
            i = lst[j]
            j += 1
            if done[i]:
                continue
            cnt += 1
            ok = True
            st = ef
            for d in deps[i]:
                if not done[d]:
                    ok = False
                    break
                f = finish[d] + (LAT if spec[d][0] != e or spec[d][4] else 0.05)
                if f > st:
                    st = f
            if not ok:
                continue
            key = (st, spec[i][6], i)
            if bkey is None or key < bkey:
                best, bkey = i, key
                if st <= ef + 1e-9 and spec[i][6] == 0:
                    break
        return best, (bkey[0] if bkey else INF)

    binder = {}
    startt = {}
    last_on = {}
    remaining = n
    while remaining:
        pick = None
        pstart = INF
        for e in engs:
            i, st = candidate(e)
            if i is not None and st < pstart:
                pick, pstart = i, st
        assert pick is not None, "scheduler deadlock"
        eng, fn, reads, writes, dma, cost, prio = spec[pick]
        done[pick] = True
        finish[pick] = pstart + cost
        if DEBUG_TAGS:
            bind = ("eng", last_on.get(eng))
            bt = efree[eng]
            for d in deps[pick]:
                f = finish[d] + (LAT if spec[d][0] != eng or spec[d][4] else 0.05)
                if f > bt + 1e-9:
                    bt = f
                    bind = ("dep", d)
            binder[pick] = bind
            startt[pick] = pstart
            last_on[eng] = pick
        efree[eng] = pstart + (0.15 if dma else cost)
        order.append(pick)
        remaining -= 1
    import os
    if os.environ.get("KDEBUG"):
        busy = {e: 0.0 for e in engs}
        for i in range(n):
            busy[spec[i][0]] += (0.15 if spec[i][4] else spec[i][5])
        print("sched makespan(us):", max(finish), "busy:", {e: round(v) for e, v in busy.items()})
    if DEBUG_TAGS:
        import collections
        cur = max(range(n), key=lambda i: finish[i])
        t_hi = float(os.environ.get("KCRIT_HI", "1e9"))
        t_lo = float(os.environ.get("KCRIT_LO", "0"))
        agg = collections.OrderedDict()
        tot = collections.Counter()
        while cur is not None:
            kind, prev = binder.get(cur, ("eng", None))
            if t_lo <= startt[cur] <= t_hi:
                key = (spec[cur][0], TAGS.get(id(spec[cur][1]), 0), kind)
                a = agg.setdefault(key, [0, 0.0])
                a[0] += 1
                a[1] += finish[cur] - startt[cur]
                tot[(spec[cur][0], kind)] += finish[cur] - startt[cur]
            cur = prev
        print("critical path ops by (eng, line, binding):")
        for k, (c, tme) in sorted(agg.items(), key=lambda x: -x[1][1])[:40]:
            print("  ", k, "n=%d time=%.1f" % (c, tme))
        print("totals:", {k: round(v, 1) for k, v in tot.items()})
    return order


def merge_chunks(recs):
    lists = [[c for c in r.chunks if c] for r in recs]
    lists = [l for l in lists if l]
    pos = [0] * len(lists)
    out = []
    while True:
        best = None
        bestf = None
        for i, l in enumerate(lists):
            if pos[i] < len(l):
                f = (pos[i] + 0.5) / len(l)
                if best is None or f < bestf:
                    best, bestf = i, f
        if best is None:
            break
        out.append(lists[best][pos[best]])
        pos[best] += 1
    return out


def build_program(nseq, S):
    NT = S // TT
    NTOK = nseq * S
    nc = bass.Bass("TRN2", target_bir_lowering=False)
    xT_d = nc.dram_tensor("xT", [NCH, 128, NTOK], F32, kind="ExternalInput").ap()
    wts_d = nc.dram_tensor("wts", [NGRP, 128, 4096], F32, kind="ExternalInput").ap()
    cm_d = nc.dram_tensor("cm", [128, 8, 128], F32, kind="ExternalInput").ap()
    vecs_d = nc.dram_tensor("vecs", [128, NVEC], F32, kind="ExternalInput").ap()
    wgate_d = nc.dram_tensor("wgate", [128, 2, 4, 128], F32, kind="ExternalInput").ap()
    wba_d = nc.dram_tensor("wba", [128, 8, 8], F32, kind="ExternalInput").ap()
    yT_d = nc.dram_tensor("yT", [NCH, 128, NTOK], F32, kind="ExternalOutput").ap()
    wbf_d = nc.dram_tensor("wbf", [NGRP, 128, 4096], BF16, kind="Internal").ap()

    es = ExitStack()
    with es:
        sb_bytes = [0]

        def sb(name, shape, dt):
            n = 1
            for d in shape[1:]:
                n *= d
            sb_bytes[0] += n * (4 if dt == F32 else 2)
            return es.enter_context(nc.sbuf_tensor(name, shape, dt))

        def ps(name, shape, dt):
            return es.enter_context(nc.psum_tensor(name, shape, dt))

        ringA_h = sb("ringA", [128, 2, 4096], BF16)
        ringB_h = sb("ringB", [128, 3, 4096], BF16)
        ringA = [T(ringA_h[:, i, :]) for i in range(2)]
        ringB = [T(ringB_h[:, i, :]) for i in range(3)]
        x_h = sb("x", [128, 2, NCH, TT], F32)
        xA = T(x_h[:, 0, :, :])
        xBc = [T(x_h[:, 1, c, :]) for c in range(NCH)]
        hA_h = sb("hA", [128, NCH, TT], BF16)
        hB_h = sb("hB", [128, NCH, TT], BF16)
        hA = [T(hA_h[:, c, :]) for c in range(NCH)]
        hB = [T(hB_h[:, c, :]) for c in range(NCH)]
        uT_h = sb("uT", [128, 16, TT], BF16)
        uT = [T(uT_h[:, c, :]) for c in range(16)]
        qkn_h = sb("qkn", [128, NH, 4, 2, 128], BF16)
        qkn = [T(qkn_h[:, :, :, kq, :]) for kq in range(2)]
        vT_h = sb("vT", [128, NH, TT], BF16)
        vT = [T(vT_h[:, h, :]) for h in range(NH)]
        sz_h = sb("sz", [128, NH, TT], BF16)
        sz = [T(sz_h[:, h, :]) for h in range(NH)]
        gg_h = sb("gg", [128, 4, TT], BF16)
        gg = [T(gg_h[:, c, :]) for c in range(4)]
        mix_h = sb("mix", [128, NCH, TT], BF16)
        mix = [T(mix_h[:, c, :]) for c in range(NCH)]
        osb_h = sb("osb", [128, NH, TT], BF16)
        osb = [T(osb_h[:, h, :]) for h in range(NH)]
        raw_h = sb("raw", [128, 2, TT + 4], F32)
        raw = [T(raw_h[:, i, :]) for i in range(2)]
        acc_h = sb("acc", [128, 2, TT], F32)
        acc = [T(acc_h[:, i, :]) for i in range(2)]
        sqb_h = sb("sqb", [128, 2, TT], BF16)
        sqb = [T(sqb_h[:, i, :]) for i in range(2)]
        rs_h = sb("rs", [128, 2, TT], F32)
        rs = [T(rs_h[:, i, :]) for i in range(2)]
        rsB = T(sb("rsB", [128, TT], F32)[:, :])
        reluB_h = sb("reluB", [128, 2, TT], BF16)
        reluB = [T(reluB_h[:, i, :]) for i in range(2)]
        lt_h = sb("lt", [128, 4, TT], F32)
        lt = [T(lt_h[:, i, :]) for i in range(4)]
        xrb = T(sb("xrb", [128, TT], BF16)[:, :])
        halo_h = sb("halo", [128, 16, 4], F32)
        halo = [T(halo_h[:, c, :]) for c in range(16)]
        hst_h = sb("hst", [128, 4, 2], F32)
        hst = [T(hst_h[:, c, :]) for c in range(4)]
        S32_h = sb("S32", [128, NH, 128], F32)
        S32 = [T(S32_h[:, h, :]) for h in range(NH)]
        S32A = S32_h[:, :, :]
        Sbf_h = sb("Sbf", [128, NH, 128], BF16)
        Sbf = [T(Sbf_h[:, h, :]) for h in range(NH)]
        SbfA = Sbf_h[:, :, :]
        cm = T(sb("cm_sb", [128, 8, 128], F32)[:, :, :], const=True)
        cmb = T(sb("cmb", [128, 4, 128], BF16)[:, :, :], const=True)
        vecs = T(sb("vecs_sb", [128, NVEC], F32)[:, :], const=True)
        dvec = T(sb("dvec", [128, 24], F32)[:, :], const=True)
        wgate = T(sb("wgate_sb", [128, 2, 4, 128], BF16)[:, :, :, :], const=True)
        wba = T(sb("wba_sb", [128, 8, 8], BF16)[:, :, :], const=True)
        L1, L2, MB, SM, ID, ONES, CM0, CM1 = [cm[:, i, :] for i in range(8)]
        Cb = cmb[:, 0, :]
        SMb = cmb[:, 1, :]
        IDb = cmb[:, 2, :]
        ONESb = cmb[:, 3, :]
        gt_h = sb("gt", [128, 12, 16], F32)
        gtT = [T(gt_h[:, i, :]) for i in range(12)]
        NSET = 2
        TN = ["decT", "decS", "egcB", "N0", "Y0", "Xa", "Ya", "kg", "Ptmp"]
        ut_h = sb("ut", [128, len(TN), NH, 128], BF16)
        UT = {nm: [T(ut_h[:, i, h, :]) for h in range(NH)] for i, nm in enumerate(TN)}
        UTA = {nm: ut_h[:, i, :, :] for i, nm in enumerate(TN)}
        PN = ["Pfin", "kdec", "vtok", "attnT", "qg", "nw0"]
        up_h = sb("up", [128, NSET, len(PN), NH, 128], BF16)
        UP = [{nm: [T(up_h[:, s_, i, h, :]) for h in range(NH)] for i, nm in enumerate(PN)} for s_ in range(NSET)]
        UPA = [{nm: up_h[:, s_, i, :, :] for i, nm in enumerate(PN)} for s_ in range(NSET)]
        ug_h = sb("ug", [128, NH, 128], F32)
        UG = [T(ug_h[:, h, :]) for h in range(NH)]
        vnew_h = sb("vnew", [128, NH, 128], BF16)
        VNEW = [T(vnew_h[:, h, :]) for h in range(NH)]
        c4_h = sb("c4", [128, 3, NH, 128], BF16)
        c4 = T(c4_h[:, :, :, :], const=True)

        def bank(name, dt=F32, n=TT):
            h_ = ps(name, [128, n], dt)
            return h_, T(h_[:, :], excl=True)
        pdense = [bank("pd%d" % i)[1] for i in range(3)]
        pdense.append(bank("pstat")[1])
        pdense += [bank("pq%d" % i)[1] for i in range(2)]
        qc_h, QC = bank("qc")
        qd_h, QD = bank("qd")

        wts_t = [T(wts_d[g], const=True) for g in range(NGRP)]
        wbf_t = [T(wbf_d[g]) for g in range(NGRP)]
        xT_t = T(xT_d, const=True)

        P = Prog()

        spec = []
        bank_free_after = [-1] * 8

        def flush(recs):
            ops = [o for chunk in merge_chunks(recs) for o in chunk]
            base = len(spec)
            last_use = {}
            for i, o in enumerate(ops):
                for v in o[2] + o[3]:
                    if isinstance(v.t, LatePD):
                        last_use[id(v.t)] = base + i
            for i, (eng, fn, ins, outs, dma, cost, prio) in enumerate(ops):
                idx = base + i
                rt = []
                for grp in (ins, outs):
                    lst = []
                    for v in grp:
                        t = v.t
                        if isinstance(t, LatePD):
                            if t.bound is None:
                                cands = [k for k in range(len(pdense)) if bank_free_after[k] < idx]
                                assert cands, "too many live short-lived PSUM tiles"
                                k = min(cands, key=lambda k_: bank_free_after[k_])
                                t.bound = pdense[k]
                                bank_free_after[k] = last_use[id(t)]
                            t = t.bound.root
                        lst.append(t)
                    rt.append(lst)
                spec.append((eng, fn, rt[0], rt[1], dma, cost, prio))

        o_nw1, o_nw2, o_fnw, o_cw, o_lcw, o_lcb, o_lba, o_lbx, o_lap, o_alog, o_dtb, o_gnw = \
            0, 8, 16, 24, 72, 88, 92, 96, 100, 104, 108, 112

        R = Rec()
        R.dma("sp", cm.v, T(cm_d, const=True).v)
        R.dma("sp", vecs.v, T(vecs_d, const=True).v)
        R.dma("pool", cmb.v, V(T(cm_d, const=True), cm_d[:, 2:6, :]))
        R.dma("pool", wgate.v, T(wgate_d, const=True).v)
        R.dma("pool", wba.v, T(wba_d, const=True).v)
        for g in range(6):
            R.dma("pool", wbf_t[g].v, wts_t[g].v)
        for h in range(NH):
            R.copy("pool", c4[:, 0, h, :], SMb)
            R.copy("pool", c4[:, 1, h, :], IDb)
            R.copy("pool", c4[:, 2, h, :], Cb)
        tmpv = T(sb("tmpv", [128, 8, 16], F32)[:, :, :])

        def ln1p_small(R, out, e, w, k0):
            z = tmpv[:, k0, 0:w]
            z2 = tmpv[:, k0 + 1, 0:w]
            pl = tmpv[:, k0 + 2, 0:w]
            R.ts("dve", z, e, 2.0, ALU.add)
            R.recip(z, z)
            R.tt("dve", z, z, e, ALU.mult)
            R.tt("dve", z2, z, z, ALU.mult)
            R.ts("dve", pl, z2, 1.0 / 9, ALU.mult, 1.0 / 7, ALU.add)
            R.tt("dve", pl, pl, z2, ALU.mult)
            R.ts("dve", pl, pl, 1.0 / 5, ALU.add)
            R.tt("dve", pl, pl, z2, ALU.mult)
            R.ts("dve", pl, pl, 1.0 / 3, ALU.add)
            R.tt("dve", pl, pl, z2, ALU.mult)
            R.ts("dve", pl, pl, 1.0, ALU.add)
            R.tt("dve", pl, pl, z, ALU.mult)
            R.ts("dve", out, pl, 2.0, ALU.mult)

        def softplus(R, out, x, w, k0):
            ab = tmpv[:, k0 + 3, 0:w]
            l1 = tmpv[:, k0 + 4, 0:w]
            R.ts("dve", ab, x, -1.0, ALU.mult)
            R.tt("dve", ab, ab, x, ALU.min)
            R.act(ab, ab, AF.Exp)
            ln1p_small(R, l1, ab, w, k0)
            R.stt(out, x, 0.0, l1, ALU.max, ALU.add)

        ngl = tmpv[:, 7, 0:4]
        R.ts("dve", ngl, vecs[:, o_lap:o_lap + 4], -1.0, ALU.mult)
        softplus(R, ngl, ngl, 4, 0)
        R.ts("dve", dvec[:, 0:4], ngl, -8.0, ALU.mult)
        R.act(dvec[:, 4:8], vecs[:, o_alog:o_alog + 4], AF.Exp)
        R.ts("dve", dvec[:, 4:8], dvec[:, 4:8], -1.0, ALU.mult)
        R.ts("dve", dvec[:, 8:12], vecs[:, o_lba:o_lba + 4], -1.0, ALU.mult)
        R.ts("dve", dvec[:, 12:16], vecs[:, o_lbx:o_lbx + 4], -1.0, ALU.mult)
        R.ts("dve", dvec[:, 16:20], dvec[:, 0:4], 2.0, ALU.mult)
        flush([R])

        ring_state = {"A": [0, 0], "B": [0, 0]}
        ntiles = nseq * NT
        seqA = [g for _ in range(ntiles) for g in range(6)]
        seqB = [g for _ in range(ntiles) for g in (6, 7, 8, 9, 10, 11, 12, 13, 14, 15, 16, 17, 18, 19, 20, 21, 22, 23)]

        def ring_next(R, which):
            ring = ringA if which == "A" else ringB
            seq = seqA if which == "A" else seqB
            st = ring_state[which]
            while st[0] < len(seq) and st[0] < st[1] + len(ring):
                R.dma("sp", ring[st[0] % len(ring)].v, wbf_t[seq[st[0]]].v)
                st[0] += 1
            slot = ring[st[1] % len(ring)]
            st[1] += 1
            return slot

        def next_pdB():
            return LatePD()

        def sig_of(R, out, x, nscale):
            R.act(out, x, AF.Exp, scale=nscale)
            R.act(out, out, AF.Ln, bias=1.0)
            R.act(out, out, AF.Exp, scale=-1.0)

        def rmsnorm_to(R, xc, hdst, wcol, rsbuf):
            for c in range(NCH):
                R.act(hdst[c].v, xc(c), AF.Square)
            pst = LatePD()
            for c in range(NCH):
                R.mm(pst.v, ONESb, hdst[c].v, start=(c == 0), stop=(c == NCH - 1))
            R.act(rsbuf.v, pst.v, AF.Ln, scale=1.0 / D, bias=EPS)
            R.act(rsbuf.v, rsbuf.v, AF.Exp, scale=-0.5)
            R.cut()
            for c in range(NCH):
                R.stt(hdst[c].v, xc(c), vecs[:, wcol + c:wcol + c + 1], rsbuf.v, ALU.mult, ALU.mult)
            R.cut()

        def gen_A(ti):
            s_i, j_i = divmod(ti, NT)
            tok0 = s_i * S + j_i * TT
            first = (j_i == 0)
            R = Rec()
            R.dma("sp", xA.v, V(xT_t, xT_d[:, :, tok0:tok0 + TT].rearrange("c p t -> p c t")))
            if first:
                for c in range(16):
                    R.memset("pool", halo[c].v, 0.0)
                for c in range(4):
                    R.memset("pool", hst[c].v, 0.0)
                for h in range(NH):
                    R.memset("pool", S32[h].v, 0.0)
                    R.memset("pool", Sbf[h].v, 0.0)
            R.cut()
            rmsnorm_to(R, lambda c: xA[:, c, :], hA, o_nw1, rs[0])
            if ti == 0:
                for g in range(6, NGRP):
                    R.dma("pool", wbf_t[g].v, wts_t[g].v, extra_ins=[rs[0].v])

            pg = LatePD()
            for jj in range(4):
                for c in range(NCH):
                    R.mm(pg[:, 8 * jj:8 * jj + 8], hA[c][:, 128 * jj:128 * (jj + 1)], wba[:, c, :],
                         start=(c == 0), stop=(c == NCH - 1))
            R.cut()
            beta, nbeta, gtm, tA, egc, erem, egl0, egl1 = [gtT[i] for i in range(8)]
            dtb3 = V(vecs, vecs.ap[:, o_dtb:o_dtb + 4])
            for jj in range(4):
                R.act(beta[:, 4 * jj:4 * jj + 4], pg[:, 8 * jj:8 * jj + 4], AF.Exp, scale=-1.0)
                R.tt("dve", tA[:, 4 * jj:4 * jj + 4], pg[:, 8 * jj + 4:8 * jj + 8], dtb3, ALU.add)
            R.act(beta.v, beta.v, AF.Ln, bias=1.0)
            R.act(beta.v, beta.v, AF.Exp, scale=-1.0)
            R.ts("dve", nbeta.v, beta.v, -1.0, ALU.mult)
            softplus(R, tA.v, tA.v, 16, 0)
            for jj in range(4):
                R.tt("dve", gtm[:, 4 * jj:4 * jj + 4], tA[:, 4 * jj:4 * jj + 4], dvec[:, 4:8], ALU.mult)
            pg2 = LatePD()
            R.mm(pg2[:, 0:16], L2, gtm.v)
            R.mm(pg2[:, 16:32], L1, gtm.v)
            R.mm(pg2[:, 32:48], CM0, gtm.v)
            R.act(egc.v, pg2[:, 0:16], AF.Exp)
            R.act(erem.v, pg2[:, 16:32], AF.Exp)
            R.act(egl0.v, pg2[:, 32:48], AF.Exp)
            R.cut()

            rot = [0]

            def conv4(R, ch, wcol, bias, out_v, pdt):
                rw = raw[rot[0] % 2]
                rot[0] += 1
                R.copy("pool", rw[:, 0:3], halo[ch][:, 0:3])
                R.copy("act", rw[:, 3:3 + TT], pdt.v)
                R.copy("pool", halo[ch][:, 0:3], rw[:, TT:TT + 3])
                if bias is None:
                    R.ts("dve", out_v, rw[:, 0:TT], vecs[:, wcol:wcol + 1], ALU.mult)
                else:
                    R.ts("dve", out_v, rw[:, 0:TT], vecs[:, wcol:wcol + 1], ALU.mult, bias, ALU.add)
                for k in range(1, 4):
                    R.stt(out_v, rw[:, k:k + TT], vecs[:, wcol + k:wcol + k + 1], out_v, ALU.mult, ALU.add)

            R2 = Rec()
            for g in range(6):
                Rc = R if g < 3 else R2
                wslot = ring_next(Rc, "A")
                for n in range(4):
                    ch = 4 * g + n
                    pdt = LatePD()
                    for c in range(NCH):
                        Rc.mm(pdt.v, wslot[:, 512 * c + 128 * n:512 * c + 128 * (n + 1)], hA[c].v,
                             start=(c == 0), stop=(c == NCH - 1))
                    Rc.cut()
                    if ch < 12:
                        a_ = acc[ch % 2]
                        conv4(Rc, ch, o_cw + 4 * ch, None, a_.v, pdt)
                        sg_ = rs[ch % 2]
                        sig_of(Rc, sg_.v, a_.v, -1.0)
                        if ch >= 8:
                            Rc.tt("pool", vT[ch - 8].v, a_.v, sg_.v, ALU.mult)
                        else:
                            kq = 1 if ch < 4 else 0
                            h = ch % 4
                            Rc.tt("pool", a_.v, a_.v, sg_.v, ALU.mult)
                            sq_ = sqb[ch % 2]
                            Rc.act(sq_.v, a_.v, AF.Square)
                            pst = LatePD()
                            Rc.mm(pst.v, ONESb, sq_.v)
                            r_ = rs[ch % 2]
                            if kq == 1:
                                Rc.act(r_.v, pst.v, AF.Ln, scale=128.0, bias=128.0 * EPS)
                            else:
                                Rc.act(r_.v, pst.v, AF.Ln, scale=1.0, bias=EPS)
                            Rc.act(r_.v, r_.v, AF.Exp, scale=-0.5)
                            Rc.tt("dve", qkn[kq][:, h, :, :],
                                 V(a_, a_.ap.rearrange("p (j t) -> p j t", j=4)),
                                 V(r_, r_.ap.rearrange("p (j t) -> p j t", j=4)), ALU.mult)
                    elif ch < 16:
                        sg_ = rs[ch % 2]
                        a_ = acc[ch % 2]
                        Rc.copy("act", a_.v, pdt.v)
                        sig_of(Rc, sg_.v, a_.v, -1.0)
                        Rc.tt("pool", sz[ch - 12].v, a_.v, sg_.v, ALU.mult)
                    elif ch < 20:
                        sg_ = rs[ch % 2]
                        a_ = acc[ch % 2]
                        Rc.copy("act", a_.v, pdt.v)
                        Rc.act(sg_.v, a_.v, AF.Square)
                        Rc.ts("pool", sg_.v, sg_.v, 0.044715, ALU.mult, 1.0, ALU.add)
                        Rc.tt("dve", sg_.v, a_.v, sg_.v, ALU.mult)
                        sig_of(Rc, sg_.v, sg_.v, -1.5957691216057308)
                        Rc.tt("dve", gg[ch - 16].v, a_.v, sg_.v, ALU.mult)
                    else:
                        lc = ch - 20
                        xr, ra, a2, ig = lt
                        conv4(Rc, 12 + lc, o_lcw + 4 * lc, vecs[:, o_lcb + lc:o_lcb + lc + 1], xr.v, pdt)
                        Rc.copy("act", xrb.v, xr.v)
                        pr = LatePD()
                        Rc.mm(pr.v, wgate[:, 0, lc, :], xrb.v)
                        pi = LatePD()
                        Rc.mm(pi.v, wgate[:, 1, lc, :], xrb.v)
                        Rc.act(ra.v, pr.v, AF.Exp, scale=-1.0, bias=dvec[:, 8 + lc:9 + lc])
                        Rc.act(ra.v, ra.v, AF.Ln, bias=1.0)
                        Rc.act(ra.v, ra.v, AF.Exp, scale=-1.0)
                        Rc.act(a2.v, ra.v, AF.Exp, scale=dvec[:, 16 + lc:17 + lc])
                        Rc.act(ra.v, ra.v, AF.Exp, scale=dvec[:, lc:lc + 1])
                        Rc.act(a2.v, a2.v, AF.Ln, scale=-1.0, bias=1.0)
                        Rc.act(a2.v, a2.v, AF.Exp, scale=0.5)
                        Rc.act(ig.v, pi.v, AF.Exp, scale=-1.0, bias=dvec[:, 12 + lc:13 + lc])
                        Rc.act(ig.v, ig.v, AF.Ln, bias=1.0)
                        Rc.act(ig.v, ig.v, AF.Exp, scale=-1.0)
                        Rc.tt("pool", ig.v, ig.v, xr.v, ALU.mult)
                        Rc.tt("pool", ig.v, ig.v, a2.v, ALU.mult)
                        Rc.scan(xr.v, ra.v, ig.v, hst[lc][:, 0:1])
                        Rc.copy("pool", hst[lc][:, 0:1], xr[:, TT - 1:TT])
                        Rc.tt("pool", mix[4 + lc].v, xr.v, gg[lc].v, ALU.mult)
                    Rc.cut()

            def hv(lst):
                return [t.v for t in lst]

            def pre_jj(Rp, jj):
                st_ = jj % NSET
                u, ua = UT, UTA
                p, pa = UP[st_], UPA[st_]
                cols = [4 * jj + h for h in range(NH)]
                kT4 = qkn[0][:, :, jj, :]
                qT4 = qkn[1][:, :, jj, :]
                for h in range(NH):
                    Rp.ts("dve", UG[h].v, L2, gtm[:, cols[h]:cols[h] + 1], ALU.mult)
                Rp.cut()
                q = LatePD()
                for h in range(NH):
                    Rp.mm(q[:, 128 * h:128 * (h + 1)], L1, UG[h].v)
                Rp.op("act", lambda e, q=q: e.activation(out=ua["decT"], in_=q.v.ap, func=AF.Exp),
                      hv(u["decT"]), [q.v], cost=0.56)
                Rp.op("pool", lambda e: e.tensor_tensor(out=ua["decT"], in0=ua["decT"], in1=c4_h[:, 2, :, :], op=ALU.mult),
                      hv(u["decT"]), hv(u["decT"]) + [c4.v], cost=1.26)
                Rp.op("pool", lambda e: e.tensor_tensor(out=ua["decS"], in0=ua["decT"], in1=c4_h[:, 0, :, :], op=ALU.mult),
                      hv(u["decS"]), hv(u["decT"]) + [c4.v], cost=1.26)
                Rp.cut()
                q = LatePD()
                for h in range(NH):
                    Rp.mm(q[:, 128 * h:128 * (h + 1)], kT4.t[:, h, jj, :], kT4.t[:, h, jj, :])
                for h in range(NH):
                    Rp.stt(u["N0"][h].v, q[:, 128 * h:128 * (h + 1)], nbeta[:, cols[h]:cols[h] + 1],
                           u["decS"][h].v, ALU.mult, ALU.mult)
                Rp.cut()
                q = LatePD()
                for h in range(NH):
                    Rp.mm(q[:, 128 * h:128 * (h + 1)], kT4.t[:, h, jj, :], qT4.t[:, h, jj, :])
                Rp.op("dve", lambda e, q=q: e.tensor_tensor(out=pa["attnT"], in0=q.v.ap, in1=ua["decT"], op=ALU.mult),
                      hv(p["attnT"]), [q.v] + hv(u["decT"]), cost=0.62)
                Rp.cut()
                q = LatePD()
                for h in range(NH):
                    Rp.mm(q[:, 128 * h:128 * (h + 1)], ONES, UG[h].v)
                Rp.op("act", lambda e, q=q: e.activation(out=ua["egcB"], in_=q.v.ap, func=AF.Exp),
                      hv(u["egcB"]), [q.v], cost=0.56)
                Rp.op("dve", lambda e: e.tensor_tensor(out=pa["qg"], in0=qT4.ap, in1=ua["egcB"], op=ALU.mult),
                      hv(p["qg"]), [qT4] + hv(u["egcB"]), cost=0.62)
                Rp.cut()
                q = LatePD()
                for h in range(NH):
                    Rp.mm(q[:, 128 * h:128 * (h + 1)], kT4.t[:, h, jj, :], IDb)
                for h in range(NH):
                    Rp.act(u["kg"][h].v, q[:, 128 * h:128 * (h + 1)], AF.Copy, scale=egc[:, cols[h]:cols[h] + 1])
                    Rp.act(p["kdec"][h].v, q[:, 128 * h:128 * (h + 1)], AF.Copy, scale=erem[:, cols[h]:cols[h] + 1])
                Rp.cut()
                q = LatePD()
                for h in range(NH):
                    Rp.mm(q[:, 128 * h:128 * (h + 1)], vT[h][:, 128 * jj:128 * (jj + 1)], IDb)
                Rp.op("act", lambda e, q=q: e.activation(out=pa["vtok"], in_=q.v.ap, func=AF.Copy),
                      hv(p["vtok"]), [q.v], cost=0.56)
                Rp.cut()
                q = LatePD()
                for h in range(NH):
                    Rp.mm(q[:, 128 * h:128 * (h + 1)], u["N0"][h].v, IDb)
                Rp.op("act", lambda e, q=q: e.activation(out=ua["Y0"], in_=q.v.ap, func=AF.Copy),
                      hv(u["Y0"]), [q.v], cost=0.56)
                Rp.op("pool", lambda e: e.tensor_tensor(out=pa["Pfin"], in0=ua["N0"], in1=c4_h[:, 1, :, :], op=ALU.add),
                      hv(p["Pfin"]), hv(u["N0"]) + [c4.v], cost=1.26)
                Rp.cut()
                Xs = ["N0", "Xa", "N0", "Xa", "N0", "Xa"]
                Ys = ["Y0", "Ya", "Y0", "Ya", "Y0", "Ya", "Y0"]
                Ps = [("p", "Pfin"), ("u", "Ptmp"), ("p", "Pfin"), ("u", "Ptmp"), ("p", "Pfin"), ("u", "Ptmp"),
                      ("p", "Pfin")]
                NLEV = 6

                def PT(k):
                    w, nm = Ps[k]
                    return (u[nm], ua[nm]) if w == "u" else (p[nm], pa[nm])
                for r in range(0, NLEV + 1):
                    Yc = Ys[r]
                    qy = qx = qp = None
                    if r < NLEV:
                        Xc = Xs[r]
                        qy = LatePD()
                        for h in range(NH):
                            Rp.mm(qy[:, 128 * h:128 * (h + 1)], u[Xc][h].v, u[Yc][h].v)
                    if r < NLEV - 1:
                        qx = LatePD()
                        for h in range(NH):
                            Rp.mm(qx[:, 128 * h:128 * (h + 1)], u[Yc][h].v, u[Xc][h].v)
                    if r >= 1:
                        Pc_t, Pc_a = PT(r - 1)
                        Pn_t, Pn_a = PT(r)
                        qp = LatePD()
                        for h in range(NH):
                            Rp.mm(qp[:, 128 * h:128 * (h + 1)], u[Yc][h].v, Pc_t[h].v)
                    if qy is not None:
                        Yn = Ys[r + 1]
                        Rp.op("dve", lambda e, qy=qy, Yn=Yn: e.tensor_copy(out=ua[Yn], in_=qy.v.ap),
                              hv(u[Yn]), [qy.v], cost=0.62)
                    if qx is not None:
                        Xn = Xs[r + 1]
                        Rp.op("act", lambda e, qx=qx, Xn=Xn: e.activation(out=ua[Xn], in_=qx.v.ap, func=AF.Copy),
                              hv(u[Xn]), [qx.v], cost=0.56)
                    if qp is not None:
                        Rp.op("dve", lambda e, qp=qp, Pn_a=Pn_a, Pc_a=Pc_a: e.tensor_tensor(out=Pn_a, in0=qp.v.ap, in1=Pc_a, op=ALU.add),
                              hv(Pn_t), [qp.v] + hv(Pc_t), cost=0.62)
                    Rp.cut()
                q = LatePD()
                for h in range(NH):
                    Rp.mm(q[:, 128 * h:128 * (h + 1)], u["kg"][h].v, p["Pfin"][h].v)
                Rp.op("act", lambda e, q=q: e.activation(out=pa["nw0"], in_=q.v.ap, func=AF.Copy, scale=-1.0),
                      hv(p["nw0"]), [q.v], cost=0.56)
                Rp.cut()

            def rec_jj(Rr, jj):
                st_ = jj % NSET
                p, pa = UP[st_], UPA[st_]
                cols = [4 * jj + h for h in range(NH)]
                for c in range(1):
                    lo, hi = 0, 128
                    for h in range(NH):
                        Rr.mm(QC[lo:hi, 128 * h:128 * (h + 1)], p["Pfin"][h][lo:hi, lo:hi], p["vtok"][h][lo:hi, :],
                              start=True, stop=False)
                        Rr.mm(QC[lo:hi, 128 * h:128 * (h + 1)], p["nw0"][h][:, lo:hi], Sbf[h].v, start=False, stop=True)
                    for h in range(NH):
                        Rr.ts("dve", VNEW[h][lo:hi, :], QC[lo:hi, 128 * h:128 * (h + 1)],
                              beta[lo:hi, cols[h]:cols[h] + 1], ALU.mult)
                    Rr.cut()
                    for h in range(NH):
                        Rr.mm(QD[:, 128 * h + lo:128 * h + hi], Sbf[h].v, p["qg"][h][:, lo:hi], start=True, stop=False)
                        Rr.mm(QD[:, 128 * h + lo:128 * h + hi], VNEW[h][lo:hi, :], p["attnT"][h][lo:hi, lo:hi],
                              start=False, stop=True)
                    for h in range(NH):
                        Rr.mm(QC[:, 128 * h:128 * (h + 1)], p["kdec"][h][lo:hi, :], VNEW[h][lo:hi, :])
                    egl = egl0 if c == 0 else egl1
                    for h in range(NH):
                        Rr.stt(S32[h].v, S32[h].v, egl[:, cols[h]:cols[h] + 1], QC[:, 128 * h:128 * (h + 1)],
                               ALU.mult, ALU.add)
                    Rr.op("act", lambda e: e.activation(out=SbfA, in_=S32A, func=AF.Copy), hv(Sbf), hv(S32), cost=0.56)
                    Rr.cut()
                Rr.op("act", lambda e, jj=jj: e.activation(out=osb_h[:, :, 128 * jj:128 * (jj + 1)],
                                                           in_=qd_h[:, :].rearrange("p (h t) -> p h t", h=NH), func=AF.Copy),
                      hv(osb), [QD.v], cost=0.56)
                Rr.cut()

            Rg = Rec()
            pre_jj(Rg, 0)
            gd = Rec()
            gd.chunks = merge_chunks([Rg])
            for jj in range(4):
                Rr = Rec()
                rec_jj(Rr, jj)
                Rp = Rec()
                if jj + 1 < 4:
                    pre_jj(Rp, jj + 1)
                gd.chunks.extend(merge_chunks([Rr, Rp]))
            flush_list = [[R], [R2, gd]]
            Rn = Rec()
            for h in range(NH):
                sq_ = sqb[h % 2]
                Rn.act(sq_.v, osb[h].v, AF.Square)
                pst = LatePD()
                Rn.mm(pst.v, ONESb, sq_.v)
                r_ = rs[h % 2]
                Rn.act(r_.v, pst.v, AF.Ln, scale=1.0 / 128, bias=EPS)
                Rn.act(r_.v, r_.v, AF.Exp, scale=-0.5)
                ot = acc[h % 2]
                Rn.stt(ot.v, osb[h].v, vecs[:, o_gnw:o_gnw + 1], r_.v, ALU.mult, ALU.mult)
                Rn.tt("dve", mix[h].v, ot.v, sz[h].v, ALU.mult)
                Rn.cut()
            flush_list.append([Rn])
            return flush_list

        def gen_B(ti):
            s_i, j_i = divmod(ti, NT)
            tok0 = s_i * S + j_i * TT
            R = Rec(prio=1)
            for c in range(NCH):
                R.dma("sp", xBc[c].v, V(xT_t, xT_d[c, :, tok0:tok0 + TT]))
            R.cut()
            for g in range(2):
                wslot = ring_next(R, "B")
                for n in range(4):
                    dc = 4 * g + n
                    pdt = next_pdB()
                    for c in range(NCH):
                        R.mm(pdt.v, wslot[:, 512 * c + 128 * n:512 * c + 128 * (n + 1)], mix[c].v,
                             start=(c == 0), stop=(c == NCH - 1))
                    R.tt("dve", xBc[dc].v, xBc[dc].v, pdt.v, ALU.add)
                    R.cut()
            rmsnorm_to(R, lambda c: xBc[c].v, hB, o_nw2, rsB)
            for half in range(2):
                for g in range(4):
                    wslot = ring_next(R, "B")
                    for n in range(4):
                        fc = 4 * g + n
                        pdt = next_pdB()
                        for c in range(NCH):
                            R.mm(pdt.v, wslot[:, 512 * c + 128 * n:512 * c + 128 * (n + 1)], hB[c].v,
                                 start=(c == 0), stop=(c == NCH - 1))
                        rl = reluB[fc % 2]
                        R.act(rl.v, pdt.v, AF.Relu)
                        R.tt("pool", uT[fc].v, rl.v, rl.v, ALU.mult)
                        R.cut()
                for g in range(4):
                    wslot = ring_next(R, "B")
                    for n in range(2):
                        dc = 2 * g + n
                        pdt = next_pdB()
                        for fc in range(16):
                            R.mm(pdt.v, wslot[:, 256 * fc + 128 * n:256 * fc + 128 * (n + 1)], uT[fc].v,
                                 start=(fc == 0), stop=(fc == 15))
                        R.tt("dve", xBc[dc].v, xBc[dc].v, pdt.v, ALU.add)
                        R.cut()
            for c in range(NCH):
                R.act(hB[c].v, xBc[c].v, AF.Square)
            pst = LatePD()
            for c in range(NCH):
                R.mm(pst.v, ONESb, hB[c].v, start=(c == 0), stop=(c == NCH - 1))
            R.act(rsB.v, pst.v, AF.Ln, scale=1.0 / D, bias=EPS)
            R.act(rsB.v, rsB.v, AF.Exp, scale=-0.5)
            for c in range(NCH):
                R.stt(xBc[c].v, xBc[c].v, vecs[:, o_fnw + c:o_fnw + c + 1], rsB.v, ALU.mult, ALU.mult)
                R.dma("sp", T(yT_d[c, :, tok0:tok0 + TT]).v, xBc[c].v)
            R.cut()
            return [[R]]

        for grp in gen_A(0):
            flush(grp)
        for ti in range(ntiles):
            Bl = gen_B(ti)
            if ti + 1 < ntiles:
                Al = gen_A(ti + 1)
                Aflat = Rec()
                Aflat.chunks = []
                for grp in Al:
                    Aflat.chunks.extend(merge_chunks(grp))
                flush([Bl[0][0], Aflat])
            else:
                flush(Bl[0])
        import os
        if os.environ.get("KDEBUG"):
            print("SBUF bytes/partition:", sb_bytes[0], "ops:", len(spec))
        order = list_schedule(spec)
        for i in order:
            eng, fn, reads, writes, dma, cost, prio = spec[i]
            P.add(eng, fn, reads, writes, dma)
        P.emit(nc)
    return nc


def _masks():
    m = np.arange(128)
    same = np.ones((128, 128), bool)
    cm = np.zeros((128, 8, 128), np.float32)
    cm[:, 0, :] = (m[:, None] > m[None, :]) & same
    cm[:, 1, :] = (m[:, None] <= m[None, :]) & same
    cm[:, 2, :] = (m[:, None] <= m[None, :]) & same
    cm[:, 3, :] = (m[:, None] < m[None, :]) & same
    cm[:, 4, :] = np.eye(128)
    cm[:, 5, :] = 1.0
    cm[:, 6, :] = 1.0
    cm[:, 7, :] = 0.0
    return cm


def prep_shared(inp):
    f = np.float32
    w_in = np.asarray(inp["w_in"], f)[0]
    wmain = np.concatenate([w_in[:, 0:1536], w_in[:, 1536:2048], w_in[:, 2568:3080], w_in[:, 2056:2568]], axis=1)
    wba_ = w_in[:, 2048:2056]
    w_out = np.asarray(inp["w_out"], f)[0]
    w1 = np.asarray(inp["w_ff1"], f)[0]
    w2 = np.asarray(inp["w_ff2"], f)[0]
    wts = np.zeros((NGRP, 128, 4096), f)

    def colgrp(W, g):
        return W[:, 512 * g:512 * (g + 1)].reshape(8, 128, 512).transpose(1, 0, 2).reshape(128, 4096)
    for g in range(6):
        wts[g] = colgrp(wmain, g)
    for g in range(2):
        wts[6 + g] = colgrp(w_out, g)
    for half in range(2):
        base = 8 + 8 * half
        for g in range(4):
            wts[base + g] = colgrp(w1, 4 * half + g)
        for g in range(4):
            blk = w2[2048 * half:2048 * (half + 1), 256 * g:256 * (g + 1)]
            wts[base + 4 + g] = blk.reshape(16, 128, 2, 128).transpose(1, 0, 2, 3).reshape(128, 4096)
    vecs = np.zeros((128, NVEC), f)

    def pc(v, n):
        return np.asarray(v, f).reshape(n, 128).T
    vecs[:, 0:8] = pc(inp["norm_mix_w"][0], 8)
    vecs[:, 8:16] = pc(inp["norm_mlp_w"][0], 8)
    vecs[:, 16:24] = pc(inp["final_norm_w"], 8)
    cw = np.asarray(inp["gdn_conv_w"], f)[0]
    vecs[:, 24:72] = cw.reshape(4, 12, 128).transpose(2, 1, 0).reshape(128, 48)
    lcw = np.asarray(inp["lru_conv_w"], f)[0]
    vecs[:, 72:88] = lcw.reshape(4, 4, 128).transpose(2, 1, 0).reshape(128, 16)
    vecs[:, 88:92] = pc(inp["lru_conv_b"][0], 4)
    vecs[:, 92:96] = pc(np.asarray(inp["lru_gate_a_b"], f)[0].reshape(512), 4)
    vecs[:, 96:100] = pc(np.asarray(inp["lru_gate_x_b"], f)[0].reshape(512), 4)
    vecs[:, 100:104] = pc(inp["lru_a_param"][0], 4)
    vecs[:, 104:108] = np.broadcast_to(np.asarray(inp["gdn_A_log"], f)[0][None, :], (128, 4))
    vecs[:, 108:112] = np.broadcast_to(np.asarray(inp["gdn_dt_bias"], f)[0][None, :], (128, 4))
    vecs[:, 112] = np.asarray(inp["gdn_norm_w"], f)[0]
    wgate = np.zeros((128, 2, 4, 128), f)
    for gi, key in enumerate(("lru_gate_a_w", "lru_gate_x_w")):
        wg = np.asarray(inp[key], f)[0]
        for lc in range(4):
            for b in range(2):
                wgate[64 * b:64 * (b + 1), gi, lc, 64 * b:64 * (b + 1)] = wg[2 * lc + b]
    wba = wba_.reshape(8, 128, 8).transpose(1, 0, 2).copy()
    return {"wts": wts, "cm": _masks(), "vecs": vecs, "wgate": wgate, "wba": np.ascontiguousarray(wba)}


def prep_x(xs):
    nseq, S, _ = xs.shape
    return np.ascontiguousarray(xs.reshape(nseq * S, NCH, 128).transpose(1, 2, 0))


def unprep_y(yT, nseq, S):
    return np.ascontiguousarray(yT.transpose(2, 0, 1).reshape(nseq, S, D))


_NC_CACHE = {}


def kernel(**inputs):
    x = np.asarray(inputs["x"], np.float32)
    B, S, _ = x.shape
    ncores = 8
    nseq = B // ncores
    shared = prep_shared(inputs)
    key = (nseq, S)
    if key not in _NC_CACHE:
        _NC_CACHE[key] = build_program(nseq, S)
    nc = _NC_CACHE[key]
    in_maps = []
    for c in range(ncores):
        m = dict(shared)
        m["xT"] = prep_x(x[c * nseq:(c + 1) * nseq])
        in_maps.append(m)
    res = run_bass_kernel_spmd(nc, in_maps, core_ids=list(range(ncores)))
    outs = [unprep_y(np.asarray(r["yT"]), nseq, S) for r in res.results]
    return np.concatenate(outs, axis=0).astype(np.float32)
```

```python
import numpy as np
import concourse.bass as bass
import concourse.mybir as mybir
from concourse.bass_utils import run_bass_kernel_spmd
from contextlib import ExitStack

F32 = mybir.dt.float32
BF16 = mybir.dt.bfloat16
AF = mybir.ActivationFunctionType
ALU = mybir.AluOpType

D = 1024
NCH = 8
TT = 512
NH = 4
EPS = 1e-6
N_DMA_SEMS = 24
N_SP_SEMS = 16
NGRP = 24
NVEC = 120


class V:
    __slots__ = ("t", "ap")

    def __init__(self, t, ap):
        self.t = t
        self.ap = ap


class LatePD:
    __slots__ = ("bound",)

    def __init__(self):
        self.bound = None

    @property
    def v(self):
        return LV(self, None, TT)

    def __getitem__(self, idx):
        n = TT
        if isinstance(idx, tuple) and isinstance(idx[-1], slice) and idx[-1].start is not None:
            n = idx[-1].stop - idx[-1].start
        return LV(self, idx, n)


class LV:
    __slots__ = ("t", "idx", "n")

    def __init__(self, t, idx, n):
        self.t = t
        self.idx = idx
        self.n = n

    @property
    def ap(self):
        b = self.t.bound
        return b.ap if self.idx is None else b.ap[self.idx]


class T:
    __slots__ = ("ap", "name", "last_w", "readers", "const", "root", "excl")

    def __init__(self, ap, name="", const=False, parent=None, excl=False):
        self.ap = ap
        self.name = name
        self.last_w = None
        self.readers = []
        self.const = const
        self.excl = excl
        self.root = parent.root if parent is not None else self

    def __getitem__(self, idx):
        return V(self.root, self.ap[idx])

    @property
    def v(self):
        return V(self.root, self.ap)


class Op:
    __slots__ = ("eng", "fn", "deps", "signal", "count", "is_dma", "dma_slot", "dma_val", "prev_dma")

    def __init__(self, eng, fn, is_dma):
        self.eng = eng
        self.fn = fn
        self.deps = []
        self.signal = False
        self.count = 0
        self.is_dma = is_dma
        self.dma_slot = -1
        self.dma_val = 0
        self.prev_dma = None


class Prog:
    ENGS = ("pe", "act", "dve", "pool", "sp")

    def __init__(self):
        self.ops = []
        self.n_dma_sp = 0
        self.n_dma_pl = 0
        self.dma_last = [None] * N_DMA_SEMS

    def add(self, eng, fn, reads=(), writes=(), dma=False):
        op = Op(eng, fn, dma)
        deps = {}
        for t in reads:
            if t.last_w is not None:
                deps[id(t.last_w)] = t.last_w
            if t.excl:
                for r in t.readers:
                    if r.eng != eng:
                        deps[id(r)] = r
        for t in writes:
            if t.last_w is not None:
                deps[id(t.last_w)] = t.last_w
            for r in t.readers:
                deps[id(r)] = r
        for t in reads:
            if not t.const:
                t.readers.append(op)
        for t in writes:
            t.last_w = op
            t.readers = []
        for d in deps.values():
            if d is op:
                continue
            if d.is_dma:
                op.deps.append(d)
            elif d.eng == eng and not dma:
                if eng != "pe":
                    op.deps.append(d)
                    d.signal = True
            else:
                op.deps.append(d)
                d.signal = True
        if dma:
            if eng == "sp":
                k = self.n_dma_sp
                self.n_dma_sp += 1
                slot = k % N_SP_SEMS
                val = 16 * (k // N_SP_SEMS + 1)
            else:
                k = self.n_dma_pl
                self.n_dma_pl += 1
                slot = N_SP_SEMS + k % (N_DMA_SEMS - N_SP_SEMS)
                val = 16 * (k // (N_DMA_SEMS - N_SP_SEMS) + 1)
            op.dma_slot = slot
            op.dma_val = val
            op.prev_dma = self.dma_last[slot]
            self.dma_last[slot] = op
        self.ops.append(op)
        return op

    def emit(self, nc):
        counts = {e: 0 for e in self.ENGS}
        for op in self.ops:
            if not op.is_dma and op.signal:
                counts[op.eng] += 1
                op.count = counts[op.eng]
        with ExitStack() as es:
            sems = {e: es.enter_context(nc.semaphore("s_" + e)) for e in self.ENGS}
            dsems = [es.enter_context(nc.semaphore("d%d" % i)) for i in range(N_DMA_SEMS)]
            block = es.enter_context(nc.Block())
            names = {"pe": "tensor", "act": "scalar", "dve": "vector", "pool": "gpsimd", "sp": "sync"}
            dma_last = self.dma_last
            for e in self.ENGS:
                myops = [op for op in self.ops if op.eng == e]

                def body(eng, myops=myops, e=e):
                    waited = {}

                    def wait(sem, val, key):
                        if waited.get(key, 0) >= val:
                            return
                        waited[key] = val
                        eng.wait_ge(sem, val)

                    for op in myops:
                        for d in op.deps:
                            if d.is_dma:
                                wait(dsems[d.dma_slot], d.dma_val, ("d", d.dma_slot))
                            else:
                                wait(sems[d.eng], d.count, d.eng)
                        if op.is_dma:
                            if op.prev_dma is not None:
                                wait(dsems[op.dma_slot], op.prev_dma.dma_val, ("d", op.dma_slot))
                            op.fn(eng).then_inc(dsems[op.dma_slot], 16)
                        else:
                            ins = op.fn(eng)
                            if op.signal:
                                ins.then_inc(sems[e], 1)
                    if e == "sp":
                        for d in dma_last:
                            if d is not None:
                                wait(dsems[d.dma_slot], d.dma_val, ("d", d.dma_slot))

                getattr(block, names[e])(body)


import os as _os
DEBUG_TAGS = bool(_os.environ.get("KCRIT"))
EVAC_PRIO = bool(int(_os.environ.get("KEVAC", "0")))
TAGS = {}


def fsz(v):
    if isinstance(v, LV):
        return v.n
    n = 1
    for d in v.ap.shape[1:]:
        n *= d
    return n


class Rec:
    def __init__(self, prio=0):
        self.chunks = [[]]
        self.prio = prio

    def cut(self):
        if self.chunks[-1]:
            self.chunks.append([])

    def op(self, eng, fn, outs, ins, dma=False, cost=0.3):
        if DEBUG_TAGS:
            import sys
            f = sys._getframe(1)
            while f.f_code.co_name in ("op", "mm", "tr", "act", "tt", "ts", "stt", "copy", "recip", "memset", "scan", "dma"):
                f = f.f_back
            TAGS[id(fn)] = f.f_lineno
        prio = self.prio
        if EVAC_PRIO and eng in ("act", "dve") and any(isinstance(v, LV) for v in ins):
            prio = -1
        self.chunks[-1].append((eng, fn, tuple(ins), tuple(outs), dma, cost, prio))

    def mm(self, out, lhsT, rhs, start=True, stop=True):
        passes = 4 if (not isinstance(rhs, LV) and rhs.ap.dtype == F32) else 1
        self.op("pe", lambda e: e.matmul(out.ap, lhsT=lhsT.ap, rhs=rhs.ap, start=start, stop=stop),
                [out], [lhsT, rhs], cost=0.035 + 0.000417 * passes * max(fsz(rhs), 64))

    def tr(self, out, in_, ident):
        self.op("pe", lambda e: e.transpose(out.ap, in_.ap, ident.ap), [out], [in_, ident], cost=0.1)

    def act(self, out, in_, func, scale=None, bias=None):
        kw = {}
        ins = [in_]
        if scale is not None:
            if isinstance(scale, V):
                kw["scale"] = scale.ap
                ins.append(scale)
            else:
                kw["scale"] = scale
        if bias is not None:
            if isinstance(bias, V):
                kw["bias"] = bias.ap
                ins.append(bias)
            else:
                kw["bias"] = bias
        self.op("act", lambda e: e.activation(out=out.ap, in_=in_.ap, func=func, **kw), [out], ins,
                cost=0.15 + 0.0008 * fsz(in_))

    def tt(self, eng, out, a, b, op):
        c = (0.07 + 0.00105 * fsz(out)) if eng == "dve" else (0.1 + 0.00227 * fsz(out))
        self.op(eng, lambda e: e.tensor_tensor(out=out.ap, in0=a.ap, in1=b.ap, op=op), [out], [a, b], cost=c)

    def ts(self, eng, out, a, s1, op0, s2=None, op1=None):
        ins = [a]
        if isinstance(s1, V):
            ins.append(s1)
            s1 = s1.ap
        if isinstance(s2, V):
            ins.append(s2)
            s2 = s2.ap
        c = (0.07 + 0.00105 * fsz(out)) if eng == "dve" else (0.1 + 0.00115 * fsz(out))
        if isinstance(s1, bass.AP) and eng == "pool":
            c = 0.1 + 0.016 * fsz(out)
        if op1 is None:
            self.op(eng, lambda e: e.tensor_scalar(out=out.ap, in0=a.ap, scalar1=s1, scalar2=None, op0=op0),
                    [out], ins, cost=c)
        else:
            self.op(eng, lambda e: e.tensor_scalar(out=out.ap, in0=a.ap, scalar1=s1, scalar2=s2, op0=op0, op1=op1),
                    [out], ins, cost=c)

    def stt(self, out, a, scalar, b, op0, op1):
        ins = [a, b]
        if isinstance(scalar, V):
            ins.append(scalar)
            scalar = scalar.ap
        self.op("dve", lambda e: e.scalar_tensor_tensor(out=out.ap, in0=a.ap, scalar=scalar, in1=b.ap,
                                                        op0=op0, op1=op1), [out], ins,
                cost=0.07 + 0.00133 * fsz(out))

    def copy(self, eng, out, in_):
        if eng == "act":
            self.op("act", lambda e: e.activation(out=out.ap, in_=in_.ap, func=AF.Copy), [out], [in_],
                    cost=0.15 + 0.0008 * fsz(in_))
        else:
            c = (0.07 + 0.00105 * fsz(out)) if eng == "dve" else (0.1 + 0.0012 * fsz(out))
            self.op(eng, lambda e: e.tensor_copy(out=out.ap, in_=in_.ap), [out], [in_], cost=c)

    def recip(self, out, in_):
        self.op("dve", lambda e: e.reciprocal(out=out.ap, in_=in_.ap), [out], [in_], cost=0.07 + 0.006 * fsz(out))

    def memset(self, eng, out, val):
        self.op(eng, lambda e: e.memset(out.ap, val), [out], [], cost=0.1 + 0.0006 * fsz(out))

    def scan(self, out, d0, d1, init):
        ins = [d0, d1]
        if isinstance(init, V):
            ins.append(init)
            init = init.ap
        self.op("dve", lambda e: e.tensor_tensor_scan(out=out.ap, data0=d0.ap, data1=d1.ap, initial=init,
                                                      op0=ALU.mult, op1=ALU.add), [out], ins,
                cost=0.07 + 0.0022 * fsz(out))

    def dma(self, eng, out, in_, extra_ins=()):
        self.op(eng, lambda e: e.dma_start(out=out.ap, in_=in_.ap), [out], [in_] + list(extra_ins), dma=True,
                cost=2.2 + out.ap.nbytes() / 150e3)


def list_schedule(spec, window=None):
    import os
    if window is None:
        window = int(os.environ.get("KWIN", "128"))
    n = len(spec)
    deps = [None] * n
    last_w = {}
    readers = {}
    for i, (eng, fn, reads, writes, dma, cost, prio) in enumerate(spec):
        d = set()
        for t in reads:
            k = id(t)
            if k in last_w:
                d.add(last_w[k])
            if t.excl:
                for r in readers.get(k, ()):
                    if spec[r][0] != eng:
                        d.add(r)
        for t in writes:
            k = id(t)
            if k in last_w:
                d.add(last_w[k])
            d.update(readers.get(k, ()))
        for t in reads:
            if not t.const:
                readers.setdefault(id(t), []).append(i)
        for t in writes:
            last_w[id(t)] = i
            readers[id(t)] = []
        d.discard(i)
        deps[i] = tuple(d)
    engs = ("pe", "act", "dve", "pool", "sp")
    if int(os.environ.get("KBLEVEL", "1")):
        bl = [0.0] * n
        for i in range(n - 1, -1, -1):
            bl[i] += spec[i][5]
            for d in deps[i]:
                v = bl[i] + 0.4
                if v > bl[d]:
                    bl[d] = v
        spec = [(o[0], o[1], o[2], o[3], o[4], o[5], -bl[i]) for i, o in enumerate(spec)]
    pending = {e: [i for i in range(n) if spec[i][0] == e] for e in engs}
    head = {e: 0 for e in engs}
    done = [False] * n
    finish = [0.0] * n
    efree = {e: 0.0 for e in engs}
    order = []
    LAT = float(os.environ.get("KLAT", "0.4"))
    INF = 1e30
    _sc = {e: float(os.environ.get("KS_" + e, "1.0")) for e in engs}
    if any(v != 1.0 for v in _sc.values()):
        spec = [(o[0], o[1], o[2], o[3], o[4], o[5] * (_sc[o[0]] if not o[4] else 1.0), o[6]) for o in spec]
    _kd = float(os.environ.get("KDMA", "1.0"))
    if _kd != 1.0:
        spec = [(o[0], o[1], o[2], o[3], o[4], o[5] * (_kd if o[4] else 1.0), o[6]) for o in spec]

    def candidate(e):
        lst = pending[e]
        h = head[e]
        while h < len(lst) and done[lst[h]]:
            h += 1
        head[e] = h
        best = None
        bkey = None
        cnt = 0
        j = h
        ef = efree[e]
        while j < len(lst) and cnt < window:
            i = lst[j]
            j += 1
            if done[i]:
                continue
            cnt += 1
            ok = True
            st = ef
            for d in deps[i]:
                if not done[d]:
                    ok = False
                    break
                f = finish[d] + (LAT if spec[d][0] != e or spec[d][4] else 0.05)
                if f > st:
                    st = f
            if not ok:
                continue
            key = (st, spec[i][6], i)
            if bkey is None or key < bkey:
                best, bkey = i, key
                if st <= ef + 1e-9 and spec[i][6] == 0:
                    break
        return best, (bkey[0] if bkey else INF)

    binder = {}
    startt = {}
    last_on = {}
    remaining = n
    while remaining:
        pick = None
        pstart = INF
        for e in engs:
            i, st = candidate(e)
            if i is not None and st < pstart:
                pick, pstart = i, st
        assert pick is not None, "scheduler deadlock"
        eng, fn, reads, writes, dma, cost, prio = spec[pick]
        done[pick] = True
        finish[pick] = pstart + cost
        if DEBUG_TAGS:
            bind = ("eng", last_on.get(eng))
            bt = efree[eng]
            for d in deps[pick]:
                f = finish[d] + (LAT if spec[d][0] != eng or spec[d][4] else 0.05)
                if f > bt + 1e-9:
                    bt = f
                    bind = ("dep", d)
            binder[pick] = bind
            startt[pick] = pstart
            last_on[eng] = pick
        efree[eng] = pstart + (0.15 if dma else cost)
        order.append(pick)
        remaining -= 1
    import os
    if os.environ.get("KDEBUG"):
        busy = {e: 0.0 for e in engs}
        for i in range(n):
            busy[spec[i][0]] += (0.15 if spec[i][4] else spec[i][5])
        print("sched makespan(us):", max(finish), "busy:", {e: round(v) for e, v in busy.items()})
    if DEBUG_TAGS:
        import collections
        cur = max(range(n), key=lambda i: finish[i])
        t_hi = float(os.environ.get("KCRIT_HI", "1e9"))
        t_lo = float(os.environ.get("KCRIT_LO", "0"))
        agg = collections.OrderedDict()
        tot = collections.Counter()
        while cur is not None:
            kind, prev = binder.get(cur, ("eng", None))
            if t_lo <= startt[cur] <= t_hi:
                key = (spec[cur][0], TAGS.get(id(spec[cur][1]), 0), kind)
                a = agg.setdefault(key, [0, 0.0])
                a[0] += 1
                a[1] += finish[cur] - startt[cur]
                tot[(spec[cur][0], kind)] += finish[cur] - startt[cur]
            cur = prev
        print("critical path ops by (eng, line, binding):")
        for k, (c, tme) in sorted(agg.items(), key=lambda x: -x[1][1])[:40]:
            print("  ", k, "n=%d time=%.1f" % (c, tme))
        print("totals:", {k: round(v, 1) for k, v in tot.items()})
    return order


def merge_chunks(recs):
    lists = [[c for c in r.chunks if c] for r in recs]
    lists = [l for l in lists if l]
    pos = [0] * len(lists)
    out = []
    while True:
        best = None
        bestf = None
        for i, l in enumerate(lists):
            if pos[i] < len(l):
                f = (pos[i] + 0.5) / len(l)
                if best is None or f < bestf:
                    best, bestf = i, f
        if best is None:
            break
        out.append(lists[best][pos[best]])
        pos[best] += 1
    return out


def build_program(nseq, S):
    NT = S // TT
    NTOK = nseq * S
    nc = bass.Bass("TRN2", target_bir_lowering=False)
    xT_d = nc.dram_tensor("xT", [NCH, 128, NTOK], F32, kind="ExternalInput").ap()
    wts_d = nc.dram_tensor("wts", [NGRP, 128, 4096], F32, kind="ExternalInput").ap()
    cm_d = nc.dram_tensor("cm", [128, 8, 128], F32, kind="ExternalInput").ap()
    vecs_d = nc.dram_tensor("vecs", [128, NVEC], F32, kind="ExternalInput").ap()
    wgate_d = nc.dram_tensor("wgate", [128, 2, 4, 128], F32, kind="ExternalInput").ap()
    wba_d = nc.dram_tensor("wba", [128, 8, 8], F32, kind="ExternalInput").ap()
    yT_d = nc.dram_tensor("yT", [NCH, 128, NTOK], F32, kind="ExternalOutput").ap()
    wbf_d = nc.dram_tensor("wbf", [NGRP, 128, 4096], BF16, kind="Internal").ap()

    es = ExitStack()
    with es:
        sb_bytes = [0]

        def sb(name, shape, dt):
            n = 1
            for d in shape[1:]:
                n *= d
            sb_bytes[0] += n * (4 if dt == F32 else 2)
            return es.enter_context(nc.sbuf_tensor(name, shape, dt))

        def ps(name, shape, dt):
            return es.enter_context(nc.psum_tensor(name, shape, dt))

        ringA_h = sb("ringA", [128, 2, 4096], BF16)
        ringB_h = sb("ringB", [128, 3, 4096], BF16)
        ringA = [T(ringA_h[:, i, :]) for i in range(2)]
        ringB = [T(ringB_h[:, i, :]) for i in range(3)]
        x_h = sb("x", [128, 2, NCH, TT], F32)
        xA = T(x_h[:, 0, :, :])
        xBc = [T(x_h[:, 1, c, :]) for c in range(NCH)]
        hA_h = sb("hA", [128, NCH, TT], BF16)
        hB_h = sb("hB", [128, NCH, TT], BF16)
        hA = [T(hA_h[:, c, :]) for c in range(NCH)]
        hB = [T(hB_h[:, c, :]) for c in range(NCH)]
        uT_h = sb("uT", [128, 16, TT], BF16)
        uT = [T(uT_h[:, c, :]) for c in range(16)]
        qkn_h = sb("qkn", [128, NH, 4, 2, 128], BF16)
        qkn = [T(qkn_h[:, :, :, kq, :]) for kq in range(2)]
        vT_h = sb("vT", [128, NH, TT], BF16)
        vT = [T(vT_h[:, h, :]) for h in range(NH)]
        sz_h = sb("sz", [128, NH, TT], BF16)
        sz = [T(sz_h[:, h, :]) for h in range(NH)]
        gg_h = sb("gg", [128, 4, TT], BF16)
        gg = [T(gg_h[:, c, :]) for c in range(4)]
        mix_h = sb("mix", [128, NCH, TT], BF16)
        mix = [T(mix_h[:, c, :]) for c in range(NCH)]
        osb_h = sb("osb", [128, NH, TT], BF16)
        osb = [T(osb_h[:, h, :]) for h in range(NH)]
        raw_h = sb("raw", [128, 2, TT + 4], F32)
        raw = [T(raw_h[:, i, :]) for i in range(2)]
        acc_h = sb("acc", [128, 2, TT], F32)
        acc = [T(acc_h[:, i, :]) for i in range(2)]
        sqb_h = sb("sqb", [128, 2, TT], BF16)
        sqb = [T(sqb_h[:, i, :]) for i in range(2)]
        rs_h = sb("rs", [128, 2, TT], F32)
        rs = [T(rs_h[:, i, :]) for i in range(2)]
        rsB = T(sb("rsB", [128, TT], F32)[:, :])
        reluB_h = sb("reluB", [128, 2, TT], BF16)
        reluB = [T(reluB_h[:, i, :]) for i in range(2)]
        lt_h = sb("lt", [128, 4, TT], F32)
        lt = [T(lt_h[:, i, :]) for i in range(4)]
        xrb = T(sb("xrb", [128, TT], BF16)[:, :])
        halo_h = sb("halo", [128, 16, 4], F32)
        halo = [T(halo_h[:, c, :]) for c in range(16)]
        hst_h = sb("hst", [128, 4, 2], F32)
        hst = [T(hst_h[:, c, :]) for c in range(4)]
        S32_h = sb("S32", [128, NH, 128], F32)
        S32 = [T(S32_h[:, h, :]) for h in range(NH)]
        S32A = S32_h[:, :, :]
        Sbf_h = sb("Sbf", [128, NH, 128], BF16)
        Sbf = [T(Sbf_h[:, h, :]) for h in range(NH)]
        SbfA = Sbf_h[:, :, :]
        cm = T(sb("cm_sb", [128, 8, 128], F32)[:, :, :], const=True)
        cmb = T(sb("cmb", [128, 4, 128], BF16)[:, :, :], const=True)
        vecs = T(sb("vecs_sb", [128, NVEC], F32)[:, :], const=True)
        dvec = T(sb("dvec", [128, 24], F32)[:, :], const=True)
        wgate = T(sb("wgate_sb", [128, 2, 4, 128], BF16)[:, :, :, :], const=True)
        wba = T(sb("wba_sb", [128, 8, 8], BF16)[:, :, :], const=True)
        L1, L2, MB, SM, ID, ONES, CM0, CM1 = [cm[:, i, :] for i in range(8)]
        Cb = cmb[:, 0, :]
        SMb = cmb[:, 1, :]
        IDb = cmb[:, 2, :]
        ONESb = cmb[:, 3, :]
        gt_h = sb("gt", [128, 12, 16], F32)
        gtT = [T(gt_h[:, i, :]) for i in range(12)]
        NSET = 2
        TN = ["decT", "decS", "egcB", "N0", "Y0", "Xa", "Ya", "kg", "Ptmp"]
        ut_h = sb("ut", [128, len(TN), NH, 128], BF16)
        UT = {nm: [T(ut_h[:, i, h, :]) for h in range(NH)] for i, nm in enumerate(TN)}
        UTA = {nm: ut_h[:, i, :, :] for i, nm in enumerate(TN)}
        PN = ["Pfin", "kdec", "vtok", "attnT", "qg", "nw0"]
        up_h = sb("up", [128, NSET, len(PN), NH, 128], BF16)
        UP = [{nm: [T(up_h[:, s_, i, h, :]) for h in range(NH)] for i, nm in enumerate(PN)} for s_ in range(NSET)]
        UPA = [{nm: up_h[:, s_, i, :, :] for i, nm in enumerate(PN)} for s_ in range(NSET)]
        ug_h = sb("ug", [128, NH, 128], F32)
        UG = [T(ug_h[:, h, :]) for h in range(NH)]
        vnew_h = sb("vnew", [128, NH, 128], BF16)
        VNEW = [T(vnew_h[:, h, :]) for h in range(NH)]
        c4_h = sb("c4", [128, 3, NH, 128], BF16)
        c4 = T(c4_h[:, :, :, :], const=True)

        def bank(name, dt=F32, n=TT):
            h_ = ps(name, [128, n], dt)
            return h_, T(h_[:, :], excl=True)
        pdense = [bank("pd%d" % i)[1] for i in range(3)]
        pdense.append(bank("pstat")[1])
        pdense += [bank("pq%d" % i)[1] for i in range(2)]
        pdense += [bank("qc")[1], bank("qd")[1]]

        wts_t = [T(wts_d[g], const=True) for g in range(NGRP)]
        wbf_t = [T(wbf_d[g]) for g in range(NGRP)]
        xT_t = T(xT_d, const=True)

        P = Prog()

        spec = []
        bank_free_after = [-1] * 8

        def flush(recs):
            ops = [o for chunk in merge_chunks(recs) for o in chunk]
            base = len(spec)
            last_use = {}
            for i, o in enumerate(ops):
                for v in o[2] + o[3]:
                    if isinstance(v.t, LatePD):
                        last_use[id(v.t)] = base + i
            for i, (eng, fn, ins, outs, dma, cost, prio) in enumerate(ops):
                idx = base + i
                rt = []
                for grp in (ins, outs):
                    lst = []
                    for v in grp:
                        t = v.t
                        if isinstance(t, LatePD):
                            if t.bound is None:
                                cands = [k for k in range(len(pdense)) if bank_free_after[k] < idx]
                                assert cands, "too many live short-lived PSUM tiles"
                                k = min(cands, key=lambda k_: bank_free_after[k_])
                                t.bound = pdense[k]
                                bank_free_after[k] = last_use[id(t)]
                            t = t.bound.root
                        lst.append(t)
                    rt.append(lst)
                spec.append((eng, fn, rt[0], rt[1], dma, cost, prio))

        o_nw1, o_nw2, o_fnw, o_cw, o_lcw, o_lcb, o_lba, o_lbx, o_lap, o_alog, o_dtb, o_gnw = \
            0, 8, 16, 24, 72, 88, 92, 96, 100, 104, 108, 112

        R = Rec()
        R.dma("sp", cm.v, T(cm_d, const=True).v)
        R.dma("sp", vecs.v, T(vecs_d, const=True).v)
        R.dma("pool", cmb.v, V(T(cm_d, const=True), cm_d[:, 2:6, :]))
        R.dma("pool", wgate.v, T(wgate_d, const=True).v)
        R.dma("pool", wba.v, T(wba_d, const=True).v)
        for g in range(6):
            R.dma("pool", wbf_t[g].v, wts_t[g].v)
        for h in range(NH):
            R.copy("pool", c4[:, 0, h, :], SMb)
            R.copy("pool", c4[:, 1, h, :], IDb)
            R.copy("pool", c4[:, 2, h, :], Cb)
        tmpv = T(sb("tmpv", [128, 8, 16], F32)[:, :, :])

        def ln1p_small(R, out, e, w, k0):
            z = tmpv[:, k0, 0:w]
            z2 = tmpv[:, k0 + 1, 0:w]
            pl = tmpv[:, k0 + 2, 0:w]
            R.ts("dve", z, e, 2.0, ALU.add)
            R.recip(z, z)
            R.tt("dve", z, z, e, ALU.mult)
            R.tt("dve", z2, z, z, ALU.mult)
            R.ts("dve", pl, z2, 1.0 / 9, ALU.mult, 1.0 / 7, ALU.add)
            R.tt("dve", pl, pl, z2, ALU.mult)
            R.ts("dve", pl, pl, 1.0 / 5, ALU.add)
            R.tt("dve", pl, pl, z2, ALU.mult)
            R.ts("dve", pl, pl, 1.0 / 3, ALU.add)
            R.tt("dve", pl, pl, z2, ALU.mult)
            R.ts("dve", pl, pl, 1.0, ALU.add)
            R.tt("dve", pl, pl, z, ALU.mult)
            R.ts("dve", out, pl, 2.0, ALU.mult)

        def softplus(R, out, x, w, k0):
            ab = tmpv[:, k0 + 3, 0:w]
            l1 = tmpv[:, k0 + 4, 0:w]
            R.ts("dve", ab, x, -1.0, ALU.mult)
            R.tt("dve", ab, ab, x, ALU.min)
            R.act(ab, ab, AF.Exp)
            ln1p_small(R, l1, ab, w, k0)
            R.stt(out, x, 0.0, l1, ALU.max, ALU.add)

        ngl = tmpv[:, 7, 0:4]
        R.ts("dve", ngl, vecs[:, o_lap:o_lap + 4], -1.0, ALU.mult)
        softplus(R, ngl, ngl, 4, 0)
        R.ts("dve", dvec[:, 0:4], ngl, -8.0, ALU.mult)
        R.act(dvec[:, 4:8], vecs[:, o_alog:o_alog + 4], AF.Exp)
        R.ts("dve", dvec[:, 4:8], dvec[:, 4:8], -1.0, ALU.mult)
        R.ts("dve", dvec[:, 8:12], vecs[:, o_lba:o_lba + 4], -1.0, ALU.mult)
        R.ts("dve", dvec[:, 12:16], vecs[:, o_lbx:o_lbx + 4], -1.0, ALU.mult)
        R.ts("dve", dvec[:, 16:20], dvec[:, 0:4], 2.0, ALU.mult)
        flush([R])

        ring_state = {"A": [0, 0], "B": [0, 0]}
        ntiles = nseq * NT
        seqA = [g for _ in range(ntiles) for g in range(6)]
        seqB = [g for _ in range(ntiles) for g in (6, 7, 8, 9, 10, 11, 12, 13, 14, 15, 16, 17, 18, 19, 20, 21, 22, 23)]

        def ring_next(R, which):
            ring = ringA if which == "A" else ringB
            seq = seqA if which == "A" else seqB
            st = ring_state[which]
            while st[0] < len(seq) and st[0] < st[1] + len(ring):
                R.dma("sp", ring[st[0] % len(ring)].v, wbf_t[seq[st[0]]].v)
                st[0] += 1
            slot = ring[st[1] % len(ring)]
            st[1] += 1
            return slot

        def next_pdB():
            return LatePD()

        def sig_of(R, out, x, nscale):
            R.act(out, x, AF.Exp, scale=nscale)
            R.act(out, out, AF.Ln, bias=1.0)
            R.act(out, out, AF.Exp, scale=-1.0)

        def rmsnorm_to(R, xc, hdst, wcol, rsbuf):
            for c in range(NCH):
                R.act(hdst[c].v, xc(c), AF.Square)
            pst = LatePD()
            for c in range(NCH):
                R.mm(pst.v, ONESb, hdst[c].v, start=(c == 0), stop=(c == NCH - 1))
            R.act(rsbuf.v, pst.v, AF.Ln, scale=1.0 / D, bias=EPS)
            R.act(rsbuf.v, rsbuf.v, AF.Exp, scale=-0.5)
            R.cut()
            for c in range(NCH):
                R.stt(hdst[c].v, xc(c), vecs[:, wcol + c:wcol + c + 1], rsbuf.v, ALU.mult, ALU.mult)
            R.cut()

        def gen_A(ti):
            s_i, j_i = divmod(ti, NT)
            tok0 = s_i * S + j_i * TT
            first = (j_i == 0)
            R = Rec()
            R.dma("sp", xA.v, V(xT_t, xT_d[:, :, tok0:tok0 + TT].rearrange("c p t -> p c t")))
            if first:
                for c in range(16):
                    R.memset("pool", halo[c].v, 0.0)
                for c in range(4):
                    R.memset("pool", hst[c].v, 0.0)
                for h in range(NH):
                    R.memset("pool", S32[h].v, 0.0)
                    R.memset("pool", Sbf[h].v, 0.0)
            R.cut()
            rmsnorm_to(R, lambda c: xA[:, c, :], hA, o_nw1, rs[0])
            if ti == 0:
                for g in range(6, NGRP):
                    R.dma("pool", wbf_t[g].v, wts_t[g].v, extra_ins=[rs[0].v])

            pg = LatePD()
            for jj in range(4):
                for c in range(NCH):
                    R.mm(pg[:, 8 * jj:8 * jj + 8], hA[c][:, 128 * jj:128 * (jj + 1)], wba[:, c, :],
                         start=(c == 0), stop=(c == NCH - 1))
            R.cut()
            beta, nbeta, gtm, tA, egc, erem, egl0, egl1 = [gtT[i] for i in range(8)]
            dtb3 = V(vecs, vecs.ap[:, o_dtb:o_dtb + 4])
            for jj in range(4):
                R.act(beta[:, 4 * jj:4 * jj + 4], pg[:, 8 * jj:8 * jj + 4], AF.Exp, scale=-1.0)
                R.tt("dve", tA[:, 4 * jj:4 * jj + 4], pg[:, 8 * jj + 4:8 * jj + 8], dtb3, ALU.add)
            R.act(beta.v, beta.v, AF.Ln, bias=1.0)
            R.act(beta.v, beta.v, AF.Exp, scale=-1.0)
            R.ts("dve", nbeta.v, beta.v, -1.0, ALU.mult)
            softplus(R, tA.v, tA.v, 16, 0)
            for jj in range(4):
                R.tt("dve", gtm[:, 4 * jj:4 * jj + 4], tA[:, 4 * jj:4 * jj + 4], dvec[:, 4:8], ALU.mult)
            pg2 = LatePD()
            R.mm(pg2[:, 0:16], L2, gtm.v)
            R.mm(pg2[:, 16:32], L1, gtm.v)
            R.mm(pg2[:, 32:48], CM0, gtm.v)
            R.act(egc.v, pg2[:, 0:16], AF.Exp)
            R.act(erem.v, pg2[:, 16:32], AF.Exp)
            R.act(egl0.v, pg2[:, 32:48], AF.Exp)
            R.cut()

            rot = [0]

            def conv4(R, ch, wcol, bias, out_v, pdt):
                rw = raw[rot[0] % 2]
                rot[0] += 1
                R.copy("pool", rw[:, 0:3], halo[ch][:, 0:3])
                R.copy("act", rw[:, 3:3 + TT], pdt.v)
                R.copy("pool", halo[ch][:, 0:3], rw[:, TT:TT + 3])
                if bias is None:
                    R.ts("dve", out_v, rw[:, 0:TT], vecs[:, wcol:wcol + 1], ALU.mult)
                else:
                    R.ts("dve", out_v, rw[:, 0:TT], vecs[:, wcol:wcol + 1], ALU.mult, bias, ALU.add)
                for k in range(1, 4):
                    R.stt(out_v, rw[:, k:k + TT], vecs[:, wcol + k:wcol + k + 1], out_v, ALU.mult, ALU.add)

            R2 = Rec()
            for g in range(6):
                Rc = R if g < 3 else R2
                wslot = ring_next(Rc, "A")
                for n in range(4):
                    ch = 4 * g + n
                    pdt = LatePD()
                    for c in range(NCH):
                        Rc.mm(pdt.v, wslot[:, 512 * c + 128 * n:512 * c + 128 * (n + 1)], hA[c].v,
                             start=(c == 0), stop=(c == NCH - 1))
                    Rc.cut()
                    if ch < 12:
                        a_ = acc[ch % 2]
                        conv4(Rc, ch, o_cw + 4 * ch, None, a_.v, pdt)
                        sg_ = rs[ch % 2]
                        sig_of(Rc, sg_.v, a_.v, -1.0)
                        if ch >= 8:
                            Rc.tt("pool", vT[ch - 8].v, a_.v, sg_.v, ALU.mult)
                        else:
                            kq = 1 if ch < 4 else 0
                            h = ch % 4
                            Rc.tt("pool", a_.v, a_.v, sg_.v, ALU.mult)
                            sq_ = sqb[ch % 2]
                            Rc.act(sq_.v, a_.v, AF.Square)
                            pst = LatePD()
                            Rc.mm(pst.v, ONESb, sq_.v)
                            r_ = rs[ch % 2]
                            if kq == 1:
                                Rc.act(r_.v, pst.v, AF.Ln, scale=128.0, bias=128.0 * EPS)
                            else:
                                Rc.act(r_.v, pst.v, AF.Ln, scale=1.0, bias=EPS)
                            Rc.act(r_.v, r_.v, AF.Exp, scale=-0.5)
                            Rc.tt("dve", qkn[kq][:, h, :, :],
                                 V(a_, a_.ap.rearrange("p (j t) -> p j t", j=4)),
                                 V(r_, r_.ap.rearrange("p (j t) -> p j t", j=4)), ALU.mult)
                    elif ch < 16:
                        sg_ = rs[ch % 2]
                        a_ = acc[ch % 2]
                        Rc.copy("act", a_.v, pdt.v)
                        sig_of(Rc, sg_.v, a_.v, -1.0)
                        Rc.tt("pool", sz[ch - 12].v, a_.v, sg_.v, ALU.mult)
                    elif ch < 20:
                        sg_ = rs[ch % 2]
                        a_ = acc[ch % 2]
                        Rc.copy("act", a_.v, pdt.v)
                        Rc.act(sg_.v, a_.v, AF.Square)
                        Rc.ts("pool", sg_.v, sg_.v, 0.044715, ALU.mult, 1.0, ALU.add)
                        Rc.tt("dve", sg_.v, a_.v, sg_.v, ALU.mult)
                        sig_of(Rc, sg_.v, sg_.v, -1.5957691216057308)
                        Rc.tt("dve", gg[ch - 16].v, a_.v, sg_.v, ALU.mult)
                    else:
                        lc = ch - 20
                        xr, ra, a2, ig = lt
                        conv4(Rc, 12 + lc, o_lcw + 4 * lc, vecs[:, o_lcb + lc:o_lcb + lc + 1], xr.v, pdt)
                        Rc.copy("act", xrb.v, xr.v)
                        pr = LatePD()
                        Rc.mm(pr.v, wgate[:, 0, lc, :], xrb.v)
                        pi = LatePD()
                        Rc.mm(pi.v, wgate[:, 1, lc, :], xrb.v)
                        Rc.act(ra.v, pr.v, AF.Exp, scale=-1.0, bias=dvec[:, 8 + lc:9 + lc])
                        Rc.act(ra.v, ra.v, AF.Ln, bias=1.0)
                        Rc.act(ra.v, ra.v, AF.Exp, scale=-1.0)
                        Rc.act(a2.v, ra.v, AF.Exp, scale=dvec[:, 16 + lc:17 + lc])
                        Rc.act(ra.v, ra.v, AF.Exp, scale=dvec[:, lc:lc + 1])
                        Rc.act(a2.v, a2.v, AF.Ln, scale=-1.0, bias=1.0)
                        Rc.act(a2.v, a2.v, AF.Exp, scale=0.5)
                        Rc.act(ig.v, pi.v, AF.Exp, scale=-1.0, bias=dvec[:, 12 + lc:13 + lc])
                        Rc.act(ig.v, ig.v, AF.Ln, bias=1.0)
                        Rc.act(ig.v, ig.v, AF.Exp, scale=-1.0)
                        Rc.tt("pool", ig.v, ig.v, xr.v, ALU.mult)
                        Rc.tt("pool", ig.v, ig.v, a2.v, ALU.mult)
                        Rc.scan(xr.v, ra.v, ig.v, hst[lc][:, 0:1])
                        Rc.copy("pool", hst[lc][:, 0:1], xr[:, TT - 1:TT])
                        Rc.tt("pool", mix[4 + lc].v, xr.v, gg[lc].v, ALU.mult)
                    Rc.cut()

            def hv(lst):
                return [t.v for t in lst]

            def pre_jj(Rp, jj):
                st_ = jj % NSET
                u, ua = UT, UTA
                p, pa = UP[st_], UPA[st_]
                cols = [4 * jj + h for h in range(NH)]
                kT4 = qkn[0][:, :, jj, :]
                qT4 = qkn[1][:, :, jj, :]
                for h in range(NH):
                    Rp.ts("dve", UG[h].v, L2, gtm[:, cols[h]:cols[h] + 1], ALU.mult)
                Rp.cut()
                q = LatePD()
                for h in range(NH):
                    Rp.mm(q[:, 128 * h:128 * (h + 1)], L1, UG[h].v)
                Rp.op("act", lambda e, q=q: e.activation(out=ua["decT"], in_=q.v.ap, func=AF.Exp),
                      hv(u["decT"]), [q.v], cost=0.56)
                Rp.op("pool", lambda e: e.tensor_tensor(out=ua["decT"], in0=ua["decT"], in1=c4_h[:, 2, :, :], op=ALU.mult),
                      hv(u["decT"]), hv(u["decT"]) + [c4.v], cost=1.26)
                Rp.op("pool", lambda e: e.tensor_tensor(out=ua["decS"], in0=ua["decT"], in1=c4_h[:, 0, :, :], op=ALU.mult),
                      hv(u["decS"]), hv(u["decT"]) + [c4.v], cost=1.26)
                Rp.cut()
                q = LatePD()
                for h in range(NH):
                    Rp.mm(q[:, 128 * h:128 * (h + 1)], kT4.t[:, h, jj, :], kT4.t[:, h, jj, :])
                for h in range(NH):
                    Rp.stt(u["N0"][h].v, q[:, 128 * h:128 * (h + 1)], nbeta[:, cols[h]:cols[h] + 1],
                           u["decS"][h].v, ALU.mult, ALU.mult)
                Rp.cut()
                q = LatePD()
                for h in range(NH):
                    Rp.mm(q[:, 128 * h:128 * (h + 1)], kT4.t[:, h, jj, :], qT4.t[:, h, jj, :])
                Rp.op("dve", lambda e, q=q: e.tensor_tensor(out=pa["attnT"], in0=q.v.ap, in1=ua["decT"], op=ALU.mult),
                      hv(p["attnT"]), [q.v] + hv(u["decT"]), cost=0.62)
                Rp.cut()
                q = LatePD()
                for h in range(NH):
                    Rp.mm(q[:, 128 * h:128 * (h + 1)], ONES, UG[h].v)
                Rp.op("act", lambda e, q=q: e.activation(out=ua["egcB"], in_=q.v.ap, func=AF.Exp),
                      hv(u["egcB"]), [q.v], cost=0.56)
                Rp.op("dve", lambda e: e.tensor_tensor(out=pa["qg"], in0=qT4.ap, in1=ua["egcB"], op=ALU.mult),
                      hv(p["qg"]), [qT4] + hv(u["egcB"]), cost=0.62)
                Rp.cut()
                q = LatePD()
                for h in range(NH):
                    Rp.mm(q[:, 128 * h:128 * (h + 1)], kT4.t[:, h, jj, :], IDb)
                for h in range(NH):
                    Rp.act(u["kg"][h].v, q[:, 128 * h:128 * (h + 1)], AF.Copy, scale=egc[:, cols[h]:cols[h] + 1])
                    Rp.act(p["kdec"][h].v, q[:, 128 * h:128 * (h + 1)], AF.Copy, scale=erem[:, cols[h]:cols[h] + 1])
                Rp.cut()
                q = LatePD()
                for h in range(NH):
                    Rp.mm(q[:, 128 * h:128 * (h + 1)], vT[h][:, 128 * jj:128 * (jj + 1)], IDb)
                Rp.op("act", lambda e, q=q: e.activation(out=pa["vtok"], in_=q.v.ap, func=AF.Copy),
                      hv(p["vtok"]), [q.v], cost=0.56)
                Rp.cut()
                q = LatePD()
                for h in range(NH):
                    Rp.mm(q[:, 128 * h:128 * (h + 1)], u["N0"][h].v, IDb)
                Rp.op("act", lambda e, q=q: e.activation(out=ua["Y0"], in_=q.v.ap, func=AF.Copy),
                      hv(u["Y0"]), [q.v], cost=0.56)
                Rp.op("pool", lambda e: e.tensor_tensor(out=pa["Pfin"], in0=ua["N0"], in1=c4_h[:, 1, :, :], op=ALU.add),
                      hv(p["Pfin"]), hv(u["N0"]) + [c4.v], cost=1.26)
                Rp.cut()
                Xs = ["N0", "Xa", "N0", "Xa", "N0", "Xa"]
                Ys = ["Y0", "Ya", "Y0", "Ya", "Y0", "Ya", "Y0"]
                Ps = [("p", "Pfin"), ("u", "Ptmp"), ("p", "Pfin"), ("u", "Ptmp"), ("p", "Pfin"), ("u", "Ptmp"),
                      ("p", "Pfin")]
                NLEV = 6

                def PT(k):
                    w, nm = Ps[k]
                    return (u[nm], ua[nm]) if w == "u" else (p[nm], pa[nm])
                for r in range(0, NLEV + 1):
                    Yc = Ys[r]
                    qy = qx = qp = None
                    if r < NLEV:
                        Xc = Xs[r]
                        qy = LatePD()
                        for h in range(NH):
                            Rp.mm(qy[:, 128 * h:128 * (h + 1)], u[Xc][h].v, u[Yc][h].v)
                    if r < NLEV - 1:
                        qx = LatePD()
                        for h in range(NH):
                            Rp.mm(qx[:, 128 * h:128 * (h + 1)], u[Yc][h].v, u[Xc][h].v)
                    if r >= 1:
                        Pc_t, Pc_a = PT(r - 1)
                        Pn_t, Pn_a = PT(r)
                        qp = LatePD()
                        for h in range(NH):
                            Rp.mm(qp[:, 128 * h:128 * (h + 1)], u[Yc][h].v, Pc_t[h].v)
                    if qy is not None:
                        Yn = Ys[r + 1]
                        Rp.op("dve", lambda e, qy=qy, Yn=Yn: e.tensor_copy(out=ua[Yn], in_=qy.v.ap),
                              hv(u[Yn]), [qy.v], cost=0.62)
                    if qx is not None:
                        Xn = Xs[r + 1]
                        Rp.op("act", lambda e, qx=qx, Xn=Xn: e.activation(out=ua[Xn], in_=qx.v.ap, func=AF.Copy),
                              hv(u[Xn]), [qx.v], cost=0.56)
                    if qp is not None:
                        Rp.op("dve", lambda e, qp=qp, Pn_a=Pn_a, Pc_a=Pc_a: e.tensor_tensor(out=Pn_a, in0=qp.v.ap, in1=Pc_a, op=ALU.add),
                              hv(Pn_t), [qp.v] + hv(Pc_t), cost=0.62)
                    Rp.cut()
                q = LatePD()
                for h in range(NH):
                    Rp.mm(q[:, 128 * h:128 * (h + 1)], u["kg"][h].v, p["Pfin"][h].v)
                Rp.op("act", lambda e, q=q: e.activation(out=pa["nw0"], in_=q.v.ap, func=AF.Copy, scale=-1.0),
                      hv(p["nw0"]), [q.v], cost=0.56)
                Rp.cut()

            def rec_jj(Rr, jj):
                st_ = jj % NSET
                p, pa = UP[st_], UPA[st_]
                cols = [4 * jj + h for h in range(NH)]
                for c in range(1):
                    lo, hi = 0, 128
                    qvn, qo, qds = LatePD(), LatePD(), LatePD()
                    for h in range(NH):
                        Rr.mm(qvn[lo:hi, 128 * h:128 * (h + 1)], p["Pfin"][h][lo:hi, lo:hi], p["vtok"][h][lo:hi, :],
                              start=True, stop=False)
                        Rr.mm(qvn[lo:hi, 128 * h:128 * (h + 1)], p["nw0"][h][:, lo:hi], Sbf[h].v, start=False, stop=True)
                    for h in range(NH):
                        Rr.ts("dve", VNEW[h][lo:hi, :], qvn[lo:hi, 128 * h:128 * (h + 1)],
                              beta[lo:hi, cols[h]:cols[h] + 1], ALU.mult)
                    Rr.cut()
                    for h in range(NH):
                        Rr.mm(qo[:, 128 * h + lo:128 * h + hi], Sbf[h].v, p["qg"][h][:, lo:hi], start=True, stop=False)
                        Rr.mm(qo[:, 128 * h + lo:128 * h + hi], VNEW[h][lo:hi, :], p["attnT"][h][lo:hi, lo:hi],
                              start=False, stop=True)
                    for h in range(NH):
                        Rr.mm(qds[:, 128 * h:128 * (h + 1)], p["kdec"][h][lo:hi, :], VNEW[h][lo:hi, :])
                    egl = egl0 if c == 0 else egl1
                    for h in range(NH):
                        Rr.stt(S32[h].v, S32[h].v, egl[:, cols[h]:cols[h] + 1], qds[:, 128 * h:128 * (h + 1)],
                               ALU.mult, ALU.add)
                    Rr.op("act", lambda e: e.activation(out=SbfA, in_=S32A, func=AF.Copy), hv(Sbf), hv(S32), cost=0.56)
                    Rr.cut()
                Rr.op("act", lambda e, jj=jj, qo=qo: e.activation(out=osb_h[:, :, 128 * jj:128 * (jj + 1)],
                                                                  in_=qo.v.ap.rearrange("p (h t) -> p h t", h=NH), func=AF.Copy),
                      hv(osb), [qo.v], cost=0.56)
                Rr.cut()

            Rg = Rec()
            pre_jj(Rg, 0)
            gd = Rec()
            gd.chunks = merge_chunks([Rg])
            for jj in range(4):
                Rr = Rec()
                rec_jj(Rr, jj)
                Rp = Rec()
                if jj + 1 < 4:
                    pre_jj(Rp, jj + 1)
                gd.chunks.extend(merge_chunks([Rr, Rp]))
            flush_list = [[R], [R2, gd]]
            Rn = Rec()
            for h in range(NH):
                sq_ = sqb[h % 2]
                Rn.act(sq_.v, osb[h].v, AF.Square)
                pst = LatePD()
                Rn.mm(pst.v, ONESb, sq_.v)
                r_ = rs[h % 2]
                Rn.act(r_.v, pst.v, AF.Ln, scale=1.0 / 128, bias=EPS)
                Rn.act(r_.v, r_.v, AF.Exp, scale=-0.5)
                ot = acc[h % 2]
                Rn.stt(ot.v, osb[h].v, vecs[:, o_gnw:o_gnw + 1], r_.v, ALU.mult, ALU.mult)
                Rn.tt("dve", mix[h].v, ot.v, sz[h].v, ALU.mult)
                Rn.cut()
            flush_list.append([Rn])
            return flush_list

        def gen_B(ti):
            s_i, j_i = divmod(ti, NT)
            tok0 = s_i * S + j_i * TT
            R = Rec(prio=1)
            for c in range(NCH):
                R.dma("sp", xBc[c].v, V(xT_t, xT_d[c, :, tok0:tok0 + TT]))
            R.cut()
            for g in range(2):
                wslot = ring_next(R, "B")
                for n in range(4):
                    dc = 4 * g + n
                    pdt = next_pdB()
                    for c in range(NCH):
                        R.mm(pdt.v, wslot[:, 512 * c + 128 * n:512 * c + 128 * (n + 1)], mix[c].v,
                             start=(c == 0), stop=(c == NCH - 1))
                    R.tt("dve", xBc[dc].v, xBc[dc].v, pdt.v, ALU.add)
                    R.cut()
            rmsnorm_to(R, lambda c: xBc[c].v, hB, o_nw2, rsB)
            for half in range(2):
                for g in range(4):
                    wslot = ring_next(R, "B")
                    for n in range(4):
                        fc = 4 * g + n
                        pdt = next_pdB()
                        for c in range(NCH):
                            R.mm(pdt.v, wslot[:, 512 * c + 128 * n:512 * c + 128 * (n + 1)], hB[c].v,
                                 start=(c == 0), stop=(c == NCH - 1))
                        rl = reluB[fc % 2]
                        R.act(rl.v, pdt.v, AF.Relu)
                        R.tt("pool", uT[fc].v, rl.v, rl.v, ALU.mult)
                        R.cut()
                for g in range(4):
                    wslot = ring_next(R, "B")
                    for n in range(2):
                        dc = 2 * g + n
                        pdt = next_pdB()
                        for fc in range(16):
                            R.mm(pdt.v, wslot[:, 256 * fc + 128 * n:256 * fc + 128 * (n + 1)], uT[fc].v,
                                 start=(fc == 0), stop=(fc == 15))
                        R.tt("dve", xBc[dc].v, xBc[dc].v, pdt.v, ALU.add)
                        R.cut()
            for c in range(NCH):
                R.act(hB[c].v, xBc[c].v, AF.Square)
            pst = LatePD()
            for c in range(NCH):
                R.mm(pst.v, ONESb, hB[c].v, start=(c == 0), stop=(c == NCH - 1))
            R.act(rsB.v, pst.v, AF.Ln, scale=1.0 / D, bias=EPS)
            R.act(rsB.v, rsB.v, AF.Exp, scale=-0.5)
            for c in range(NCH):
                R.stt(xBc[c].v, xBc[c].v, vecs[:, o_fnw + c:o_fnw + c + 1], rsB.v, ALU.mult, ALU.mult)
                R.dma("sp", T(yT_d[c, :, tok0:tok0 + TT]).v, xBc[c].v)
            R.cut()
            return [[R]]

        for grp in gen_A(0):
            flush(grp)
        for ti in range(ntiles):
            Bl = gen_B(ti)
            if ti + 1 < ntiles:
                Al = gen_A(ti + 1)
                Aflat = Rec()
                Aflat.chunks = []
                for grp in Al:
                    Aflat.chunks.extend(merge_chunks(grp))
                flush([Bl[0][0], Aflat])
            else:
                flush(Bl[0])
        import os
        if os.environ.get("KDEBUG"):
            print("SBUF bytes/partition:", sb_bytes[0], "ops:", len(spec))
        order = list_schedule(spec)
        for i in order:
            eng, fn, reads, writes, dma, cost, prio = spec[i]
            P.add(eng, fn, reads, writes, dma)
        P.emit(nc)
    return nc


def _masks():
    m = np.arange(128)
    same = np.ones((128, 128), bool)
    cm = np.zeros((128, 8, 128), np.float32)
    cm[:, 0, :] = (m[:, None] > m[None, :]) & same
    cm[:, 1, :] = (m[:, None] <= m[None, :]) & same
    cm[:, 2, :] = (m[:, None] <= m[None, :]) & same
    cm[:, 3, :] = (m[:, None] < m[None, :]) & same
    cm[:, 4, :] = np.eye(128)
    cm[:, 5, :] = 1.0
    cm[:, 6, :] = 1.0
    cm[:, 7, :] = 0.0
    return cm


def prep_shared(inp):
    f = np.float32
    w_in = np.asarray(inp["w_in"], f)[0]
    wmain = np.concatenate([w_in[:, 0:1536], w_in[:, 1536:2048], w_in[:, 2568:3080], w_in[:, 2056:2568]], axis=1)
    wba_ = w_in[:, 2048:2056]
    w_out = np.asarray(inp["w_out"], f)[0]
    w1 = np.asarray(inp["w_ff1"], f)[0]
    w2 = np.asarray(inp["w_ff2"], f)[0]
    wts = np.zeros((NGRP, 128, 4096), f)

    def colgrp(W, g):
        return W[:, 512 * g:512 * (g + 1)].reshape(8, 128, 512).transpose(1, 0, 2).reshape(128, 4096)
    for g in range(6):
        wts[g] = colgrp(wmain, g)
    for g in range(2):
        wts[6 + g] = colgrp(w_out, g)
    for half in range(2):
        base = 8 + 8 * half
        for g in range(4):
            wts[base + g] = colgrp(w1, 4 * half + g)
        for g in range(4):
            blk = w2[2048 * half:2048 * (half + 1), 256 * g:256 * (g + 1)]
            wts[base + 4 + g] = blk.reshape(16, 128, 2, 128).transpose(1, 0, 2, 3).reshape(128, 4096)
    vecs = np.zeros((128, NVEC), f)

    def pc(v, n):
        return np.asarray(v, f).reshape(n, 128).T
    vecs[:, 0:8] = pc(inp["norm_mix_w"][0], 8)
    vecs[:, 8:16] = pc(inp["norm_mlp_w"][0], 8)
    vecs[:, 16:24] = pc(inp["final_norm_w"], 8)
    cw = np.asarray(inp["gdn_conv_w"], f)[0]
    vecs[:, 24:72] = cw.reshape(4, 12, 128).transpose(2, 1, 0).reshape(128, 48)
    lcw = np.asarray(inp["lru_conv_w"], f)[0]
    vecs[:, 72:88] = lcw.reshape(4, 4, 128).transpose(2, 1, 0).reshape(128, 16)
    vecs[:, 88:92] = pc(inp["lru_conv_b"][0], 4)
    vecs[:, 92:96] = pc(np.asarray(inp["lru_gate_a_b"], f)[0].reshape(512), 4)
    vecs[:, 96:100] = pc(np.asarray(inp["lru_gate_x_b"], f)[0].reshape(512), 4)
    vecs[:, 100:104] = pc(inp["lru_a_param"][0], 4)
    vecs[:, 104:108] = np.broadcast_to(np.asarray(inp["gdn_A_log"], f)[0][None, :], (128, 4))
    vecs[:, 108:112] = np.broadcast_to(np.asarray(inp["gdn_dt_bias"], f)[0][None, :], (128, 4))
    vecs[:, 112] = np.asarray(inp["gdn_norm_w"], f)[0]
    wgate = np.zeros((128, 2, 4, 128), f)
    for gi, key in enumerate(("lru_gate_a_w", "lru_gate_x_w")):
        wg = np.asarray(inp[key], f)[0]
        for lc in range(4):
            for b in range(2):
                wgate[64 * b:64 * (b + 1), gi, lc, 64 * b:64 * (b + 1)] = wg[2 * lc + b]
    wba = wba_.reshape(8, 128, 8).transpose(1, 0, 2).copy()
    return {"wts": wts, "cm": _masks(), "vecs": vecs, "wgate": wgate, "wba": np.ascontiguousarray(wba)}


def prep_x(xs):
    nseq, S, _ = xs.shape
    return np.ascontiguousarray(xs.reshape(nseq * S, NCH, 128).transpose(1, 2, 0))


def unprep_y(yT, nseq, S):
    return np.ascontiguousarray(yT.transpose(2, 0, 1).reshape(nseq, S, D))


_NC_CACHE = {}


def kernel(**inputs):
    x = np.asarray(inputs["x"], np.float32)
    B, S, _ = x.shape
    ncores = 8
    nseq = B // ncores
    shared = prep_shared(inputs)
    key = (nseq, S)
    if key not in _NC_CACHE:
        _NC_CACHE[key] = build_program(nseq, S)
    nc = _NC_CACHE[key]
    in_maps = []
    for c in range(ncores):
        m = dict(shared)
        m["xT"] = prep_x(x[c * nseq:(c + 1) * nseq])
        in_maps.append(m)
    res = run_bass_kernel_spmd(nc, in_maps, core_ids=list(range(ncores)))
    outs = [unprep_y(np.asarray(r["yT"]), nseq, S) for r in res.results]
    return np.concatenate(outs, axis=0).astype(np.float32)
```

```python
import numpy as np
import concourse.bass as bass
import concourse.mybir as mybir
from concourse.bass_utils import run_bass_kernel_spmd
from contextlib import ExitStack

F32 = mybir.dt.float32
BF16 = mybir.dt.bfloat16
AF = mybir.ActivationFunctionType
ALU = mybir.AluOpType

D = 1024
NCH = 8
TT = 512
NH = 4
EPS = 1e-6
N_DMA_SEMS = 24
N_SP_SEMS = 16
NGRP = 24
NVEC = 120


class V:
    __slots__ = ("t", "ap")

    def __init__(self, t, ap):
        self.t = t
        self.ap = ap


class LatePD:
    __slots__ = ("bound",)

    def __init__(self):
        self.bound = None

    @property
    def v(self):
        return LV(self, None, TT)

    def __getitem__(self, idx):
        n = TT
        if isinstance(idx, tuple) and isinstance(idx[-1], slice) and idx[-1].start is not None:
            n = idx[-1].stop - idx[-1].start
        return LV(self, idx, n)


class LV:
    __slots__ = ("t", "idx", "n")

    def __init__(self, t, idx, n):
        self.t = t
        self.idx = idx
        self.n = n

    @property
    def ap(self):
        b = self.t.bound
        return b.ap if self.idx is None else b.ap[self.idx]


class T:
    __slots__ = ("ap", "name", "last_w", "readers", "const", "root", "excl")

    def __init__(self, ap, name="", const=False, parent=None, excl=False):
        self.ap = ap
        self.name = name
        self.last_w = None
        self.readers = []
        self.const = const
        self.excl = excl
        self.root = parent.root if parent is not None else self

    def __getitem__(self, idx):
        return V(self.root, self.ap[idx])

    @property
    def v(self):
        return V(self.root, self.ap)


class Op:
    __slots__ = ("eng", "fn", "deps", "signal", "count", "is_dma", "dma_slot", "dma_val", "prev_dma")

    def __init__(self, eng, fn, is_dma):
        self.eng = eng
        self.fn = fn
        self.deps = []
        self.signal = False
        self.count = 0
        self.is_dma = is_dma
        self.dma_slot = -1
        self.dma_val = 0
        self.prev_dma = None


class Prog:
    ENGS = ("pe", "act", "dve", "pool", "sp")

    def __init__(self):
        self.ops = []
        self.n_dma_sp = 0
        self.n_dma_pl = 0
        self.dma_last = [None] * N_DMA_SEMS

    def add(self, eng, fn, reads=(), writes=(), dma=False):
        op = Op(eng, fn, dma)
        deps = {}
        for t in reads:
            if t.last_w is not None:
                deps[id(t.last_w)] = t.last_w
            if t.excl:
                for r in t.readers:
                    if r.eng != eng:
                        deps[id(r)] = r
        for t in writes:
            if t.last_w is not None:
                deps[id(t.last_w)] = t.last_w
            for r in t.readers:
                deps[id(r)] = r
        for t in reads:
            if not t.const:
                t.readers.append(op)
        for t in writes:
            t.last_w = op
            t.readers = []
        for d in deps.values():
            if d is op:
                continue
            if d.is_dma:
                op.deps.append(d)
            elif d.eng == eng and not dma:
                if eng != "pe":
                    op.deps.append(d)
                    d.signal = True
            else:
                op.deps.append(d)
                d.signal = True
        if dma:
            if eng == "sp":
                k = self.n_dma_sp
                self.n_dma_sp += 1
                slot = k % N_SP_SEMS
                val = 16 * (k // N_SP_SEMS + 1)
            else:
                k = self.n_dma_pl
                self.n_dma_pl += 1
                slot = N_SP_SEMS + k % (N_DMA_SEMS - N_SP_SEMS)
                val = 16 * (k // (N_DMA_SEMS - N_SP_SEMS) + 1)
            op.dma_slot = slot
            op.dma_val = val
            op.prev_dma = self.dma_last[slot]
            self.dma_last[slot] = op
        self.ops.append(op)
        return op

    def emit(self, nc):
        counts = {e: 0 for e in self.ENGS}
        for op in self.ops:
            if not op.is_dma and op.signal:
                counts[op.eng] += 1
                op.count = counts[op.eng]
        with ExitStack() as es:
            sems = {e: es.enter_context(nc.semaphore("s_" + e)) for e in self.ENGS}
            dsems = [es.enter_context(nc.semaphore("d%d" % i)) for i in range(N_DMA_SEMS)]
            block = es.enter_context(nc.Block())
            names = {"pe": "tensor", "act": "scalar", "dve": "vector", "pool": "gpsimd", "sp": "sync"}
            dma_last = self.dma_last
            for e in self.ENGS:
                myops = [op for op in self.ops if op.eng == e]

                def body(eng, myops=myops, e=e):
                    waited = {}

                    def wait(sem, val, key):
                        if waited.get(key, 0) >= val:
                            return
                        waited[key] = val
                        eng.wait_ge(sem, val)

                    for op in myops:
                        for d in op.deps:
                            if d.is_dma:
                                wait(dsems[d.dma_slot], d.dma_val, ("d", d.dma_slot))
                            else:
                                wait(sems[d.eng], d.count, d.eng)
                        if op.is_dma:
                            if op.prev_dma is not None:
                                wait(dsems[op.dma_slot], op.prev_dma.dma_val, ("d", op.dma_slot))
                            op.fn(eng).then_inc(dsems[op.dma_slot], 16)
                        else:
                            ins = op.fn(eng)
                            if op.signal:
                                ins.then_inc(sems[e], 1)
                    if e == "sp":
                        for d in dma_last:
                            if d is not None:
                                wait(dsems[d.dma_slot], d.dma_val, ("d", d.dma_slot))

                getattr(block, names[e])(body)


import os as _os
DEBUG_TAGS = bool(_os.environ.get("KCRIT"))
EVAC_PRIO = bool(int(_os.environ.get("KEVAC", "0")))
TAGS = {}


def fsz(v):
    if isinstance(v, LV):
        return v.n
    n = 1
    for d in v.ap.shape[1:]:
        n *= d
    return n


class Rec:
    def __init__(self, prio=0):
        self.chunks = [[]]
        self.prio = prio

    def cut(self):
        if self.chunks[-1]:
            self.chunks.append([])

    def op(self, eng, fn, outs, ins, dma=False, cost=0.3):
        if DEBUG_TAGS:
            import sys
            f = sys._getframe(1)
            while f.f_code.co_name in ("op", "mm", "tr", "act", "tt", "ts", "stt", "copy", "recip", "memset", "scan", "dma"):
                f = f.f_back
            TAGS[id(fn)] = f.f_lineno
        prio = self.prio
        if EVAC_PRIO and eng in ("act", "dve") and any(isinstance(v, LV) for v in ins):
            prio = -1
        self.chunks[-1].append((eng, fn, tuple(ins), tuple(outs), dma, cost, prio))

    def mm(self, out, lhsT, rhs, start=True, stop=True):
        passes = 4 if (not isinstance(rhs, LV) and rhs.ap.dtype == F32) else 1
        self.op("pe", lambda e: e.matmul(out.ap, lhsT=lhsT.ap, rhs=rhs.ap, start=start, stop=stop),
                [out], [lhsT, rhs], cost=0.035 + 0.000417 * passes * max(fsz(rhs), 64))

    def tr(self, out, in_, ident):
        self.op("pe", lambda e: e.transpose(out.ap, in_.ap, ident.ap), [out], [in_, ident], cost=0.1)

    def act(self, out, in_, func, scale=None, bias=None):
        kw = {}
        ins = [in_]
        if scale is not None:
            if isinstance(scale, V):
                kw["scale"] = scale.ap
                ins.append(scale)
            else:
                kw["scale"] = scale
        if bias is not None:
            if isinstance(bias, V):
                kw["bias"] = bias.ap
                ins.append(bias)
            else:
                kw["bias"] = bias
        self.op("act", lambda e: e.activation(out=out.ap, in_=in_.ap, func=func, **kw), [out], ins,
                cost=0.15 + 0.0008 * fsz(in_))

    def tt(self, eng, out, a, b, op):
        c = (0.07 + 0.00105 * fsz(out)) if eng == "dve" else (0.1 + 0.00227 * fsz(out))
        self.op(eng, lambda e: e.tensor_tensor(out=out.ap, in0=a.ap, in1=b.ap, op=op), [out], [a, b], cost=c)

    def ts(self, eng, out, a, s1, op0, s2=None, op1=None):
        ins = [a]
        if isinstance(s1, V):
            ins.append(s1)
            s1 = s1.ap
        if isinstance(s2, V):
            ins.append(s2)
            s2 = s2.ap
        c = (0.07 + 0.00105 * fsz(out)) if eng == "dve" else (0.1 + 0.00115 * fsz(out))
        if isinstance(s1, bass.AP) and eng == "pool":
            c = 0.1 + 0.016 * fsz(out)
        if op1 is None:
            self.op(eng, lambda e: e.tensor_scalar(out=out.ap, in0=a.ap, scalar1=s1, scalar2=None, op0=op0),
                    [out], ins, cost=c)
        else:
            self.op(eng, lambda e: e.tensor_scalar(out=out.ap, in0=a.ap, scalar1=s1, scalar2=s2, op0=op0, op1=op1),
                    [out], ins, cost=c)

    def stt(self, out, a, scalar, b, op0, op1):
        ins = [a, b]
        if isinstance(scalar, V):
            ins.append(scalar)
            scalar = scalar.ap
        self.op("dve", lambda e: e.scalar_tensor_tensor(out=out.ap, in0=a.ap, scalar=scalar, in1=b.ap,
                                                        op0=op0, op1=op1), [out], ins,
                cost=0.07 + 0.00133 * fsz(out))

    def copy(self, eng, out, in_):
        if eng == "act":
            self.op("act", lambda e: e.activation(out=out.ap, in_=in_.ap, func=AF.Copy), [out], [in_],
                    cost=0.15 + 0.0008 * fsz(in_))
        else:
            c = (0.07 + 0.00105 * fsz(out)) if eng == "dve" else (0.1 + 0.0012 * fsz(out))
            self.op(eng, lambda e: e.tensor_copy(out=out.ap, in_=in_.ap), [out], [in_], cost=c)

    def recip(self, out, in_):
        self.op("dve", lambda e: e.reciprocal(out=out.ap, in_=in_.ap), [out], [in_], cost=0.07 + 0.006 * fsz(out))

    def memset(self, eng, out, val):
        self.op(eng, lambda e: e.memset(out.ap, val), [out], [], cost=0.1 + 0.0006 * fsz(out))

    def scan(self, out, d0, d1, init):
        ins = [d0, d1]
        if isinstance(init, V):
            ins.append(init)
            init = init.ap
        self.op("dve", lambda e: e.tensor_tensor_scan(out=out.ap, data0=d0.ap, data1=d1.ap, initial=init,
                                                      op0=ALU.mult, op1=ALU.add), [out], ins,
                cost=0.07 + 0.0022 * fsz(out))

    def dma(self, eng, out, in_, extra_ins=()):
        self.op(eng, lambda e: e.dma_start(out=out.ap, in_=in_.ap), [out], [in_] + list(extra_ins), dma=True,
                cost=2.2 + out.ap.nbytes() / 150e3)


def list_schedule(spec, window=None):
    import os
    if window is None:
        window = int(os.environ.get("KWIN", "64"))
    n = len(spec)
    deps = [None] * n
    last_w = {}
    readers = {}
    for i, (eng, fn, reads, writes, dma, cost, prio) in enumerate(spec):
        d = set()
        for t in reads:
            k = id(t)
            if k in last_w:
                d.add(last_w[k])
            if t.excl:
                for r in readers.get(k, ()):
                    if spec[r][0] != eng:
                        d.add(r)
        for t in writes:
            k = id(t)
            if k in last_w:
                d.add(last_w[k])
            d.update(readers.get(k, ()))
        for t in reads:
            if not t.const:
                readers.setdefault(id(t), []).append(i)
        for t in writes:
            last_w[id(t)] = i
            readers[id(t)] = []
        d.discard(i)
        deps[i] = tuple(d)
    engs = ("pe", "act", "dve", "pool", "sp")
    if int(os.environ.get("KBLEVEL", "1")):
        bl = [0.0] * n
        for i in range(n - 1, -1, -1):
            bl[i] += spec[i][5]
            for d in deps[i]:
                v = bl[i] + 0.4
                if v > bl[d]:
                    bl[d] = v
        spec = [(o[0], o[1], o[2], o[3], o[4], o[5], -bl[i]) for i, o in enumerate(spec)]
    pending = {e: [i for i in range(n) if spec[i][0] == e] for e in engs}
    head = {e: 0 for e in engs}
    done = [False] * n
    finish = [0.0] * n
    efree = {e: 0.0 for e in engs}
    order = []
    LAT = float(os.environ.get("KLAT", "0.4"))
    INF = 1e30
    _sc = {e: float(os.environ.get("KS_" + e, "1.0")) for e in engs}
    if any(v != 1.0 for v in _sc.values()):
        spec = [(o[0], o[1], o[2], o[3], o[4], o[5] * (_sc[o[0]] if not o[4] else 1.0), o[6]) for o in spec]
    _kd = float(os.environ.get("KDMA", "1.0"))
    if _kd != 1.0:
        spec = [(o[0], o[1], o[2], o[3], o[4], o[5] * (_kd if o[4] else 1.0), o[6]) for o in spec]

    def candidate(e):
        lst = pending[e]
        h = head[e]
        while h < len(lst) and done[lst[h]]:
            h += 1
        head[e] = h
        best = None
        bkey = None
        cnt = 0
        j = h
        ef = efree[e]
        while j < len(lst) and cnt < window:
            i = lst[j]
            j += 1
            if done[i]:
                continue
            cnt += 1
            ok = True
            st = ef
            for d in deps[i]:
                if not done[d]:
                    ok = False
                    break
                f = finish[d] + (LAT if spec[d][0] != e or spec[d][4] else 0.05)
                if f > st:
                    st = f
            if not ok:
                continue
            key = (st, spec[i][6], i)
            if bkey is None or key < bkey:
                best, bkey = i, key
                if st <= ef + 1e-9 and spec[i][6] == 0:
                    break
        return best, (bkey[0] if bkey else INF)

    binder = {}
    startt = {}
    last_on = {}
    remaining = n
    while remaining:
        pick = None
        pstart = INF
        for e in engs:
            i, st = candidate(e)
            if i is not None and st < pstart:
                pick, pstart = i, st
        assert pick is not None, "scheduler deadlock"
        eng, fn, reads, writes, dma, cost, prio = spec[pick]
        done[pick] = True
        finish[pick] = pstart + cost
        if DEBUG_TAGS:
            bind = ("eng", last_on.get(eng))
            bt = efree[eng]
            for d in deps[pick]:
                f = finish[d] + (LAT if spec[d][0] != eng or spec[d][4] else 0.05)
                if f > bt + 1e-9:
                    bt = f
                    bind = ("dep", d)
            binder[pick] = bind
            startt[pick] = pstart
            last_on[eng] = pick
        efree[eng] = pstart + (0.15 if dma else cost)
        order.append(pick)
        remaining -= 1
    import os
    if os.environ.get("KDEBUG"):
        busy = {e: 0.0 for e in engs}
        for i in range(n):
            busy[spec[i][0]] += (0.15 if spec[i][4] else spec[i][5])
        print("sched makespan(us):", max(finish), "busy:", {e: round(v) for e, v in busy.items()})
    if DEBUG_TAGS:
        import collections
        cur = max(range(n), key=lambda i: finish[i])
        t_hi = float(os.environ.get("KCRIT_HI", "1e9"))
        t_lo = float(os.environ.get("KCRIT_LO", "0"))
        agg = collections.OrderedDict()
        tot = collections.Counter()
        while cur is not None:
            kind, prev = binder.get(cur, ("eng", None))
            if t_lo <= startt[cur] <= t_hi:
                key = (spec[cur][0], TAGS.get(id(spec[cur][1]), 0), kind)
                a = agg.setdefault(key, [0, 0.0])
                a[0] += 1
                a[1] += finish[cur] - startt[cur]
                tot[(spec[cur][0], kind)] += finish[cur] - startt[cur]
            cur = prev
        print("critical path ops by (eng, line, binding):")
        for k, (c, tme) in sorted(agg.items(), key=lambda x: -x[1][1])[:40]:
            print("  ", k, "n=%d time=%.1f" % (c, tme))
        print("totals:", {k: round(v, 1) for k, v in tot.items()})
    return order


def merge_chunks(recs):
    lists = [[c for c in r.chunks if c] for r in recs]
    lists = [l for l in lists if l]
    pos = [0] * len(lists)
    out = []
    while True:
        best = None
        bestf = None
        for i, l in enumerate(lists):
            if pos[i] < len(l):
                f = (pos[i] + 0.5) / len(l)
                if best is None or f < bestf:
                    best, bestf = i, f
        if best is None:
            break
        out.append(lists[best][pos[best]])
        pos[best] += 1
    return out


def build_program(nseq, S):
    NT = S // TT
    NTOK = nseq * S
    nc = bass.Bass("TRN2", target_bir_lowering=False)
    xT_d = nc.dram_tensor("xT", [NCH, 128, NTOK], F32, kind="ExternalInput").ap()
    wts_d = nc.dram_tensor("wts", [NGRP, 128, 4096], F32, kind="ExternalInput").ap()
    cm_d = nc.dram_tensor("cm", [128, 8, 128], F32, kind="ExternalInput").ap()
    vecs_d = nc.dram_tensor("vecs", [128, NVEC], F32, kind="ExternalInput").ap()
    wgate_d = nc.dram_tensor("wgate", [128, 2, 4, 128], F32, kind="ExternalInput").ap()
    wba_d = nc.dram_tensor("wba", [128, 8, 8], F32, kind="ExternalInput").ap()
    yT_d = nc.dram_tensor("yT", [NCH, 128, NTOK], F32, kind="ExternalOutput").ap()
    wbf_d = nc.dram_tensor("wbf", [NGRP, 128, 4096], BF16, kind="Internal").ap()

    es = ExitStack()
    with es:
        sb_bytes = [0]

        def sb(name, shape, dt):
            n = 1
            for d in shape[1:]:
                n *= d
            sb_bytes[0] += n * (4 if dt == F32 else 2)
            return es.enter_context(nc.sbuf_tensor(name, shape, dt))

        def ps(name, shape, dt):
            return es.enter_context(nc.psum_tensor(name, shape, dt))

        ringA_h = sb("ringA", [128, 2, 4096], BF16)
        ringB_h = sb("ringB", [128, 3, 4096], BF16)
        ringA = [T(ringA_h[:, i, :]) for i in range(2)]
        ringB = [T(ringB_h[:, i, :]) for i in range(3)]
        x_h = sb("x", [128, 2, NCH, TT], F32)
        xA = T(x_h[:, 0, :, :])
        xBc = [T(x_h[:, 1, c, :]) for c in range(NCH)]
        hA_h = sb("hA", [128, NCH, TT], BF16)
        hB_h = sb("hB", [128, NCH, TT], BF16)
        hA = [T(hA_h[:, c, :]) for c in range(NCH)]
        hB = [T(hB_h[:, c, :]) for c in range(NCH)]
        uT_h = sb("uT", [128, 16, TT], BF16)
        uT = [T(uT_h[:, c, :]) for c in range(16)]
        qkn_h = sb("qkn", [128, NH, 4, 2, 128], BF16)
        qkn = [T(qkn_h[:, :, :, kq, :]) for kq in range(2)]
        vT_h = sb("vT", [128, NH, TT], BF16)
        vT = [T(vT_h[:, h, :]) for h in range(NH)]
        sz_h = sb("sz", [128, NH, TT], BF16)
        sz = [T(sz_h[:, h, :]) for h in range(NH)]
        gg_h = sb("gg", [128, 4, TT], BF16)
        gg = [T(gg_h[:, c, :]) for c in range(4)]
        mix_h = sb("mix", [128, NCH, TT], BF16)
        mix = [T(mix_h[:, c, :]) for c in range(NCH)]
        osb_h = sb("osb", [128, NH, TT], BF16)
        osb = [T(osb_h[:, h, :]) for h in range(NH)]
        raw_h = sb("raw", [128, 2, TT + 4], F32)
        raw = [T(raw_h[:, i, :]) for i in range(2)]
        acc_h = sb("acc", [128, 2, TT], F32)
        acc = [T(acc_h[:, i, :]) for i in range(2)]
        sqb_h = sb("sqb", [128, 2, TT], BF16)
        sqb = [T(sqb_h[:, i, :]) for i in range(2)]
        rs_h = sb("rs", [128, 2, TT], F32)
        rs = [T(rs_h[:, i, :]) for i in range(2)]
        rsB = T(sb("rsB", [128, TT], F32)[:, :])
        reluB_h = sb("reluB", [128, 2, TT], BF16)
        reluB = [T(reluB_h[:, i, :]) for i in range(2)]
        lt_h = sb("lt", [128, 4, TT], F32)
        lt = [T(lt_h[:, i, :]) for i in range(4)]
        xrb = T(sb("xrb", [128, TT], BF16)[:, :])
        halo_h = sb("halo", [128, 16, 4], F32)
        halo = [T(halo_h[:, c, :]) for c in range(16)]
        hst_h = sb("hst", [128, 4, 2], F32)
        hst = [T(hst_h[:, c, :]) for c in range(4)]
        S32_h = sb("S32", [128, NH, 128], F32)
        S32 = [T(S32_h[:, h, :]) for h in range(NH)]
        S32A = S32_h[:, :, :]
        Sbf_h = sb("Sbf", [128, NH, 128], BF16)
        Sbf = [T(Sbf_h[:, h, :]) for h in range(NH)]
        SbfA = Sbf_h[:, :, :]
        cm = T(sb("cm_sb", [128, 8, 128], F32)[:, :, :], const=True)
        cmb = T(sb("cmb", [128, 4, 128], BF16)[:, :, :], const=True)
        vecs = T(sb("vecs_sb", [128, NVEC], F32)[:, :], const=True)
        dvec = T(sb("dvec", [128, 24], F32)[:, :], const=True)
        wgate = T(sb("wgate_sb", [128, 2, 4, 128], BF16)[:, :, :, :], const=True)
        wba = T(sb("wba_sb", [128, 8, 8], BF16)[:, :, :], const=True)
        L1, L2, MB, SM, ID, ONES, CM0, CM1 = [cm[:, i, :] for i in range(8)]
        Cb = cmb[:, 0, :]
        SMb = cmb[:, 1, :]
        IDb = cmb[:, 2, :]
        ONESb = cmb[:, 3, :]
        gt_h = sb("gt", [128, 12, 16], F32)
        gtT = [T(gt_h[:, i, :]) for i in range(12)]
        NSET = 2
        TN = ["decT", "decS", "egcB", "N0", "Y0", "Xa", "Ya", "kg", "Ptmp"]
        ut_h = sb("ut", [128, len(TN), NH, 128], BF16)
        UT = {nm: [T(ut_h[:, i, h, :]) for h in range(NH)] for i, nm in enumerate(TN)}
        UTA = {nm: ut_h[:, i, :, :] for i, nm in enumerate(TN)}
        PN = ["Pfin", "kdec", "vtok", "attnT", "qg", "nw0"]
        up_h = sb("up", [128, NSET, len(PN), NH, 128], BF16)
        UP = [{nm: [T(up_h[:, s_, i, h, :]) for h in range(NH)] for i, nm in enumerate(PN)} for s_ in range(NSET)]
        UPA = [{nm: up_h[:, s_, i, :, :] for i, nm in enumerate(PN)} for s_ in range(NSET)]
        ug_h = sb("ug", [128, NH, 128], F32)
        UG = [T(ug_h[:, h, :]) for h in range(NH)]
        vnew_h = sb("vnew", [128, NH, 128], BF16)
        VNEW = [T(vnew_h[:, h, :]) for h in range(NH)]
        c4_h = sb("c4", [128, 3, NH, 128], BF16)
        c4 = T(c4_h[:, :, :, :], const=True)

        def bank(name, dt=F32, n=TT):
            h_ = ps(name, [128, n], dt)
            return h_, T(h_[:, :], excl=True)
        pdense = [bank("pd%d" % i)[1] for i in range(3)]
        pdense.append(bank("pstat")[1])
        pdense += [bank("pq%d" % i)[1] for i in range(2)]
        pdense += [bank("qc")[1], bank("qd")[1]]

        wts_t = [T(wts_d[g], const=True) for g in range(NGRP)]
        wbf_t = [T(wbf_d[g]) for g in range(NGRP)]
        xT_t = T(xT_d, const=True)

        P = Prog()

        spec = []
        bank_free_after = [-1] * 8

        def flush(recs):
            ops = [o for chunk in merge_chunks(recs) for o in chunk]
            base = len(spec)
            last_use = {}
            for i, o in enumerate(ops):
                for v in o[2] + o[3]:
                    if isinstance(v.t, LatePD):
                        last_use[id(v.t)] = base + i
            for i, (eng, fn, ins, outs, dma, cost, prio) in enumerate(ops):
                idx = base + i
                rt = []
                for grp in (ins, outs):
                    lst = []
                    for v in grp:
                        t = v.t
                        if isinstance(t, LatePD):
                            if t.bound is None:
                                cands = [k for k in range(len(pdense)) if bank_free_after[k] < idx]
                                assert cands, "too many live short-lived PSUM tiles"
                                k = min(cands, key=lambda k_: bank_free_after[k_])
                                t.bound = pdense[k]
                                bank_free_after[k] = last_use[id(t)]
                            t = t.bound.root
                        lst.append(t)
                    rt.append(lst)
                spec.append((eng, fn, rt[0], rt[1], dma, cost, prio))

        o_nw1, o_nw2, o_fnw, o_cw, o_lcw, o_lcb, o_lba, o_lbx, o_lap, o_alog, o_dtb, o_gnw = \
            0, 8, 16, 24, 72, 88, 92, 96, 100, 104, 108, 112

        R = Rec()
        R.dma("sp", cm.v, T(cm_d, const=True).v)
        R.dma("sp", vecs.v, T(vecs_d, const=True).v)
        R.dma("pool", cmb.v, V(T(cm_d, const=True), cm_d[:, 2:6, :]))
        R.dma("pool", wgate.v, T(wgate_d, const=True).v)
        R.dma("pool", wba.v, T(wba_d, const=True).v)
        for g in range(6):
            R.dma("pool", wbf_t[g].v, wts_t[g].v)
        for h in range(NH):
            R.copy("pool", c4[:, 0, h, :], SMb)
            R.copy("pool", c4[:, 1, h, :], IDb)
            R.copy("pool", c4[:, 2, h, :], Cb)
        tmpv = T(sb("tmpv", [128, 8, 16], F32)[:, :, :])

        def ln1p_small(R, out, e, w, k0):
            z = tmpv[:, k0, 0:w]
            z2 = tmpv[:, k0 + 1, 0:w]
            pl = tmpv[:, k0 + 2, 0:w]
            R.ts("dve", z, e, 2.0, ALU.add)
            R.recip(z, z)
            R.tt("dve", z, z, e, ALU.mult)
            R.tt("dve", z2, z, z, ALU.mult)
            R.ts("dve", pl, z2, 1.0 / 9, ALU.mult, 1.0 / 7, ALU.add)
            R.tt("dve", pl, pl, z2, ALU.mult)
            R.ts("dve", pl, pl, 1.0 / 5, ALU.add)
            R.tt("dve", pl, pl, z2, ALU.mult)
            R.ts("dve", pl, pl, 1.0 / 3, ALU.add)
            R.tt("dve", pl, pl, z2, ALU.mult)
            R.ts("dve", pl, pl, 1.0, ALU.add)
            R.tt("dve", pl, pl, z, ALU.mult)
            R.ts("dve", out, pl, 2.0, ALU.mult)

        def softplus(R, out, x, w, k0):
            ab = tmpv[:, k0 + 3, 0:w]
            l1 = tmpv[:, k0 + 4, 0:w]
            R.ts("dve", ab, x, -1.0, ALU.mult)
            R.tt("dve", ab, ab, x, ALU.min)
            R.act(ab, ab, AF.Exp)
            ln1p_small(R, l1, ab, w, k0)
            R.stt(out, x, 0.0, l1, ALU.max, ALU.add)

        ngl = tmpv[:, 7, 0:4]
        R.ts("dve", ngl, vecs[:, o_lap:o_lap + 4], -1.0, ALU.mult)
        softplus(R, ngl, ngl, 4, 0)
        R.ts("dve", dvec[:, 0:4], ngl, -8.0, ALU.mult)
        R.act(dvec[:, 4:8], vecs[:, o_alog:o_alog + 4], AF.Exp)
        R.ts("dve", dvec[:, 4:8], dvec[:, 4:8], -1.0, ALU.mult)
        R.ts("dve", dvec[:, 8:12], vecs[:, o_lba:o_lba + 4], -1.0, ALU.mult)
        R.ts("dve", dvec[:, 12:16], vecs[:, o_lbx:o_lbx + 4], -1.0, ALU.mult)
        R.ts("dve", dvec[:, 16:20], dvec[:, 0:4], 2.0, ALU.mult)
        flush([R])

        ring_state = {"A": [0, 0], "B": [0, 0]}
        ntiles = nseq * NT
        seqA = [g for _ in range(ntiles) for g in range(6)]
        seqB = [g for _ in range(ntiles) for g in (6, 7, 8, 9, 10, 11, 12, 13, 14, 15, 16, 17, 18, 19, 20, 21, 22, 23)]

        def ring_next(R, which):
            ring = ringA if which == "A" else ringB
            seq = seqA if which == "A" else seqB
            st = ring_state[which]
            while st[0] < len(seq) and st[0] < st[1] + len(ring):
                R.dma("sp", ring[st[0] % len(ring)].v, wbf_t[seq[st[0]]].v)
                st[0] += 1
            slot = ring[st[1] % len(ring)]
            st[1] += 1
            return slot

        def next_pdB():
            return LatePD()

        def sig_of(R, out, x, nscale):
            R.act(out, x, AF.Exp, scale=nscale)
            R.act(out, out, AF.Ln, bias=1.0)
            R.act(out, out, AF.Exp, scale=-1.0)

        def rmsnorm_to(R, xc, hdst, wcol, rsbuf):
            for c in range(NCH):
                R.act(hdst[c].v, xc(c), AF.Square)
            pst = LatePD()
            for c in range(NCH):
                R.mm(pst.v, ONESb, hdst[c].v, start=(c == 0), stop=(c == NCH - 1))
            R.act(rsbuf.v, pst.v, AF.Ln, scale=1.0 / D, bias=EPS)
            R.act(rsbuf.v, rsbuf.v, AF.Exp, scale=-0.5)
            R.cut()
            for c in range(NCH):
                R.stt(hdst[c].v, xc(c), vecs[:, wcol + c:wcol + c + 1], rsbuf.v, ALU.mult, ALU.mult)
            R.cut()

        def gen_A(ti):
            s_i, j_i = divmod(ti, NT)
            tok0 = s_i * S + j_i * TT
            first = (j_i == 0)
            R = Rec()
            R.dma("sp", xA.v, V(xT_t, xT_d[:, :, tok0:tok0 + TT].rearrange("c p t -> p c t")))
            if first:
                for c in range(16):
                    R.memset("pool", halo[c].v, 0.0)
                for c in range(4):
                    R.memset("pool", hst[c].v, 0.0)
                for h in range(NH):
                    R.memset("pool", S32[h].v, 0.0)
                    R.memset("pool", Sbf[h].v, 0.0)
            R.cut()
            rmsnorm_to(R, lambda c: xA[:, c, :], hA, o_nw1, rs[0])
            if ti == 0:
                for g in range(6, NGRP):
                    R.dma("pool", wbf_t[g].v, wts_t[g].v, extra_ins=[rs[0].v])

            pg = LatePD()
            for jj in range(4):
                for c in range(NCH):
                    R.mm(pg[:, 8 * jj:8 * jj + 8], hA[c][:, 128 * jj:128 * (jj + 1)], wba[:, c, :],
                         start=(c == 0), stop=(c == NCH - 1))
            R.cut()
            beta, nbeta, gtm, tA, egc, erem, egl0, egl1 = [gtT[i] for i in range(8)]
            dtb3 = V(vecs, vecs.ap[:, o_dtb:o_dtb + 4])
            for jj in range(4):
                R.act(beta[:, 4 * jj:4 * jj + 4], pg[:, 8 * jj:8 * jj + 4], AF.Exp, scale=-1.0)
                R.tt("dve", tA[:, 4 * jj:4 * jj + 4], pg[:, 8 * jj + 4:8 * jj + 8], dtb3, ALU.add)
            R.act(beta.v, beta.v, AF.Ln, bias=1.0)
            R.act(beta.v, beta.v, AF.Exp, scale=-1.0)
            R.ts("dve", nbeta.v, beta.v, -1.0, ALU.mult)
            softplus(R, tA.v, tA.v, 16, 0)
            for jj in range(4):
                R.tt("dve", gtm[:, 4 * jj:4 * jj + 4], tA[:, 4 * jj:4 * jj + 4], dvec[:, 4:8], ALU.mult)
            pg2 = LatePD()
            R.mm(pg2[:, 0:16], L2, gtm.v)
            R.mm(pg2[:, 16:32], L1, gtm.v)
            R.mm(pg2[:, 32:48], CM0, gtm.v)
            R.act(egc.v, pg2[:, 0:16], AF.Exp)
            R.act(erem.v, pg2[:, 16:32], AF.Exp)
            R.act(egl0.v, pg2[:, 32:48], AF.Exp)
            R.cut()

            rot = [0]

            def conv4(R, ch, wcol, bias, out_v, pdt):
                rw = raw[rot[0] % 2]
                rot[0] += 1
                R.copy("pool", rw[:, 0:3], halo[ch][:, 0:3])
                R.copy("act", rw[:, 3:3 + TT], pdt.v)
                R.copy("pool", halo[ch][:, 0:3], rw[:, TT:TT + 3])
                if bias is None:
                    R.ts("dve", out_v, rw[:, 0:TT], vecs[:, wcol:wcol + 1], ALU.mult)
                else:
                    R.ts("dve", out_v, rw[:, 0:TT], vecs[:, wcol:wcol + 1], ALU.mult, bias, ALU.add)
                for k in range(1, 4):
                    R.stt(out_v, rw[:, k:k + TT], vecs[:, wcol + k:wcol + k + 1], out_v, ALU.mult, ALU.add)

            R2 = Rec()
            for g in range(6):
                Rc = R if g < 3 else R2
                wslot = ring_next(Rc, "A")
                for n in range(4):
                    ch = 4 * g + n
                    pdt = LatePD()
                    for c in range(NCH):
                        Rc.mm(pdt.v, wslot[:, 512 * c + 128 * n:512 * c + 128 * (n + 1)], hA[c].v,
                             start=(c == 0), stop=(c == NCH - 1))
                    Rc.cut()
                    if ch < 12:
                        a_ = acc[ch % 2]
                        conv4(Rc, ch, o_cw + 4 * ch, None, a_.v, pdt)
                        sg_ = rs[ch % 2]
                        sig_of(Rc, sg_.v, a_.v, -1.0)
                        if ch >= 8:
                            Rc.tt("pool", vT[ch - 8].v, a_.v, sg_.v, ALU.mult)
                        else:
                            kq = 1 if ch < 4 else 0
                            h = ch % 4
                            Rc.tt("pool", a_.v, a_.v, sg_.v, ALU.mult)
                            sq_ = sqb[ch % 2]
                            Rc.act(sq_.v, a_.v, AF.Square)
                            pst = LatePD()
                            Rc.mm(pst.v, ONESb, sq_.v)
                            r_ = rs[ch % 2]
                            if kq == 1:
                                Rc.act(r_.v, pst.v, AF.Ln, scale=128.0, bias=128.0 * EPS)
                            else:
                                Rc.act(r_.v, pst.v, AF.Ln, scale=1.0, bias=EPS)
                            Rc.act(r_.v, r_.v, AF.Exp, scale=-0.5)
                            Rc.tt("dve", qkn[kq][:, h, :, :],
                                 V(a_, a_.ap.rearrange("p (j t) -> p j t", j=4)),
                                 V(r_, r_.ap.rearrange("p (j t) -> p j t", j=4)), ALU.mult)
                    elif ch < 16:
                        sg_ = rs[ch % 2]
                        a_ = acc[ch % 2]
                        Rc.copy("act", a_.v, pdt.v)
                        sig_of(Rc, sg_.v, a_.v, -1.0)
                        Rc.tt("pool", sz[ch - 12].v, a_.v, sg_.v, ALU.mult)
                    elif ch < 20:
                        sg_ = rs[ch % 2]
                        a_ = acc[ch % 2]
                        Rc.copy("act", a_.v, pdt.v)
                        Rc.act(sg_.v, a_.v, AF.Square)
                        Rc.ts("pool", sg_.v, sg_.v, 0.044715, ALU.mult, 1.0, ALU.add)
                        Rc.tt("dve", sg_.v, a_.v, sg_.v, ALU.mult)
                        sig_of(Rc, sg_.v, sg_.v, -1.5957691216057308)
                        Rc.tt("dve", gg[ch - 16].v, a_.v, sg_.v, ALU.mult)
                    else:
                        lc = ch - 20
                        xr, ra, a2, ig = lt
                        conv4(Rc, 12 + lc, o_lcw + 4 * lc, vecs[:, o_lcb + lc:o_lcb + lc + 1], xr.v, pdt)
                        Rc.copy("act", xrb.v, xr.v)
                        pr = LatePD()
                        Rc.mm(pr.v, wgate[:, 0, lc, :], xrb.v)
                        pi = LatePD()
                        Rc.mm(pi.v, wgate[:, 1, lc, :], xrb.v)
                        Rc.act(ra.v, pr.v, AF.Exp, scale=-1.0, bias=dvec[:, 8 + lc:9 + lc])
                        Rc.act(ra.v, ra.v, AF.Ln, bias=1.0)
                        Rc.act(ra.v, ra.v, AF.Exp, scale=-1.0)
                        Rc.act(a2.v, ra.v, AF.Exp, scale=dvec[:, 16 + lc:17 + lc])
                        Rc.act(ra.v, ra.v, AF.Exp, scale=dvec[:, lc:lc + 1])
                        Rc.act(a2.v, a2.v, AF.Ln, scale=-1.0, bias=1.0)
                        Rc.act(a2.v, a2.v, AF.Exp, scale=0.5)
                        Rc.act(ig.v, pi.v, AF.Exp, scale=-1.0, bias=dvec[:, 12 + lc:13 + lc])
                        Rc.act(ig.v, ig.v, AF.Ln, bias=1.0)
                        Rc.act(ig.v, ig.v, AF.Exp, scale=-1.0)
                        Rc.tt("pool", ig.v, ig.v, xr.v, ALU.mult)
                        Rc.tt("pool", ig.v, ig.v, a2.v, ALU.mult)
                        Rc.scan(xr.v, ra.v, ig.v, hst[lc][:, 0:1])
                        Rc.copy("pool", hst[lc][:, 0:1], xr[:, TT - 1:TT])
                        Rc.tt("pool", mix[4 + lc].v, xr.v, gg[lc].v, ALU.mult)
                    Rc.cut()

            def hv(lst):
                return [t.v for t in lst]

            def pre_jj(Rp, jj):
                st_ = jj % NSET
                u, ua = UT, UTA
                p, pa = UP[st_], UPA[st_]
                cols = [4 * jj + h for h in range(NH)]
                kT4 = qkn[0][:, :, jj, :]
                qT4 = qkn[1][:, :, jj, :]
                for h in range(NH):
                    Rp.ts("dve", UG[h].v, L2, gtm[:, cols[h]:cols[h] + 1], ALU.mult)
                Rp.cut()
                q = LatePD()
                for h in range(NH):
                    Rp.mm(q[:, 128 * h:128 * (h + 1)], L1, UG[h].v)
                Rp.op("act", lambda e, q=q: e.activation(out=ua["decT"], in_=q.v.ap, func=AF.Exp),
                      hv(u["decT"]), [q.v], cost=0.56)
                Rp.op("pool", lambda e: e.tensor_tensor(out=ua["decT"], in0=ua["decT"], in1=c4_h[:, 2, :, :], op=ALU.mult),
                      hv(u["decT"]), hv(u["decT"]) + [c4.v], cost=1.26)
                Rp.op("pool", lambda e: e.tensor_tensor(out=ua["decS"], in0=ua["decT"], in1=c4_h[:, 0, :, :], op=ALU.mult),
                      hv(u["decS"]), hv(u["decT"]) + [c4.v], cost=1.26)
                Rp.cut()
                q = LatePD()
                for h in range(NH):
                    Rp.mm(q[:, 128 * h:128 * (h + 1)], kT4.t[:, h, jj, :], kT4.t[:, h, jj, :])
                for h in range(NH):
                    Rp.stt(u["N0"][h].v, q[:, 128 * h:128 * (h + 1)], nbeta[:, cols[h]:cols[h] + 1],
                           u["decS"][h].v, ALU.mult, ALU.mult)
                Rp.cut()
                q = LatePD()
                for h in range(NH):
                    Rp.mm(q[:, 128 * h:128 * (h + 1)], kT4.t[:, h, jj, :], qT4.t[:, h, jj, :])
                Rp.op("dve", lambda e, q=q: e.tensor_tensor(out=pa["attnT"], in0=q.v.ap, in1=ua["decT"], op=ALU.mult),
                      hv(p["attnT"]), [q.v] + hv(u["decT"]), cost=0.62)
                Rp.cut()
                q = LatePD()
                for h in range(NH):
                    Rp.mm(q[:, 128 * h:128 * (h + 1)], ONES, UG[h].v)
                Rp.op("act", lambda e, q=q: e.activation(out=ua["egcB"], in_=q.v.ap, func=AF.Exp),
                      hv(u["egcB"]), [q.v], cost=0.56)
                Rp.op("dve", lambda e: e.tensor_tensor(out=pa["qg"], in0=qT4.ap, in1=ua["egcB"], op=ALU.mult),
                      hv(p["qg"]), [qT4] + hv(u["egcB"]), cost=0.62)
                Rp.cut()
                q = LatePD()
                for h in range(NH):
                    Rp.mm(q[:, 128 * h:128 * (h + 1)], kT4.t[:, h, jj, :], IDb)
                for h in range(NH):
                    Rp.act(u["kg"][h].v, q[:, 128 * h:128 * (h + 1)], AF.Copy, scale=egc[:, cols[h]:cols[h] + 1])
                    Rp.act(p["kdec"][h].v, q[:, 128 * h:128 * (h + 1)], AF.Copy, scale=erem[:, cols[h]:cols[h] + 1])
                Rp.cut()
                q = LatePD()
                for h in range(NH):
                    Rp.mm(q[:, 128 * h:128 * (h + 1)], vT[h][:, 128 * jj:128 * (jj + 1)], IDb)
                Rp.op("act", lambda e, q=q: e.activation(out=pa["vtok"], in_=q.v.ap, func=AF.Copy),
                      hv(p["vtok"]), [q.v], cost=0.56)
                Rp.cut()
                q = LatePD()
                for h in range(NH):
                    Rp.mm(q[:, 128 * h:128 * (h + 1)], u["N0"][h].v, IDb)
                Rp.op("act", lambda e, q=q: e.activation(out=ua["Y0"], in_=q.v.ap, func=AF.Copy),
                      hv(u["Y0"]), [q.v], cost=0.56)
                Rp.op("pool", lambda e: e.tensor_tensor(out=pa["Pfin"], in0=ua["N0"], in1=c4_h[:, 1, :, :], op=ALU.add),
                      hv(p["Pfin"]), hv(u["N0"]) + [c4.v], cost=1.26)
                Rp.cut()
                Xs = ["N0", "Xa", "N0", "Xa", "N0", "Xa"]
                Ys = ["Y0", "Ya", "Y0", "Ya", "Y0", "Ya", "Y0"]
                Ps = [("p", "Pfin"), ("u", "Ptmp"), ("p", "Pfin"), ("u", "Ptmp"), ("p", "Pfin"), ("u", "Ptmp"),
                      ("p", "Pfin")]
                NLEV = 6

                def PT(k):
                    w, nm = Ps[k]
                    return (u[nm], ua[nm]) if w == "u" else (p[nm], pa[nm])
                for r in range(0, NLEV + 1):
                    Yc = Ys[r]
                    qy = qx = qp = None
                    if r < NLEV:
                        Xc = Xs[r]
                        qy = LatePD()
                        for h in range(NH):
                            Rp.mm(qy[:, 128 * h:128 * (h + 1)], u[Xc][h].v, u[Yc][h].v)
                    if r < NLEV - 1:
                        qx = LatePD()
                        for h in range(NH):
                            Rp.mm(qx[:, 128 * h:128 * (h + 1)], u[Yc][h].v, u[Xc][h].v)
                    if r >= 1:
                        Pc_t, Pc_a = PT(r - 1)
                        Pn_t, Pn_a = PT(r)
                        qp = LatePD()
                        for h in range(NH):
                            Rp.mm(qp[:, 128 * h:128 * (h + 1)], u[Yc][h].v, Pc_t[h].v)
                    if qy is not None:
                        Yn = Ys[r + 1]
                        Rp.op("dve", lambda e, qy=qy, Yn=Yn: e.tensor_copy(out=ua[Yn], in_=qy.v.ap),
                              hv(u[Yn]), [qy.v], cost=0.62)
                    if qx is not None:
                        Xn = Xs[r + 1]
                        Rp.op("act", lambda e, qx=qx, Xn=Xn: e.activation(out=ua[Xn], in_=qx.v.ap, func=AF.Copy),
                              hv(u[Xn]), [qx.v], cost=0.56)
                    if qp is not None:
                        Rp.op("dve", lambda e, qp=qp, Pn_a=Pn_a, Pc_a=Pc_a: e.tensor_tensor(out=Pn_a, in0=qp.v.ap, in1=Pc_a, op=ALU.add),
                              hv(Pn_t), [qp.v] + hv(Pc_t), cost=0.62)
                    Rp.cut()
                q = LatePD()
                for h in range(NH):
                    Rp.mm(q[:, 128 * h:128 * (h + 1)], u["kg"][h].v, p["Pfin"][h].v)
                Rp.op("act", lambda e, q=q: e.activation(out=pa["nw0"], in_=q.v.ap, func=AF.Copy, scale=-1.0),
                      hv(p["nw0"]), [q.v], cost=0.56)
                Rp.cut()

            def rec_jj(Rr, jj):
                st_ = jj % NSET
                p, pa = UP[st_], UPA[st_]
                cols = [4 * jj + h for h in range(NH)]
                for c in range(1):
                    lo, hi = 0, 128
                    qvn, qo, qds = LatePD(), LatePD(), LatePD()
                    for h in range(NH):
                        Rr.mm(qvn[lo:hi, 128 * h:128 * (h + 1)], p["Pfin"][h][lo:hi, lo:hi], p["vtok"][h][lo:hi, :],
                              start=True, stop=False)
                        Rr.mm(qvn[lo:hi, 128 * h:128 * (h + 1)], p["nw0"][h][:, lo:hi], Sbf[h].v, start=False, stop=True)
                    for h in range(NH):
                        Rr.ts("dve", VNEW[h][lo:hi, :], qvn[lo:hi, 128 * h:128 * (h + 1)],
                              beta[lo:hi, cols[h]:cols[h] + 1], ALU.mult)
                    Rr.cut()
                    for h in range(NH):
                        Rr.mm(qo[:, 128 * h + lo:128 * h + hi], Sbf[h].v, p["qg"][h][:, lo:hi], start=True, stop=False)
                        Rr.mm(qo[:, 128 * h + lo:128 * h + hi], VNEW[h][lo:hi, :], p["attnT"][h][lo:hi, lo:hi],
                              start=False, stop=True)
                    for h in range(NH):
                        Rr.mm(qds[:, 128 * h:128 * (h + 1)], p["kdec"][h][lo:hi, :], VNEW[h][lo:hi, :])
                    egl = egl0 if c == 0 else egl1
                    for h in range(NH):
                        Rr.stt(S32[h].v, S32[h].v, egl[:, cols[h]:cols[h] + 1], qds[:, 128 * h:128 * (h + 1)],
                               ALU.mult, ALU.add)
                    Rr.op("act", lambda e: e.activation(out=SbfA, in_=S32A, func=AF.Copy), hv(Sbf), hv(S32), cost=0.56)
                    Rr.cut()
                Rr.op("act", lambda e, jj=jj, qo=qo: e.activation(out=osb_h[:, :, 128 * jj:128 * (jj + 1)],
                                                                  in_=qo.v.ap.rearrange("p (h t) -> p h t", h=NH), func=AF.Copy),
                      hv(osb), [qo.v], cost=0.56)
                Rr.cut()

            Rg = Rec()
            pre_jj(Rg, 0)
            gd = Rec()
            gd.chunks = merge_chunks([Rg])
            for jj in range(4):
                Rr = Rec()
                rec_jj(Rr, jj)
                Rp = Rec()
                if jj + 1 < 4:
                    pre_jj(Rp, jj + 1)
                gd.chunks.extend(merge_chunks([Rr, Rp]))
            flush_list = [[R], [R2, gd]]
            Rn = Rec()
            for h in range(NH):
                sq_ = sqb[h % 2]
                Rn.act(sq_.v, osb[h].v, AF.Square)
                pst = LatePD()
                Rn.mm(pst.v, ONESb, sq_.v)
                r_ = rs[h % 2]
                Rn.act(r_.v, pst.v, AF.Ln, scale=1.0 / 128, bias=EPS)
                Rn.act(r_.v, r_.v, AF.Exp, scale=-0.5)
                ot = acc[h % 2]
                Rn.stt(ot.v, osb[h].v, vecs[:, o_gnw:o_gnw + 1], r_.v, ALU.mult, ALU.mult)
                Rn.tt("dve", mix[h].v, ot.v, sz[h].v, ALU.mult)
                Rn.cut()
            flush_list.append([Rn])
            return flush_list

        def gen_B(ti):
            s_i, j_i = divmod(ti, NT)
            tok0 = s_i * S + j_i * TT
            R = Rec(prio=1)
            for c in range(NCH):
                R.dma("sp", xBc[c].v, V(xT_t, xT_d[c, :, tok0:tok0 + TT]))
            R.cut()
            for g in range(2):
                wslot = ring_next(R, "B")
                for n in range(4):
                    dc = 4 * g + n
                    pdt = next_pdB()
                    for c in range(NCH):
                        R.mm(pdt.v, wslot[:, 512 * c + 128 * n:512 * c + 128 * (n + 1)], mix[c].v,
                             start=(c == 0), stop=(c == NCH - 1))
                    R.tt("dve", xBc[dc].v, xBc[dc].v, pdt.v, ALU.add)
                    R.cut()
            rmsnorm_to(R, lambda c: xBc[c].v, hB, o_nw2, rsB)
            for half in range(2):
                for g in range(4):
                    wslot = ring_next(R, "B")
                    for n in range(4):
                        fc = 4 * g + n
                        pdt = next_pdB()
                        for c in range(NCH):
                            R.mm(pdt.v, wslot[:, 512 * c + 128 * n:512 * c + 128 * (n + 1)], hB[c].v,
                                 start=(c == 0), stop=(c == NCH - 1))
                        rl = reluB[fc % 2]
                        R.act(rl.v, pdt.v, AF.Relu)
                        R.tt("pool", uT[fc].v, rl.v, rl.v, ALU.mult)
                        R.cut()
                for g in range(4):
                    wslot = ring_next(R, "B")
                    for n in range(2):
                        dc = 2 * g + n
                        pdt = next_pdB()
                        for fc in range(16):
                            R.mm(pdt.v, wslot[:, 256 * fc + 128 * n:256 * fc + 128 * (n + 1)], uT[fc].v,
                                 start=(fc == 0), stop=(fc == 15))
                        R.tt("dve", xBc[dc].v, xBc[dc].v, pdt.v, ALU.add)
                        R.cut()
            for c in range(NCH):
                R.act(hB[c].v, xBc[c].v, AF.Square)
            pst = LatePD()
            for c in range(NCH):
                R.mm(pst.v, ONESb, hB[c].v, start=(c == 0), stop=(c == NCH - 1))
            R.act(rsB.v, pst.v, AF.Ln, scale=1.0 / D, bias=EPS)
            R.act(rsB.v, rsB.v, AF.Exp, scale=-0.5)
            for c in range(NCH):
                R.stt(xBc[c].v, xBc[c].v, vecs[:, o_fnw + c:o_fnw + c + 1], rsB.v, ALU.mult, ALU.mult)
                R.dma("sp", T(yT_d[c, :, tok0:tok0 + TT]).v, xBc[c].v)
            R.cut()
            return [[R]]

        for grp in gen_A(0):
            flush(grp)
        for ti in range(ntiles):
            Bl = gen_B(ti)
            if ti + 1 < ntiles:
                Al = gen_A(ti + 1)
                Aflat = Rec()
                Aflat.chunks = []
                for grp in Al:
                    Aflat.chunks.extend(merge_chunks(grp))
                flush([Bl[0][0], Aflat])
            else:
                flush(Bl[0])
        import os
        if os.environ.get("KDEBUG"):
            print("SBUF bytes/partition:", sb_bytes[0], "ops:", len(spec))
        order = list_schedule(spec)
        for i in order:
            eng, fn, reads, writes, dma, cost, prio = spec[i]
            P.add(eng, fn, reads, writes, dma)
        P.emit(nc)
    return nc


def _masks():
    m = np.arange(128)
    same = np.ones((128, 128), bool)
    cm = np.zeros((128, 8, 128), np.float32)
    cm[:, 0, :] = (m[:, None] > m[None, :]) & same
    cm[:, 1, :] = (m[:, None] <= m[None, :]) & same
    cm[:, 2, :] = (m[:, None] <= m[None, :]) & same
    cm[:, 3, :] = (m[:, None] < m[None, :]) & same
    cm[:, 4, :] = np.eye(128)
    cm[:, 5, :] = 1.0
    cm[:, 6, :] = 1.0
    cm[:, 7, :] = 0.0
    return cm


def prep_shared(inp):
    f = np.float32
    w_in = np.asarray(inp["w_in"], f)[0]
    wmain = np.concatenate([w_in[:, 0:1536], w_in[:, 1536:2048], w_in[:, 2568:3080], w_in[:, 2056:2568]], axis=1)
    wba_ = w_in[:, 2048:2056]
    w_out = np.asarray(inp["w_out"], f)[0]
    w1 = np.asarray(inp["w_ff1"], f)[0]
    w2 = np.asarray(inp["w_ff2"], f)[0]
    wts = np.zeros((NGRP, 128, 4096), f)

    def colgrp(W, g):
        return W[:, 512 * g:512 * (g + 1)].reshape(8, 128, 512).transpose(1, 0, 2).reshape(128, 4096)
    for g in range(6):
        wts[g] = colgrp(wmain, g)
    for g in range(2):
        wts[6 + g] = colgrp(w_out, g)
    for half in range(2):
        base = 8 + 8 * half
        for g in range(4):
            wts[base + g] = colgrp(w1, 4 * half + g)
        for g in range(4):
            blk = w2[2048 * half:2048 * (half + 1), 256 * g:256 * (g + 1)]
            wts[base + 4 + g] = blk.reshape(16, 128, 2, 128).transpose(1, 0, 2, 3).reshape(128, 4096)
    vecs = np.zeros((128, NVEC), f)

    def pc(v, n):
        return np.asarray(v, f).reshape(n, 128).T
    vecs[:, 0:8] = pc(inp["norm_mix_w"][0], 8)
    vecs[:, 8:16] = pc(inp["norm_mlp_w"][0], 8)
    vecs[:, 16:24] = pc(inp["final_norm_w"], 8)
    cw = np.asarray(inp["gdn_conv_w"], f)[0]
    vecs[:, 24:72] = cw.reshape(4, 12, 128).transpose(2, 1, 0).reshape(128, 48)
    lcw = np.asarray(inp["lru_conv_w"], f)[0]
    vecs[:, 72:88] = lcw.reshape(4, 4, 128).transpose(2, 1, 0).reshape(128, 16)
    vecs[:, 88:92] = pc(inp["lru_conv_b"][0], 4)
    vecs[:, 92:96] = pc(np.asarray(inp["lru_gate_a_b"], f)[0].reshape(512), 4)
    vecs[:, 96:100] = pc(np.asarray(inp["lru_gate_x_b"], f)[0].reshape(512), 4)
    vecs[:, 100:104] = pc(inp["lru_a_param"][0], 4)
    vecs[:, 104:108] = np.broadcast_to(np.asarray(inp["gdn_A_log"], f)[0][None, :], (128, 4))
    vecs[:, 108:112] = np.broadcast_to(np.asarray(inp["gdn_dt_bias"], f)[0][None, :], (128, 4))
    vecs[:, 112] = np.asarray(inp["gdn_norm_w"], f)[0]
    wgate = np.zeros((128, 2, 4, 128), f)
    for gi, key in enumerate(("lru_gate_a_w", "lru_gate_x_w")):
        wg = np.asarray(inp[key], f)[0]
        for lc in range(4):
            for b in range(2):
                wgate[64 * b:64 * (b + 1), gi, lc, 64 * b:64 * (b + 1)] = wg[2 * lc + b]
    wba = wba_.reshape(8, 128, 8).transpose(1, 0, 2).copy()
    return {"wts": wts, "cm": _masks(), "vecs": vecs, "wgate": wgate, "wba": np.ascontiguousarray(wba)}


def prep_x(xs):
    nseq, S, _ = xs.shape
    return np.ascontiguousarray(xs.reshape(nseq * S, NCH, 128).transpose(1, 2, 0))


def unprep_y(yT, nseq, S):
    return np.ascontiguousarray(yT.transpose(2, 0, 1).reshape(nseq, S, D))


_NC_CACHE = {}


def kernel(**inputs):
    x = np.asarray(inputs["x"], np.float32)
    B, S, _ = x.shape
    ncores = 8
    nseq = B // ncores
    shared = prep_shared(inputs)
    key = (nseq, S)
    if key not in _NC_CACHE:
        _NC_CACHE[key] = build_program(nseq, S)
    nc = _NC_CACHE[key]
    in_maps = []
    for c in range(ncores):
        m = dict(shared)
        m["xT"] = prep_x(x[c * nseq:(c + 1) * nseq])
        in_maps.append(m)
    res = run_bass_kernel_spmd(nc, in_maps, core_ids=list(range(ncores)))
    outs = [unprep_y(np.asarray(r["yT"]), nseq, S) for r in res.results]
    return np.concatenate(outs, axis=0).astype(np.float32)
```
